# Optimizing a Trainium2 kernel written in Bass

```python
import math
import jax
import jax.numpy as jnp
from jax import lax
import numpy as np

D_MODEL = 1024
BATCH = 4
SEQ = 8192
DEPTH = 4

N_MIXERS = 3
N_FOX_LAYERS = (DEPTH + 2) // 3
N_GLA_LAYERS = (DEPTH + 1) // 3
N_GDN_LAYERS = DEPTH // 3
RMS_EPS = 1e-6

FOX_HEADS = 8
FOX_HEAD_DIM = D_MODEL // FOX_HEADS
FOX_WIDTH = FOX_HEADS * FOX_HEAD_DIM
FOX_BLOCK = 128
FOX_FGATE_BIAS = 3.0
FOX_IN_DIM = 4 * FOX_WIDTH + FOX_HEADS

GLA_HEADS = 4
GLA_KEY_DIM = D_MODEL // 2 // GLA_HEADS
GLA_VAL_DIM = D_MODEL // GLA_HEADS
GLA_WIDTH = GLA_HEADS * GLA_VAL_DIM
GLA_RANK = 16
GLA_TAU = 16.0
GLA_CHUNK = 64
GLA_IN_DIM = 2 * GLA_HEADS * GLA_KEY_DIM + 2 * GLA_WIDTH + GLA_RANK

GDN_QK_HEADS = 4
GDN_V_HEADS = 8
GDN_HEAD_DIM = D_MODEL // GDN_V_HEADS
GDN_WIDTH = GDN_V_HEADS * GDN_HEAD_DIM
GDN_CONV = 4
GDN_CHUNK = 64
GDN_CONV_DIM = 2 * GDN_QK_HEADS * GDN_HEAD_DIM + GDN_WIDTH
GDN_IN_DIM = GDN_CONV_DIM + GDN_WIDTH + 2 * GDN_V_HEADS

kernel_name = 'hybrid_fox_gla_gdn_trunk'


def rms_norm(x, w):
    xf = x.astype(jnp.float32)
    y = xf * lax.rsqrt(jnp.mean(xf * xf, axis=-1, keepdims=True) + RMS_EPS)
    return (y * w.astype(jnp.float32)).astype(x.dtype)


def l2_norm(x):
    xf = x.astype(jnp.float32)
    return xf * lax.rsqrt(jnp.sum(xf * xf, axis=-1, keepdims=True) + RMS_EPS)


def to_chunks(t, chunk):
    b, s, h = t.shape[:3]
    t = t.reshape((b, s // chunk, chunk, h) + t.shape[3:])
    return jnp.moveaxis(t, 3, 1)


def from_chunks(t):
    b, h, nc, l, d = t.shape
    return jnp.moveaxis(t, 1, 3).reshape(b, nc * l, h, d)


def causal_conv(u, w):
    k, c = w.shape
    return lax.conv_general_dilated(u, w[:, None, :].astype(u.dtype), window_strides=(1,), padding=[(k - 1, 0)], dimension_numbers=('NWC', 'WIO', 'NWC'), feature_group_count=c)


def fox_mixer(xn, w_in, b_f, q_gain, k_gain, w_out):
    bsz, s, _ = xn.shape
    h, dh = FOX_HEADS, FOX_HEAD_DIM
    q, k, v, z, f_logit = jnp.split(xn @ w_in, [FOX_WIDTH, 2 * FOX_WIDTH, 3 * FOX_WIDTH, 4 * FOX_WIDTH], axis=-1)
    q = rms_norm(q.reshape(bsz, s, h, dh), q_gain).transpose(0, 2, 1, 3)
    k = rms_norm(k.reshape(bsz, s, h, dh), k_gain).transpose(0, 2, 1, 3)
    v = v.reshape(bsz, s, h, dh).transpose(0, 2, 1, 3)
    log_f = jax.nn.log_sigmoid((f_logit + b_f).astype(jnp.float32))
    cum = jnp.cumsum(log_f, axis=1).transpose(0, 2, 1)
    nb = s // FOX_BLOCK
    q_blocks = q.reshape(bsz, h, nb, FOX_BLOCK, dh).transpose(2, 0, 1, 3, 4)
    c_blocks = cum.reshape(bsz, h, nb, FOX_BLOCK).transpose(2, 0, 1, 3)
    pos_blocks = jnp.arange(s).reshape(nb, FOX_BLOCK)
    key_pos = jnp.arange(s)
    scale = dh ** -0.5

    def attend_block(blk):
        qb, cb, pb = blk
        logits = jnp.einsum('bhqd,bhkd->bhqk', qb, k).astype(jnp.float32) * scale + (cb[..., :, None] - cum[..., None, :])
        logits = jnp.where(pb[:, None] >= key_pos[None, :], logits, -jnp.inf)
        p = jax.nn.softmax(logits, axis=-1).astype(v.dtype)
        return jnp.einsum('bhqk,bhkd->bhqd', p, v)

    o = lax.map(attend_block, (q_blocks, c_blocks, pos_blocks))
    o = o.transpose(1, 0, 3, 2, 4).reshape(bsz, s, FOX_WIDTH)
    return (o * jax.nn.silu(z)) @ w_out


def gla_mixer(xn, w_in, w_gate_up, b_gate, o_gain, w_out):
    bsz, s, _ = xn.shape
    h, dk, dv, L = GLA_HEADS, GLA_KEY_DIM, GLA_VAL_DIM, GLA_CHUNK
    f32 = jnp.float32
    qk_w = h * dk
    q, k, v, z, g_low = jnp.split(xn @ w_in, [qk_w, 2 * qk_w, 2 * qk_w + GLA_WIDTH, 2 * qk_w + 2 * GLA_WIDTH], axis=-1)
    log_a = jax.nn.log_sigmoid((g_low @ w_gate_up + b_gate).astype(f32)) / GLA_TAU
    qc = to_chunks(q.reshape(bsz, s, h, dk).astype(f32) * dk ** -0.5, L)
    kc = to_chunks(k.reshape(bsz, s, h, dk).astype(f32), L)
    vc = to_chunks(v.reshape(bsz, s, h, dv).astype(f32), L)
    ac = to_chunks(log_a.reshape(bsz, s, h, dk), L)
    bcum = jnp.cumsum(ac, axis=3)
    b_last = bcum[..., -1:, :]
    q_dec = qc * jnp.exp(bcum)
    k_inv = kc * jnp.exp(-bcum)
    k_dec = kc * jnp.exp(b_last - bcum)
    causal = jnp.tril(jnp.ones((L, L), dtype=bool))
    attn = jnp.where(causal, jnp.einsum('bhnld,bhnmd->bhnlm', q_dec, k_inv), 0.0)
    o_intra = jnp.einsum('bhnlm,bhnmv->bhnlv', attn, vc)

    def step(state, inp):
        qd, kd, vv, dec = inp
        o = jnp.einsum('bhld,bhdv->bhlv', qd, state)
        state = state * dec[..., :, None] + jnp.einsum('bhld,bhlv->bhdv', kd, vv)
        return state, o

    xs = (jnp.moveaxis(q_dec, 2, 0), jnp.moveaxis(k_dec, 2, 0), jnp.moveaxis(vc, 2, 0), jnp.moveaxis(jnp.exp(b_last[..., 0, :]), 2, 0))
    state0 = jnp.zeros((bsz, h, dk, dv), f32)
    _, o_inter = lax.scan(step, state0, xs)
    o = o_intra + jnp.moveaxis(o_inter, 0, 2)
    o = rms_norm(from_chunks(o), o_gain).reshape(bsz, s, GLA_WIDTH).astype(xn.dtype)
    return (o * jax.nn.silu(z)) @ w_out


def gdn_mixer(xn, w_in, conv_w, a_log, dt_bias, o_gain, w_out):
    bsz, s, _ = xn.shape
    hq, hv, d, L = GDN_QK_HEADS, GDN_V_HEADS, GDN_HEAD_DIM, GDN_CHUNK
    f32 = jnp.float32
    qkv, z, a, b = jnp.split(xn @ w_in, [GDN_CONV_DIM, GDN_CONV_DIM + GDN_WIDTH, GDN_CONV_DIM + GDN_WIDTH + hv], axis=-1)
    qkv = jax.nn.silu(causal_conv(qkv, conv_w))
    q, k, v = jnp.split(qkv, [hq * d, 2 * hq * d], axis=-1)
    rep = hv // hq
    q = jnp.repeat(l2_norm(q.reshape(bsz, s, hq, d)), rep, axis=2) * d ** -0.5
    k = jnp.repeat(l2_norm(k.reshape(bsz, s, hq, d)), rep, axis=2)
    v = v.reshape(bsz, s, hv, d).astype(f32)
    beta = jax.nn.sigmoid(b.astype(f32))
    g = -jnp.exp(a_log) * jax.nn.softplus(a.astype(f32) + dt_bias)
    qc, kc, vc = to_chunks(q, L), to_chunks(k, L), to_chunks(v, L)
    bc = to_chunks(beta, L)
    gc = jnp.cumsum(to_chunks(g, L), axis=-1)
    causal = jnp.tril(jnp.ones((L, L), dtype=bool))
    strict = jnp.tril(jnp.ones((L, L), dtype=bool), k=-1)
    diff = gc[..., :, None] - gc[..., None, :]
    decay = jnp.where(causal, jnp.exp(jnp.where(causal, diff, 0.0)), 0.0)
    kb = kc * bc[..., None]
    vb = vc * bc[..., None]
    tri = jnp.where(strict, jnp.einsum('bhnld,bhnmd->bhnlm', kb, kc) * decay, 0.0) + jnp.eye(L, dtype=f32)
    u = lax.linalg.triangular_solve(tri, vb, left_side=True, lower=True)
    w = lax.linalg.triangular_solve(tri, kb * jnp.exp(gc)[..., None], left_side=True, lower=True)
    qk = jnp.einsum('bhnld,bhnmd->bhnlm', qc, kc) * decay
    q_dec = qc * jnp.exp(gc)[..., None]
    g_last = gc[..., -1]
    k_dec = kc * jnp.exp(g_last[..., None] - gc)[..., None]

    def step(state, inp):
        uu, ww, qkk, qd, kd, dec = inp
        v_new = uu - jnp.einsum('bhld,bhdv->bhlv', ww, state)
        o = jnp.einsum('bhld,bhdv->bhlv', qd, state) + jnp.einsum('bhlm,bhmv->bhlv', qkk, v_new)
        state = state * dec[..., None, None] + jnp.einsum('bhld,bhlv->bhdv', kd, v_new)
        return state, o

    xs = tuple(jnp.moveaxis(t, 2, 0) for t in (u, w, qk, q_dec, k_dec, jnp.exp(g_last)))
    state0 = jnp.zeros((bsz, hv, d, d), f32)
    _, o_inter = lax.scan(step, state0, xs)
    o = from_chunks(jnp.moveaxis(o_inter, 0, 2))
    o = rms_norm(o, o_gain).reshape(bsz, s, GDN_WIDTH).astype(xn.dtype)
    return (o * jax.nn.silu(z)) @ w_out


def setup_inputs(seed: int = 0) -> dict:
    key = jax.random.key(seed)
    ks = jax.random.split(key, 20)
    f32 = jnp.float32

    def nrm(k, shape, scale):
        return jax.random.normal(k, shape, f32) * scale

    x = nrm(ks[0], (BATCH, SEQ, D_MODEL), 1.0)
    norm_w = 1.0 + nrm(ks[1], (DEPTH, D_MODEL), 0.02)
    fox_w_in = nrm(ks[2], (N_FOX_LAYERS, D_MODEL, FOX_IN_DIM), D_MODEL ** -0.5)
    fox_b_f = FOX_FGATE_BIAS + nrm(ks[3], (N_FOX_LAYERS, FOX_HEADS), 0.5)
    fox_q_gain = 1.0 + nrm(ks[4], (N_FOX_LAYERS, FOX_HEAD_DIM), 0.02)
    fox_k_gain = 1.0 + nrm(ks[5], (N_FOX_LAYERS, FOX_HEAD_DIM), 0.02)
    fox_w_out = nrm(ks[6], (N_FOX_LAYERS, FOX_WIDTH, D_MODEL), FOX_WIDTH ** -0.5)
    gla_w_in = nrm(ks[7], (N_GLA_LAYERS, D_MODEL, GLA_IN_DIM), D_MODEL ** -0.5)
    gla_w_gate_up = nrm(ks[8], (N_GLA_LAYERS, GLA_RANK, GLA_HEADS * GLA_KEY_DIM), GLA_RANK ** -0.5)
    gla_b_gate = nrm(ks[9], (N_GLA_LAYERS, GLA_HEADS * GLA_KEY_DIM), 0.1)
    gla_o_gain = 1.0 + nrm(ks[10], (N_GLA_LAYERS, GLA_VAL_DIM), 0.02)
    gla_w_out = nrm(ks[11], (N_GLA_LAYERS, GLA_WIDTH, D_MODEL), GLA_WIDTH ** -0.5)
    gdn_w_in = nrm(ks[12], (N_GDN_LAYERS, D_MODEL, GDN_IN_DIM), D_MODEL ** -0.5)
    gdn_conv_w = nrm(ks[13], (N_GDN_LAYERS, GDN_CONV, GDN_CONV_DIM), GDN_CONV ** -0.5)
    gdn_a_log = jnp.log(jax.random.uniform(ks[14], (N_GDN_LAYERS, GDN_V_HEADS), f32, minval=1.0, maxval=16.0))
    dt = jnp.exp(jax.random.uniform(ks[15], (N_GDN_LAYERS, GDN_V_HEADS), f32, minval=math.log(1e-3), maxval=math.log(1e-1)))
    gdn_dt_bias = dt + jnp.log(-jnp.expm1(-dt))
    gdn_o_gain = 1.0 + nrm(ks[16], (N_GDN_LAYERS, GDN_HEAD_DIM), 0.02)
    gdn_w_out = nrm(ks[17], (N_GDN_LAYERS, GDN_WIDTH, D_MODEL), GDN_WIDTH ** -0.5)
    return {'x': x, 'norm_w': norm_w,
            'fox_w_in': fox_w_in, 'fox_b_f': fox_b_f, 'fox_q_gain': fox_q_gain, 'fox_k_gain': fox_k_gain, 'fox_w_out': fox_w_out,
            'gla_w_in': gla_w_in, 'gla_w_gate_up': gla_w_gate_up, 'gla_b_gate': gla_b_gate, 'gla_o_gain': gla_o_gain, 'gla_w_out': gla_w_out,
            'gdn_w_in': gdn_w_in, 'gdn_conv_w': gdn_conv_w, 'gdn_a_log': gdn_a_log, 'gdn_dt_bias': gdn_dt_bias, 'gdn_o_gain': gdn_o_gain, 'gdn_w_out': gdn_w_out}


def reference(x, norm_w, fox_w_in, fox_b_f, fox_q_gain, fox_k_gain, fox_w_out, gla_w_in, gla_w_gate_up, gla_b_gate, gla_o_gain, gla_w_out, gdn_w_in, gdn_conv_w, gdn_a_log, gdn_dt_bias, gdn_o_gain, gdn_w_out):
    for layer in range(DEPTH):
        xn = rms_norm(x, norm_w[layer])
        kind, idx = layer % N_MIXERS, layer // N_MIXERS
        if kind == 0:
            y = fox_mixer(xn, fox_w_in[idx], fox_b_f[idx], fox_q_gain[idx], fox_k_gain[idx], fox_w_out[idx])
        elif kind == 1:
            y = gla_mixer(xn, gla_w_in[idx], gla_w_gate_up[idx], gla_b_gate[idx], gla_o_gain[idx], gla_w_out[idx])
        else:
            y = gdn_mixer(xn, gdn_w_in[idx], gdn_conv_w[idx], gdn_a_log[idx], gdn_dt_bias[idx], gdn_o_gain[idx], gdn_w_out[idx])
        x = x + y.astype(x.dtype)
    return x
```

```python
import numpy as np
import concourse.bass as bass
import concourse.mybir as mybir
from concourse.bass_utils import run_bass_kernel_spmd

F32 = mybir.dt.float32
BF16 = mybir.dt.bfloat16
AF = mybir.ActivationFunctionType
ALU = mybir.AluOpType
AX = mybir.AxisListType

ENGS = ["pe", "act", "dve", "pool", "sp"]
NDSEM = 8
SEM_ROLL = 20000


class Buf:
    __slots__ = ("name", "lw", "rd", "rd_dma", "psum", "wr_dma")

    def __init__(self, name="", psum=False):
        self.name = name
        self.psum = psum
        self.lw = None
        self.rd = {}
        self.rd_dma = []
        self.wr_dma = []


class Op:
    __slots__ = ("eng", "fn", "deps", "sig", "idx", "dma", "seq", "sem", "val", "barred", "cc", "phase")


class Sched:
    def __init__(self, nc):
        self.nc = nc
        self.phase = None
        self.scopes = False
        self.q = {e: [] for e in ENGS}

    def begin_record(self):
        self._rec = []

    def end_record(self):
        r, self._rec = self._rec, None
        return r

    def replay_zip(self, a, b):
        ia = ib = 0
        while ia < len(a) or ib < len(b):
            if ib >= len(b) or (ia < len(a) and ia * len(b) <= ib * len(a)):
                self.add(*a[ia])
                ia += 1
            else:
                self.add(*b[ib])
                ib += 1

    def add(self, eng, fn, reads=(), writes=(), dma=False, cc=False):
        if getattr(self, "_rec", None) is not None:
            self._rec.append((eng, fn, tuple(reads), tuple(writes), dma, cc))
            return None
        op = Op()
        op.eng, op.fn, op.dma, op.sig, op.cc = eng, fn, dma, False, cc
        op.seq = op.sem = op.val = None
        op.barred = False
        op.phase = self.phase
        deps, seen = [], set()

        def adddep(d):
            if d is None or id(d) in seen:
                return
            seen.add(id(d))
            deps.append(d)

        for b in reads:
            adddep(b.lw)
            for w in b.wr_dma:
                adddep(w)
            if b.psum:
                for e2, r in b.rd.items():
                    if e2 != eng:
                        adddep(r)
        for b in writes:
            had_readers = bool(b.rd) or bool(b.rd_dma)
            for r in b.rd.values():
                adddep(r)
            for r in b.rd_dma:
                adddep(r)
            if dma and not cc:
                if had_readers or (b.lw is not None and not b.lw.dma):
                    adddep(b.lw)
                    b.wr_dma = []
            else:
                adddep(b.lw)
                for w in b.wr_dma:
                    adddep(w)
        op.deps = [d for d in deps
                   if not (eng == "pe" and d.eng == "pe" and not d.dma and not dma)]
        for b in reads:
            if dma:
                b.rd_dma.append(op)
            else:
                b.rd[eng] = op
        for b in writes:
            b.rd = {}
            b.rd_dma = []
            if dma and not cc:
                b.wr_dma.append(op)
                b.lw = None
            else:
                b.lw = op
                b.wr_dma = []
        op.idx = len(self.q[eng])
        self.q[eng].append(op)
        return op

    def barrier(self):
        pre = []
        for e in ENGS:
            last = None
            for op in self.q[e]:
                if op.dma:
                    if not getattr(op, "barred", False):
                        pre.append(op)
                        op.barred = True
                else:
                    last = op
            if last is not None:
                pre.append(last)
        for e in ENGS:
            op = self.add(e, lambda eng: eng.nop())
            op.deps = [d for d in pre if d is not op]

    def emit(self):
        nc = self.nc
        for e in ENGS:
            for op in self.q[e]:
                for d in op.deps:
                    d.sig = True
        csem = {}
        for e in ENGS:
            nsig = sum(1 for op in self.q[e] if op.sig and not op.dma)
            csem[e] = [nc.alloc_semaphore(f"c_{e}_{i}") for i in range(nsig // SEM_ROLL + 1)]
            s = 0
            for op in self.q[e]:
                if op.sig and not op.dma:
                    op.sem = csem[e][s // SEM_ROLL]
                    op.val = s % SEM_ROLL + 1
                    op.seq = s
                    s += 1
        dsem = {}
        for e in ENGS:
            nd = sum(1 for op in self.q[e] if op.dma)
            if nd == 0:
                continue
            dsem[e] = [nc.alloc_semaphore(f"d_{e}_{i}") for i in range(NDSEM)]
            n = 0
            for op in self.q[e]:
                if op.dma and op.cc:
                    op.sem = nc.alloc_semaphore(f"cc_{e}_{op.idx}")
                    op.val = 1
                elif op.dma:
                    op.sem = dsem[e][n % NDSEM]
                    op.val = 16 * (n // NDSEM + 1)
                    n += 1

        def run_engine(e, eng):
            waited = {}

            def wait(sem, val):
                key = sem.num
                if waited.get(key, 0) >= val:
                    return
                waited[key] = val
                eng.wait_ge(sem, val)

            cur = [None, None]

            def scope(ph):
                if not self.scopes or ph == cur[0]:
                    return
                if cur[1] is not None:
                    cur[1].__exit__(None, None, None)
                    cur[1] = None
                cur[0] = ph
                if ph is not None:
                    cur[1] = nc.named_scope(ph)
                    cur[1].__enter__()

            for op in self.q[e]:
                scope(op.phase)
                for d in op.deps:
                    wait(d.sem, d.val)
                if op.dma and op.cc:
                    ins = op.fn(eng)
                    ins.then_inc(op.sem)
                elif op.dma:
                    if op.val > 16:
                        wait(op.sem, op.val - 16)
                    ins = op.fn(eng)
                    ins.then_inc(op.sem, 16)
                else:
                    ins = op.fn(eng)
                    if op.sig:
                        ins.then_inc(op.sem, 1)
            scope(None)
            if e in dsem:
                last = {}
                for op in self.q[e]:
                    if op.dma:
                        last[op.sem.num] = (op.sem, max(op.val, last.get(op.sem.num, (None, 0))[1]))
                for sem, val in last.values():
                    wait(sem, val)

        with nc.Block() as block:
            @block.tensor
            def _(eng):
                run_engine("pe", eng)

            @block.scalar
            def _(eng):
                run_engine("act", eng)

            @block.vector
            def _(eng):
                run_engine("dve", eng)

            @block.gpsimd
            def _(eng):
                run_engine("pool", eng)

            @block.sync
            def _(eng):
                run_engine("sp", eng)


D_MODEL = 1024
NB = 4
RMS_EPS = 1e-6
NCST = 7 * 128


def make_consts():
    c = np.zeros((128, NCST), np.float32)
    i = np.arange(128)
    c[:, 0:128] = np.eye(128)
    c[:, 128:256] = (i[:, None] <= i[None, :])
    same = (i[:, None] // 64) == (i[None, :] // 64)
    c[:, 256:384] = (i[:, None] <= i[None, :]) & same
    c[:, 384:512] = (i[:, None] < i[None, :]) & same
    c[:, 512:640] = 1.0
    c[:, 640:768] = same
    c[:, 768:896] = (i[:, None] > i[None, :]) & same
    return c


class Ctx:
    pass


def sb(cx, name, shape, dt):
    es = getattr(cx, "es", None)
    if es is not None:
        return es.enter_context(cx.nc.sbuf_tensor(name, shape, dt))
    return cx.nc.alloc_sbuf_tensor(name, shape, dt)


def setup_consts(cx):
    nc, S = cx.nc, cx.S
    cx.cst_d = nc.dram_tensor("cst", [128, NCST], F32, kind="ExternalInput").ap()
    cx.cst = sb(cx, "cst_sb", [128, NCST], F32)
    cx.cstb = sb(cx, "cstb_sb", [128, NCST], BF16)
    cx.b_cst = Buf("cst")
    cx.b_cstb = Buf("cstb")
    cx.s.add("sp", lambda e: e.dma_start(out=cx.cst[:], in_=cx.cst_d), writes=[cx.b_cst], dma=True)
    cx.s.add("dve", lambda e: e.tensor_copy(out=cx.cstb[:], in_=cx.cst[:]), reads=[cx.b_cst], writes=[cx.b_cstb])
    cx.ident_b = cx.cstb[:, 0:128]
    cx.ones_b = cx.cstb[:, 512:640]
    cx.tri_f = cx.cst[:, 128:256]
    cx.tri_b = cx.cstb[:, 128:256]
    cx.epsq = sb(cx, "epsq", [128, 4], F32)
    cx.b_epsq = Buf("epsq")
    cx.s.add("dve", lambda e: e.memset(cx.epsq[:, 0:1], float(128 * RMS_EPS)), writes=[cx.b_epsq])
    cx.s.add("dve", lambda e: e.memset(cx.epsq[:, 1:2], 1.0), writes=[cx.b_epsq])
    cx.s.add("dve", lambda e: e.memset(cx.epsq[:, 2:3], float(D_MODEL * RMS_EPS)), writes=[cx.b_epsq])
    cx.s.add("dve", lambda e: e.memset(cx.epsq[:, 3:4], float(RMS_EPS)), writes=[cx.b_epsq])
    cx.ident_f = cx.cst[:, 0:128]
    cx.ones_f = cx.cst[:, 512:640]
    cx.ps = [nc.alloc_psum_tensor(f"ps{i}", [128, 512], F32) for i in range(8)]
    cx.b_ps = [Buf(f"ps{i}", psum=True) for i in range(8)]


def load_weights(cx, tag, w_d, ncols, nw_row, wbuf, b_w):
    s = cx.s
    n = 0
    for kc in range(8):
        for c0 in range(0, ncols, 516):
            c1 = min(ncols, c0 + 516)
            st = cx.wst[n % 2]
            b_st = cx.b_wst[n % 2]
            n += 1
            s.add("sp", lambda e, st=st, kc=kc, c0=c0, c1=c1: e.dma_start(out=st[:, 0:c1 - c0], in_=w_d[kc * 128:(kc + 1) * 128, c0:c1]),
                  writes=[b_st], dma=True)
            if nw_row is not None:
                s.add("dve", lambda e, st=st, kc=kc, c0=c0, c1=c1: e.tensor_scalar(
                    out=wbuf[:, kc, c0:c1], in0=st[:, 0:c1 - c0], scalar1=cx.normw[:, nw_row * 8 + kc: nw_row * 8 + kc + 1],
                    scalar2=None, op0=ALU.mult), reads=[b_st, cx.b_normw], writes=[b_w])
            else:
                s.add("dve", lambda e, st=st, kc=kc, c0=c0, c1=c1: e.tensor_copy(out=wbuf[:, kc, c0:c1], in_=st[:, 0:c1 - c0]),
                      reads=[b_st], writes=[b_w])


def boundary(cx, L, kind, x_src, x_dst, wout_d, win_d, ncols, emit_proj, final_out=None, tok_range=None):
    nc, s, S = cx.nc, cx.s, cx.S
    MT = S // 512
    has_out = wout_d is not None
    if has_out:
        load_weights(cx, f"wo{L}", wout_d, 1024, None, cx.wout, cx.b_wout)
    if win_d is not None:
        load_weights(cx, f"wi{L}", win_d, ncols, L, cx.win, cx.b_win)
    mts = list(range(MT) if tok_range is None else tok_range)
    dst = final_out if final_out is not None else x_dst

    def phase_a(mt):
        if has_out:
            go, b_go = cx.goT[0], cx.b_goT[0]
            jj, c0 = (mt * 512) // cx.CW, (mt * 512) % cx.CW
            s.add("sp", lambda e: e.dma_start(
                out=go[:], in_=cx.gofull_d[jj][:, c0:c0 + 512].rearrange("(c p) t -> p c t", p=128)),
                reads=[cx.b_gofull_d[jj]], writes=[b_go], dma=True)
        for sub in range(4):
            sub_a(mt, sub)

    def sub_a(mt, sub):
        tt = mt * 4 + sub
        xt, b_xt = cx.xt[tt % 2], cx.b_xt[tt % 2]
        s.add("sp", lambda e: e.dma_start(out=xt[:], in_=x_src[tt * 128:(tt + 1) * 128, :]),
              reads=[cx.b_xres_d[tt]] if x_src is cx.xres_d else [], writes=[b_xt], dma=True)
        if has_out:
            go, b_go = cx.goT[0], cx.b_goT[0]
            for half in range(2):
                yp, b_yp = cx.ps[1], cx.b_ps[1]
                for kc in range(8):
                    s.add("pe", lambda e, kc=kc, half=half: e.matmul(
                        yp[:, :], lhsT=go[:, kc, sub * 128:(sub + 1) * 128],
                        rhs=cx.wout[:, kc, half * 512:(half + 1) * 512], start=(kc == 0), stop=(kc == 7)),
                        reads=[b_go, cx.b_wout], writes=[b_yp])
                s.add("dve", lambda e, half=half: e.tensor_tensor(
                    out=xt[:, half * 512:(half + 1) * 512], in0=yp[:, :], in1=xt[:, half * 512:(half + 1) * 512],
                    op=ALU.add), reads=[b_yp, b_xt], writes=[b_xt])
            s.add("pool", lambda e: e.dma_start(out=dst[tt * 128:(tt + 1) * 128, :], in_=xt[:]),
                  reads=[b_xt], writes=[cx.b_xres_d[tt]], dma=True)
        if win_d is None:
            return
        sq, b_sq = cx.junk[tt % 2], cx.b_junk[tt % 2]
        ssq, b_ssq = cx.stat[tt % 4], cx.b_stat[tt % 4]
        s.add("act", lambda e: e.activation(out=sq[:], in_=xt[:], func=AF.Square, accum_out=ssq[:, 0:1]),
              reads=[b_xt], writes=[b_sq, b_ssq])
        s.add("act", lambda e: e.activation(out=ssq[:, 2:3], in_=ssq[:, 0:1], func=AF.Ln, bias=cx.epsq[:, 2:3]),
              reads=[b_ssq, cx.b_epsq], writes=[b_ssq])
        s.add("act", lambda e: e.activation(out=ssq[:, 1:2], in_=ssq[:, 2:3], func=AF.Exp, scale=-0.5),
              reads=[b_ssq], writes=[b_ssq])
        xs, b_xs = cx.xs[tt % 4], cx.b_xs[tt % 4]
        s.add("dve", lambda e: e.tensor_scalar(
            out=xs[:], in0=xt[:], scalar1=ssq[:, 1:2], scalar2=float(np.sqrt(D_MODEL)),
            op0=ALU.mult, op1=ALU.mult), reads=[b_xt, b_ssq], writes=[b_xs])

    def phase_t(mt):
        xT, b_xT = cx.xT[mt % 2], cx.b_xT[mt % 2]
        tp = cx.ps[0].bitcast(BF16)
        b_tp = cx.b_ps[0]
        for sub in range(4):
            tt = mt * 4 + sub
            xs, b_xs = cx.xs[tt % 4], cx.b_xs[tt % 4]
            for kc in range(8):
                s.add("pe", lambda e, kc=kc, xs=xs: e.transpose(
                    tp[:, kc * 128:(kc + 1) * 128], xs[:, kc * 128:(kc + 1) * 128], cx.ident_b),
                    reads=[b_xs, cx.b_cstb], writes=[b_tp])
            s.add("act", lambda e, sub=sub: e.copy(
                out=xT[:, :, sub * 128:(sub + 1) * 128], in_=tp[:, :].rearrange("p (c t) -> p c t", c=8)),
                reads=[b_tp], writes=[b_xT])

    if win_d is None:
        for mt in mts:
            phase_a(mt)
        return
    phase_a(mts[0])
    phase_t(mts[0])
    for i, mt in enumerate(mts):
        nxt = mts[i + 1] if i + 1 < len(mts) else None
        if nxt is not None:
            phase_a(nxt)
        emit_proj(mt, cx.xT[mt % 2], cx.b_xT[mt % 2])
        if nxt is not None:
            phase_t(nxt)
    if hasattr(emit_proj, "flush"):
        emit_proj.flush()


def fox_setup(cx):
    nc, S = cx.nc, cx.S
    cx.qT_d = nc.dram_tensor("qT_d", [4, 128, S], BF16, kind="Internal").ap()
    cx.kT_d = nc.dram_tensor("kT_d", [4, 128, S], BF16, kind="Internal").ap()
    cx.gT_d = nc.dram_tensor("gT_d", [4, 128, S], BF16, kind="Internal").ap()
    cx.v_d = nc.dram_tensor("v_d", [S, 512], BF16, kind="Internal").ap()
    cx.crow_d = nc.dram_tensor("crow_d", [4, S], F32, kind="Internal").ap()
    cx.b_qT_d, cx.b_kT_d, cx.b_gT_d, cx.b_v_d, cx.b_crow_d = (Buf("qT_d"), Buf("kT_d"), Buf("gT_d"), Buf("v_d"), Buf("crow_d"))
    cx.fxp = sb(cx, "fxp", [128, 16], F32)
    cx.b_fxp = Buf("fxp")
    cx.cumcol = sb(cx, "cumcol", [128, S // 128, 4], F32)
    cx.b_cumcol = Buf("cumcol")
    cx.carry = sb(cx, "carry", [128, 4], F32)
    cx.b_carry = Buf("carry")
    cx.fl = [sb(cx, f"fl{i}", [128, 16], F32) for i in range(2)]
    cx.b_fl = [Buf(f"fl{i}") for i in range(2)]
    cx.crow_sb = [sb(cx, f"crow{i}", [4, 128], F32) for i in range(2)]
    cx.b_crow_sb = [Buf(f"crow{i}") for i in range(2)]
    cx.sqb = [sb(cx, f"sqb{i}", [128, 512], BF16) for i in range(2)]
    cx.b_sqb = [Buf(f"sqb{i}") for i in range(2)]
    cx.rs = [sb(cx, f"rs{i}", [128, 512], F32) for i in range(2)]
    cx.b_rs = [Buf(f"rs{i}") for i in range(2)]
    cx.ob = [sb(cx, f"ob{i}", [128, 512], BF16) for i in range(4)]
    cx.b_ob = [Buf(f"ob{i}") for i in range(4)]
    cx.b_p2 = [Buf(f"p2_{i}") for i in range(4)]
    cx.kT_sb = sb(cx, "kT_sb", [128, S], BF16)
    cx.b_kT_sb = Buf("kT_sb")
    cx.v_sb = sb(cx, "v_sb", [128, S // 128, 128], BF16)
    cx.b_v_sb = Buf("v_sb")
    cx.q_sb = [sb(cx, f"q_sb{i}", [128, 512], BF16) for i in range(2)]
    cx.b_q_sb = [Buf(f"q_sb{i}") for i in range(2)]
    cx.g_sb = [sb(cx, f"g_sb{i}", [128, 512], BF16) for i in range(2)]
    cx.b_g_sb = [Buf(f"g_sb{i}") for i in range(2)]
    cx.cnq = [sb(cx, f"cnq{i}", [128, 512], F32) for i in range(2)]
    cx.b_cnq = [Buf(f"cnq{i}") for i in range(2)]
    cx.tt_sb = [sb(cx, f"tt_sb{i}", [128, 512], F32) for i in range(4)]
    cx.b_tt_sb = [Buf(f"tt_sb{i}") for i in range(4)]
    cx.p_sb = [sb(cx, f"p_sb{i}", [128, 512], BF16) for i in range(4)]
    cx.b_p_sb = [Buf(f"p_sb{i}") for i in range(4)]
    cx.rl = [sb(cx, f"rl{i}", [128, 512], F32) for i in range(2)]
    cx.b_rl = [Buf(f"rl{i}") for i in range(2)]
    cx.o1 = [sb(cx, f"o1{i}", [128, 512], F32) for i in range(2)]
    cx.b_o1 = [Buf(f"o1{i}") for i in range(2)]
    cx.go_sb = [sb(cx, f"go_sb{i}", [128, 512], BF16) for i in range(2)]
    cx.b_go_sb = [Buf(f"go_sb{i}") for i in range(2)]


def fox_load_params(cx, li):
    s = cx.s
    d = cx.fx_d[li]
    s.add("sp", lambda e: e.dma_start(out=cx.fxp[:], in_=d), writes=[cx.b_fxp], dma=True)
    s.add("dve", lambda e: e.tensor_scalar(out=cx.fxp[:, 1:2], in0=cx.fxp[:, 1:2], scalar1=float(np.sqrt(128.0)),
                                           scalar2=None, op0=ALU.mult), reads=[cx.b_fxp], writes=[cx.b_fxp])
    s.add("dve", lambda e: e.memset(cx.carry[:], 0.0), writes=[cx.b_carry])


def fox_proj(cx):
    s = cx.s
    win = cx.win

    p2 = cx.ps[2]
    b_p2 = cx.b_ps[2]
    pend = []

    def phaseA(tt, sub, xT, b_xT):
        for kc in range(8):
            s.add("pe", lambda e, kc=kc: e.matmul(
                p2[:, 0:4], lhsT=xT[:, kc, sub * 128:(sub + 1) * 128], rhs=win[:, kc, 2048:2052],
                start=(kc == 0), stop=(kc == 7)), reads=[cx.b_win, b_xT], writes=[b_p2])
        fl, b_fl = cx.fl[tt % 2], cx.b_fl[tt % 2]
        s.add("dve", lambda e: e.tensor_tensor(out=fl[:, 0:4], in0=p2[:, 0:4], in1=cx.fxp[:, 8:12], op=ALU.add),
              reads=[b_p2, cx.b_fxp], writes=[b_fl])
        s.add("act", lambda e: e.activation(out=fl[:, 4:8], in_=fl[:, 0:4], func=AF.Exp, scale=-1.0),
              reads=[b_fl], writes=[b_fl])
        s.add("act", lambda e: e.activation(out=fl[:, 8:12], in_=fl[:, 4:8], func=AF.Ln, bias=cx.epsq[:, 1:2]),
              reads=[b_fl, cx.b_epsq], writes=[b_fl])

    def phaseB(tt):
        fl, b_fl = cx.fl[tt % 2], cx.b_fl[tt % 2]
        s.add("pe", lambda e: e.matmul(p2[:, 8:12], lhsT=cx.tri_f, rhs=fl[:, 8:12], start=True, stop=True),
              reads=[b_fl, cx.b_cst], writes=[b_p2])
        s.add("pe", lambda e: e.matmul(p2[:, 16:20], lhsT=cx.ones_f, rhs=fl[:, 8:12], start=True, stop=True),
              reads=[b_fl, cx.b_cst], writes=[b_p2])
        s.add("dve", lambda e: e.tensor_tensor(out=cx.cumcol[:, tt, :], in0=p2[:, 8:12], in1=cx.carry[:], op=ALU.add),
              reads=[b_p2, cx.b_carry], writes=[cx.b_cumcol])
        s.add("dve", lambda e: e.tensor_tensor(out=cx.carry[:], in0=p2[:, 16:20], in1=cx.carry[:], op=ALU.add),
              reads=[b_p2, cx.b_carry], writes=[cx.b_carry])

    def phaseC(tt):
        s.add("pe", lambda e: e.matmul(p2[0:4, 32:160], lhsT=cx.cumcol[:, tt, :], rhs=cx.ident_f, start=True, stop=True),
              reads=[cx.b_cumcol, cx.b_cst], writes=[b_p2])
        cr, b_cr = cx.crow_sb[tt % 2], cx.b_crow_sb[tt % 2]
        s.add("dve", lambda e: e.tensor_copy(out=cr[:], in_=p2[0:4, 32:160]), reads=[b_p2], writes=[b_cr])
        s.add("pool", lambda e: e.dma_start(out=cx.crow_d[:, tt * 128:(tt + 1) * 128], in_=cr[:]),
              reads=[b_cr], writes=[cx.b_crow_d], dma=True)

    def fchain(tt, sub, xT, b_xT):
        phaseA(tt, sub, xT, b_xT)
        if tt >= 1:
            phaseB(tt - 1)
        if tt >= 2:
            phaseC(tt - 2)

    def flush():
        TT = cx.S // 128
        phaseB(TT - 1)
        phaseC(TT - 2)
        phaseC(TT - 1)

    def emit(mt, xT, b_xT):
        for g in range(12):
            typ, h = g // 4, g % 4
            pb = 3 + (g % 2)
            ps, b_p = cx.ps[pb], cx.b_ps[pb]
            for kc in range(8):
                s.add("pe", lambda e, kc=kc, g=g, ps=ps: e.matmul(
                    ps[:, :], lhsT=win[:, kc, g * 128:(g + 1) * 128], rhs=xT[:, kc, :], start=(kc == 0), stop=(kc == 7)),
                    reads=[cx.b_win, b_xT], writes=[b_p])
            ob, b_ob = cx.ob[g % 4], cx.b_ob[g % 4]
            if typ < 2:
                sq, b_sq = cx.sqb[g % 2], cx.b_sqb[g % 2]
                rs, b_rs = cx.rs[g % 2], cx.b_rs[g % 2]
                s.add("act", lambda e, sq=sq, ps=ps: e.activation(out=sq[:], in_=ps[:, :], func=AF.Square),
                      reads=[b_p], writes=[b_sq])
                p5, b_p5 = cx.ps[5], cx.b_ps[5]
                s.add("pe", lambda e, sq=sq, p5=p5: e.matmul(p5[:, :], lhsT=cx.ones_b, rhs=sq[:], start=True, stop=True),
                      reads=[b_sq, cx.b_cstb], writes=[b_p5])
                s.add("act", lambda e, rs=rs, p5=p5: e.activation(
                    out=rs[:], in_=p5[:, :], func=AF.Ln, bias=cx.epsq[:, 0:1]),
                    reads=[b_p5, cx.b_epsq], writes=[b_rs])
                s.add("act", lambda e, rs=rs: e.activation(out=rs[:], in_=rs[:], func=AF.Exp, scale=-0.5),
                      reads=[b_rs], writes=[b_rs])
                s.add("dve", lambda e, ob=ob, ps=ps, rs=rs, typ=typ: e.scalar_tensor_tensor(
                    out=ob[:], in0=ps[:, :], scalar=cx.fxp[:, typ:typ + 1], in1=rs[:], op0=ALU.mult, op1=ALU.mult),
                    reads=[b_p, b_rs, cx.b_fxp], writes=[b_ob])
                dst, b_dst = (cx.qT_d, cx.b_qT_d) if typ == 0 else (cx.kT_d, cx.b_kT_d)
            else:
                s.add("act", lambda e, ob=ob, ps=ps: e.activation(out=ob[:], in_=ps[:, :], func=AF.Silu),
                      reads=[b_p], writes=[b_ob])
                dst, b_dst = cx.gT_d, cx.b_gT_d
            s.add("pool", lambda e, ob=ob, dst=dst, h=h, mt=mt: e.dma_start(
                out=dst[h, :, mt * 512:(mt + 1) * 512], in_=ob[:]), reads=[b_ob], writes=[b_dst], dma=True)
        for sub in range(4):
            tt = mt * 4 + sub
            pb = 6 + (sub % 2)
            ps, b_p = cx.ps[pb], cx.b_ps[pb]
            for kc in range(8):
                s.add("pe", lambda e, kc=kc, sub=sub, ps=ps: e.matmul(
                    ps[:, :], lhsT=xT[:, kc, sub * 128:(sub + 1) * 128], rhs=win[:, kc, 1536:2048],
                    start=(kc == 0), stop=(kc == 7)), reads=[cx.b_win, b_xT], writes=[b_p])
            ob, b_ob = cx.ob[sub % 4], cx.b_ob[sub % 4]
            s.add("act", lambda e, ob=ob, ps=ps: e.copy(out=ob[:], in_=ps[:, :]), reads=[b_p], writes=[b_ob])
            s.add("pool", lambda e, ob=ob, tt=tt: e.dma_start(out=cx.v_d[tt * 128:(tt + 1) * 128, :], in_=ob[:]),
                  reads=[b_ob], writes=[cx.b_v_d], dma=True)
            fchain(tt, sub, xT, b_xT)
    emit.flush = flush
    return emit


def fox_attn(cx):
    s, S = cx.s, cx.S
    NQ = S // 512
    for h in range(4):
        s.add("sp", lambda e, h=h: e.dma_start(out=cx.kT_sb[:], in_=cx.kT_d[h]), reads=[cx.b_kT_d], writes=[cx.b_kT_sb], dma=True)
        s.add("sp", lambda e, h=h: e.dma_start(
            out=cx.v_sb[:], in_=cx.v_d[:, h * 128:(h + 1) * 128].rearrange("(kb p) d -> p kb d", p=128)),
            reads=[cx.b_v_d], writes=[cx.b_v_sb], dma=True)
        steps = [(T, kb) for T in range(NQ) for kb in range(4 * T + 4)]
        LOOK = 3

        def front(i, h=h):
            T, kb = steps[i]
            if kb == 0:
                q, b_q = cx.q_sb[T % 2], cx.b_q_sb[T % 2]
                g, b_g = cx.g_sb[T % 2], cx.b_g_sb[T % 2]
                cn, b_cn = cx.cnq[T % 2], cx.b_cnq[T % 2]
                s.add("sp", lambda e: e.dma_start(out=q[:], in_=cx.qT_d[h, :, T * 512:(T + 1) * 512]),
                      reads=[cx.b_qT_d], writes=[b_q], dma=True)
                s.add("sp", lambda e: e.dma_start(out=g[:], in_=cx.gT_d[h, :, T * 512:(T + 1) * 512]),
                      reads=[cx.b_gT_d], writes=[b_g], dma=True)
                s.add("sp", lambda e: e.dma_start(out=cn[:], in_=cx.crow_d[h:h + 1, T * 512:(T + 1) * 512].partition_broadcast(128)),
                      reads=[cx.b_crow_d], writes=[b_cn], dma=True)
            q, b_q = cx.q_sb[T % 2], cx.b_q_sb[T % 2]
            cn, b_cn = cx.cnq[T % 2], cx.b_cnq[T % 2]
            j = kb - 4 * T
            c0 = max(0, j) * 128
            sp_, b_sp = cx.ps[i % 4], cx.b_ps[i % 4]
            tt_, b_tt = cx.tt_sb[i % 4], cx.b_tt_sb[i % 4]
            p_, b_pp = cx.p_sb[i % 4], cx.b_p_sb[i % 4]
            s.add("pe", lambda e: e.matmul(sp_[:, c0:512], lhsT=cx.kT_sb[:, kb * 128:(kb + 1) * 128], rhs=q[:, c0:512],
                                           start=True, stop=True), reads=[cx.b_kT_sb, b_q], writes=[b_sp])
            s.add("dve", lambda e: e.scalar_tensor_tensor(
                out=tt_[:, c0:512], in0=sp_[:, c0:512], scalar=cx.cumcol[:, kb, h:h + 1], in1=cn[:, c0:512],
                op0=ALU.add, op1=ALU.subtract), reads=[b_sp, cx.b_cumcol, b_cn], writes=[b_tt])
            s.add("act", lambda e: e.activation(out=p_[:, c0:512], in_=tt_[:, c0:512], func=AF.Exp),
                  reads=[b_tt], writes=[b_pp])
            if j >= 0:
                s.add("pool", lambda e: e.tensor_tensor(out=p_[:, c0:c0 + 128], in0=p_[:, c0:c0 + 128], in1=cx.tri_b, op=ALU.mult),
                      reads=[b_pp, cx.b_cstb], writes=[b_pp])

        def back(i, h=h):
            T, kb = steps[i]
            j = kb - 4 * T
            c0 = max(0, j) * 128
            p_, b_pp = cx.p_sb[i % 4], cx.b_p_sb[i % 4]
            op_, b_op = cx.ps[4 + T % 2], cx.b_ps[4 + T % 2]
            lp_, b_lp = cx.ps[6 + T % 2], cx.b_ps[6 + T % 2]
            last = (kb == 4 * T + 3)
            s.add("pe", lambda e: e.matmul(op_[:, c0:512], lhsT=cx.v_sb[:, kb, :], rhs=p_[:, c0:512],
                                           start=(kb == 0), stop=last, skip_group_check=True), reads=[cx.b_v_sb, b_pp], writes=[b_op])
            s.add("pe", lambda e: e.matmul(lp_[:, c0:512], lhsT=cx.ones_b, rhs=p_[:, c0:512],
                                           start=(kb == 0), stop=last, skip_group_check=True), reads=[cx.b_cstb, b_pp], writes=[b_lp])
            if last:
                rl, b_rl = cx.rl[T % 2], cx.b_rl[T % 2]
                o1, b_o1 = cx.o1[T % 2], cx.b_o1[T % 2]
                go, b_go = cx.go_sb[T % 2], cx.b_go_sb[T % 2]
                g, b_g = cx.g_sb[T % 2], cx.b_g_sb[T % 2]
                s.add("act", lambda e: e.activation(out=rl[:], in_=lp_[:, :], func=AF.Ln), reads=[b_lp], writes=[b_rl])
                s.add("act", lambda e: e.activation(out=rl[:], in_=rl[:], func=AF.Exp, scale=-1.0), reads=[b_rl], writes=[b_rl])
                s.add("dve", lambda e: e.tensor_tensor(out=o1[:], in0=op_[:, :], in1=rl[:], op=ALU.mult),
                      reads=[b_op, b_rl], writes=[b_o1])
                s.add("pool", lambda e: e.tensor_tensor(out=go[:], in0=o1[:], in1=g[:], op=ALU.mult),
                      reads=[b_o1, b_g], writes=[b_go])
                jj, gc0 = (T * 512) // cx.CW, (T * 512) % cx.CW
                s.add("pool", lambda e: e.dma_start(out=cx.gohalf_d[jj][h * 128:(h + 1) * 128, gc0:gc0 + 512], in_=go[:]),
                      reads=[b_go], writes=[cx.b_gohalf_d[jj]], dma=True)

        n = len(steps)
        for i in range(n + LOOK):
            if i < n:
                front(i)
            if i >= LOOK:
                back(i - LOOK)


def common_setup(cx):
    nc, S = cx.nc, cx.S
    cx.wst = [sb(cx, f"wst{i}", [128, 516], F32) for i in range(2)]
    cx.b_wst = [Buf(f"wst{i}") for i in range(2)]
    cx.win = sb(cx, "win", [128, 8, 2064], BF16)
    cx.b_win = Buf("win")
    cx.wout = sb(cx, "wout", [128, 8, 1024], BF16)
    cx.b_wout = Buf("wout")
    cx.normw = sb(cx, "normw", [128, 32], F32)
    cx.b_normw = Buf("normw")
    cx.normw_d = nc.dram_tensor("normw_in", [128, 32], F32, kind="ExternalInput").ap()
    cx.s.add("sp", lambda e: e.dma_start(out=cx.normw[:], in_=cx.normw_d), writes=[cx.b_normw], dma=True)
    cx.xT = [sb(cx, f"xT{i}", [128, 8, 512], BF16) for i in range(2)]
    cx.b_xT = [Buf(f"xT{i}") for i in range(2)]
    cx.goT = [sb(cx, f"goT{i}", [128, 8, 512], BF16) for i in range(1)]
    cx.b_goT = [Buf(f"goT{i}") for i in range(1)]
    cx.xt = [sb(cx, f"xt{i}", [128, 1024], F32) for i in range(2)]
    cx.b_xt = [Buf(f"xt{i}") for i in range(2)]
    cx.junk = [sb(cx, f"junk{i}", [128, 1024], BF16) for i in range(2)]
    cx.b_junk = [Buf(f"junk{i}") for i in range(2)]
    cx.stat = [sb(cx, f"stat{i}", [128, 4], F32) for i in range(4)]
    cx.b_stat = [Buf(f"stat{i}") for i in range(4)]
    cx.xs = [sb(cx, f"xs{i}", [128, 1024], BF16) for i in range(4)]
    cx.b_xs = [Buf(f"xs{i}") for i in range(4)]
    cx.x_in = nc.dram_tensor("x", [S, 1024], F32, kind="ExternalInput").ap()
    cx.xres_d = nc.dram_tensor("xres_d", [S, 1024], F32, kind=getattr(cx, "xres_kind", "Internal")).ap()
    cx.b_xres_d = [Buf(f"xres{t}") for t in range(S // 128)]
    cx.CW = min(1024, S)
    cx.NCH = S // cx.CW
    cx.gofull_d = [nc.dram_tensor(f"gofull_d{j}", [1024, cx.CW], BF16, kind="Internal").ap() for j in range(cx.NCH)]
    cx.b_gofull_d = [Buf(f"gofull_d{j}") for j in range(cx.NCH)]


def mixer_common_setup(cx):
    cx.gomt = sb(cx, "gomt", [128, 4, 512], BF16)
    cx.b_gomt = Buf("gomt")
    cx.st_f = [sb(cx, f"st_f{i}", [128, 256], F32) for i in range(4)]
    cx.b_st_f = [Buf(f"st_f{i}") for i in range(4)]
    cx.st_b = [sb(cx, f"st_b{i}", [128, 256], BF16) for i in range(4)]
    cx.b_st_b = [Buf(f"st_b{i}") for i in range(4)]
    cx.mparam = sb(cx, "mparam", [64, 1024], F32)
    cx.b_mparam = Buf("mparam")

    def two(name, shape, dt):
        return [sb(cx, f"{name}{i}", shape, dt) for i in range(2)], [Buf(f"{name}{i}") for i in range(2)]
    cx.w_gz, cx.b_w_gz = two("w_gz", [64, 512], F32)
    cx.w_og, cx.b_w_og = two("w_og", [64, 512], BF16)


def gla_setup(cx):
    cx.cmraw = [sb(cx, f"cmraw{i}", [128, 512], F32) for i in range(4)]
    cx.b_cmraw = [Buf(f"cmraw{i}") for i in range(4)]
    cx.glT = sb(cx, "glT", [32, 512], F32)
    cx.b_glT = Buf("glT")
    cx.wup = sb(cx, "wup", [32, 256], F32)
    cx.b_wup = Buf("wup")

    def two(name, shape, dt):
        return [sb(cx, f"{name}{i}", shape, dt) for i in range(2)], [Buf(f"{name}{i}") for i in range(2)]
    cx.w_a, cx.b_w_a = two("w_a", [64, 512], F32)
    cx.w_b, cx.b_w_b = two("w_b", [64, 512], F32)
    cx.w_c, cx.b_w_c = two("w_c", [64, 512], F32)
    cx.w_v, cx.b_w_v = two("w_v", [64, 512], BF16)
    cx.w_kd, cx.b_w_kd = two("w_kd", [64, 512], BF16)
    cx.w_e1, cx.b_w_e1 = two("w_e1", [128, 256], F32)
    cx.w_e2, cx.b_w_e2 = two("w_e2", [128, 256], F32)
    cx.w_qd, cx.b_w_qd = two("w_qd", [128, 256], BF16)
    cx.w_ki, cx.b_w_ki = two("w_ki", [128, 256], BF16)
    cx.w_at, cx.b_w_at = two("w_at", [64, 256], BF16)
    cx.w_st, cx.b_w_st = two("w_st", [64, 16], F32)


def gla_load_params(cx, wup_d, gain_d):
    s = cx.s
    s.add("dve", lambda e: e.memset(cx.glT[:], 1.0), writes=[cx.b_glT])
    s.add("sp", lambda e: e.dma_start(out=cx.wup[0:17, :], in_=wup_d), writes=[cx.b_wup], dma=True)
    s.add("sp", lambda e: e.dma_start(out=cx.mparam[:, 0:512], in_=gain_d), writes=[cx.b_mparam], dma=True)
    for h in range(2):
        s.add("dve", lambda e, h=h: e.memset(cx.st_f[h][:], 0.0), writes=[cx.b_st_f[h]])
        s.add("dve", lambda e, h=h: e.memset(cx.st_b[h][:], 0.0), writes=[cx.b_st_b[h]])


def gla_proj(cx):
    s, win = cx.s, cx.win
    U = cx.cst[0:64, 128:192]
    Ub = cx.cstb[0:64, 128:192]
    ONES = cx.cst[0:64, 512:576]
    IDB = cx.cstb[0:64, 0:64]
    ps, bp = cx.ps, cx.b_ps

    def emit(mt, xT, b_xT):
        for g in range(4):
            pb = 2
            for kc in range(8):
                s.add("pe", lambda e, kc=kc, g=g: e.matmul(
                    ps[pb][:, :], lhsT=win[:, kc, g * 128:(g + 1) * 128], rhs=xT[:, kc, :], start=(kc == 0), stop=(kc == 7)),
                    reads=[cx.b_win, b_xT], writes=[bp[pb]])
            s.add("act", lambda e, g=g: e.copy(out=cx.cmraw[g][:], in_=ps[pb][:, :]), reads=[bp[pb]], writes=[cx.b_cmraw[g]])
        for kc in range(8):
            s.add("pe", lambda e, kc=kc: e.matmul(
                ps[2][0:16, :], lhsT=win[:, kc, 512:528], rhs=xT[:, kc, :], start=(kc == 0), stop=(kc == 7)),
                reads=[cx.b_win, b_xT], writes=[bp[2]])
        s.add("act", lambda e: e.copy(out=cx.glT[0:16, :], in_=ps[2][0:16, :]), reads=[bp[2]], writes=[cx.b_glT])
        chunk(mt, 0, xT, b_xT, 'Y')
        for ci in range(8):
            if ci + 1 < 8:
                s.begin_record()
                chunk(mt, ci, xT, b_xT, 'X')
                lx = s.end_record()
                s.begin_record()
                chunk(mt, ci + 1, xT, b_xT, 'Y')
                ly = s.end_record()
                s.replay_zip(lx, ly)
            else:
                chunk(mt, ci, xT, b_xT, 'X')
        jj, c0 = (mt * 512) // cx.CW, (mt * 512) % cx.CW
        s.add("pool", lambda e: e.dma_start(out=cx.gohalf_d[jj][:, c0:c0 + 512].rearrange("(j p) t -> p j t", p=128),
                                            in_=cx.gomt[:]), reads=[cx.b_gomt], writes=[cx.b_gohalf_d[jj]], dma=True)

    def chunk(mt, ci, xT, b_xT, part):
        if True:
            c = mt * 8 + ci
            par = c % 2
            cs = slice(ci * 64, (ci + 1) * 64)
            wa, b_wa = cx.w_a[par], cx.b_w_a[par]
            wb, b_wb = cx.w_b[par], cx.b_w_b[par]
            wc, b_wc = cx.w_c[par], cx.b_w_c[par]
            wv, b_wv = cx.w_v[par], cx.b_w_v[par]
            gz, b_gz = cx.w_gz[par], cx.b_w_gz[par]
            kd, b_kd = cx.w_kd[par], cx.b_w_kd[par]
            e1, b_e1 = cx.w_e1[par], cx.b_w_e1[par]
            e2, b_e2 = cx.w_e2[par], cx.b_w_e2[par]
            qd, b_qd = cx.w_qd[par], cx.b_w_qd[par]
            ki, b_ki = cx.w_ki[par], cx.b_w_ki[par]
            at, b_at = cx.w_at[par], cx.b_w_at[par]
            og, b_og = cx.w_og[par], cx.b_w_og[par]
            st, b_st = cx.w_st[par], cx.b_w_st[par]
            if part == 'Y':
                for kc in range(8):
                    s.add("pe", lambda e, kc=kc: e.matmul(ps[4][0:64, :], lhsT=xT[:, kc, cs], rhs=win[:, kc, 528:1040],
                                                          start=(kc == 0), stop=(kc == 7)), reads=[cx.b_win, b_xT], writes=[bp[4]])
                s.add("act", lambda e: e.copy(out=wv[:], in_=ps[4][0:64, :]), reads=[bp[4]], writes=[b_wv])
                for kc in range(8):
                    s.add("pe", lambda e, kc=kc: e.matmul(ps[5][0:64, :], lhsT=xT[:, kc, cs], rhs=win[:, kc, 1040:1552],
                                                          start=(kc == 0), stop=(kc == 7)), reads=[cx.b_win, b_xT], writes=[bp[5]])
                s.add("act", lambda e: e.activation(out=gz[:], in_=ps[5][0:64, :], func=AF.Silu), reads=[bp[5]], writes=[b_gz])
                s.add("pool", lambda e: e.tensor_tensor(out=gz[:], in0=gz[:], in1=cx.mparam[:, 0:512], op=ALU.mult),
                      reads=[b_gz, cx.b_mparam], writes=[b_gz])
                s.add("pe", lambda e: e.matmul(ps[6][0:64, 0:256], lhsT=cx.glT[0:17, cs], rhs=cx.wup[0:17, :], start=True, stop=True),
                      reads=[cx.b_glT, cx.b_wup], writes=[bp[6]])
                s.add("act", lambda e: e.activation(out=wa[:, 0:256], in_=ps[6][0:64, 0:256], func=AF.Exp, scale=-1.0),
                      reads=[bp[6]], writes=[b_wa])
                s.add("act", lambda e: e.activation(out=wa[:, 256:512], in_=wa[:, 0:256], func=AF.Ln, bias=cx.epsq[0:64, 1:2]),
                      reads=[b_wa, cx.b_epsq], writes=[b_wa])
                nla = wa[:, 256:512]
                s.add("pe", lambda e: e.matmul(ps[6][0:64, 0:256], lhsT=U, rhs=nla, start=True, stop=True),
                      reads=[b_wa, cx.b_cst], writes=[bp[6]])
                s.add("pe", lambda e: e.matmul(ps[6][0:64, 256:512], lhsT=ONES, rhs=nla, start=True, stop=True),
                      reads=[b_wa, cx.b_cst], writes=[bp[6]])
                for h in range(2):
                    s.add("pe", lambda e, h=h: e.matmul(ps[7][:, h * 64:(h + 1) * 64], lhsT=wa[:, 256 + h * 128:256 + (h + 1) * 128], rhs=U,
                                                        start=True, stop=True), reads=[b_wa, cx.b_cst], writes=[bp[7]])
                for kc in range(8):
                    s.add("pe", lambda e, kc=kc: e.matmul(ps[4][0:64, 0:256], lhsT=xT[:, kc, cs], rhs=win[:, kc, 256:512],
                                                          start=(kc == 0), stop=(kc == 7)), reads=[cx.b_win, b_xT], writes=[bp[4]])
                s.add("act", lambda e: e.copy(out=wb[:, 0:256], in_=ps[6][0:64, 0:256]), reads=[bp[6]], writes=[b_wb])
                s.add("dve", lambda e: e.tensor_tensor(out=wb[:, 256:512], in0=ps[6][0:64, 256:512], in1=wb[:, 0:256], op=ALU.subtract),
                      reads=[bp[6], b_wb], writes=[b_wb])
                s.add("act", lambda e: e.activation(out=wb[:, 256:512], in_=wb[:, 256:512], func=AF.Exp, scale=-1.0 / 16),
                      reads=[b_wb], writes=[b_wb])
                s.add("dve", lambda e: e.tensor_tensor(out=kd[:, 0:256], in0=ps[4][0:64, 0:256], in1=wb[:, 256:512], op=ALU.mult),
                      reads=[bp[4], b_wb], writes=[b_kd])
                s.add("act", lambda e: e.activation(out=e1[:, 0:128], in_=ps[7][:, 0:128], func=AF.Exp, scale=-1.0 / 16),
                      reads=[bp[7]], writes=[b_e1])
                s.add("act", lambda e: e.activation(out=e2[:, 0:128], in_=ps[7][:, 0:128], func=AF.Exp, scale=1.0 / 16),
                      reads=[bp[7]], writes=[b_e2])
                for h in range(2):
                    s.add("dve", lambda e, h=h: e.scalar_tensor_tensor(
                        out=qd[:, h * 64:(h + 1) * 64], in0=cx.cmraw[h][:, cs], scalar=float(128 ** -0.5),
                        in1=e1[:, h * 64:(h + 1) * 64], op0=ALU.mult, op1=ALU.mult),
                        reads=[cx.b_cmraw[h], b_e1], writes=[b_qd])
                    s.add("dve", lambda e, h=h: e.tensor_tensor(
                        out=ki[:, h * 64:(h + 1) * 64], in0=cx.cmraw[2 + h][:, cs], in1=e2[:, h * 64:(h + 1) * 64], op=ALU.mult),
                        reads=[cx.b_cmraw[2 + h], b_e2], writes=[b_ki])
                for h in range(2):
                    s.add("pe", lambda e, h=h: e.matmul(ps[7][0:64, 128 + h * 64:128 + (h + 1) * 64], lhsT=ki[:, h * 64:(h + 1) * 64],
                                                        rhs=qd[:, h * 64:(h + 1) * 64], start=True, stop=True),
                          reads=[b_ki, b_qd], writes=[bp[7]])
                for h in range(2):
                    s.add("dve", lambda e, h=h: e.tensor_tensor(out=at[:, h * 64:(h + 1) * 64], in0=ps[7][0:64, 128 + h * 64:128 + (h + 1) * 64],
                                                                in1=U, op=ALU.mult), reads=[bp[7], cx.b_cst], writes=[b_at])
                return
            for h in range(2):
                s.add("pe", lambda e, h=h: e.matmul(ps[3][0:64, h * 256:(h + 1) * 256], lhsT=at[:, h * 64:(h + 1) * 64],
                                                    rhs=wv[:, h * 256:(h + 1) * 256], start=True, stop=False),
                      reads=[b_at, b_wv], writes=[bp[3]])
                s.add("pe", lambda e, h=h: e.matmul(ps[3][0:64, h * 256:(h + 1) * 256], lhsT=qd[:, h * 64:(h + 1) * 64],
                                                    rhs=cx.st_b[h][:], start=False, stop=True),
                      reads=[b_qd, cx.b_st_b[h]], writes=[bp[3]])
            for h in range(2):
                s.add("pe", lambda e, h=h: e.matmul(ps[1][:, h * 256:(h + 1) * 256], lhsT=kd[:, h * 128:(h + 1) * 128],
                                                    rhs=wv[:, h * 256:(h + 1) * 256], start=True, stop=True),
                      reads=[b_kd, b_wv], writes=[bp[1]])
            for h in range(2):
                s.add("dve", lambda e, h=h: e.scalar_tensor_tensor(
                    out=cx.st_f[h][:], in0=cx.st_f[h][:], scalar=e1[:, h * 64 + 63:h * 64 + 64], in1=ps[1][:, h * 256:(h + 1) * 256],
                    op0=ALU.mult, op1=ALU.add), reads=[cx.b_st_f[h], b_e1, bp[1]], writes=[cx.b_st_f[h]])
                s.add("act", lambda e, h=h: e.copy(out=cx.st_b[h][:], in_=cx.st_f[h][:]), reads=[cx.b_st_f[h]], writes=[cx.b_st_b[h]])
            for h in range(2):
                s.add("act", lambda e, h=h: e.activation(out=wc[:, h * 256:(h + 1) * 256], in_=ps[3][0:64, h * 256:(h + 1) * 256],
                                                         func=AF.Square, accum_out=st[:, h:h + 1]), reads=[bp[3]], writes=[b_wc, b_st])
            s.add("act", lambda e: e.activation(out=st[:, 2:4], in_=st[:, 0:2], func=AF.Ln, scale=1.0 / 256, bias=cx.epsq[0:64, 3:4]),
                  reads=[b_st, cx.b_epsq], writes=[b_st])
            s.add("act", lambda e: e.activation(out=st[:, 4:6], in_=st[:, 2:4], func=AF.Exp, scale=-0.5), reads=[b_st], writes=[b_st])
            for h in range(2):
                s.add("dve", lambda e, h=h: e.scalar_tensor_tensor(
                    out=og[:, h * 256:(h + 1) * 256], in0=ps[3][0:64, h * 256:(h + 1) * 256], scalar=st[:, 4 + h:5 + h],
                    in1=gz[:, h * 256:(h + 1) * 256], op0=ALU.mult, op1=ALU.mult), reads=[bp[3], b_st, b_gz], writes=[b_og])
            tp = cx.ps[0].bitcast(BF16)
            for j in range(4):
                s.add("pe", lambda e, j=j: e.transpose(tp[:, j * 64:(j + 1) * 64], og[:, j * 128:(j + 1) * 128], IDB),
                      reads=[b_og, cx.b_cstb], writes=[bp[0]])
            s.add("act", lambda e: e.copy(out=cx.gomt[:, :, cs], in_=tp[:, 0:256].rearrange("p (j t) -> p j t", j=4)),
                  reads=[bp[0]], writes=[cx.b_gomt])
    return emit


def gdn_setup(cx):
    def mk(name, shape, dt, n):
        return [sb(cx, f"{name}{i}", shape, dt) for i in range(n)], [Buf(f"{name}{i}") for i in range(n)]
    cx.ub, cx.b_ub = mk("ub", [128, 516], F32, 8)
    cx.qkn, cx.b_qkn = mk("qkn", [128, 512], BF16, 4)
    cx.vc, cx.b_vc = mk("vc", [128, 512], BF16, 4)
    cx.cacc, cx.b_cacc = mk("cacc", [128, 512], F32, 2)
    cx.grs, cx.b_grs = mk("grs", [128, 512], F32, 2)
    cx.gp = sb(cx, "gp", [128, 32], F32)
    cx.b_gp = Buf("gp")
    cx.nP, cx.b_nP = mk("nP", [64, 512], BF16, 2)
    cx.nQ, cx.b_nQ = mk("nQ", [64, 512], BF16, 2)
    cx.rsh, cx.b_rsh = mk("rsh", [64, 512], BF16, 1)
    cx.nR, cx.b_nR = mk("nR", [64, 512], F32, 1)
    cx.rbf, cx.b_rbf = mk("rbf", [64, 512], BF16, 2)
    cx.gb, cx.b_gb = mk("gb", [64, 256], F32, 1)
    cx.dA, cx.b_dA = mk("dA", [64, 256], F32, 1)
    cx.dT, cx.b_dT = mk("dT", [64, 256], F32, 1)
    cx.aqt, cx.b_aqt = mk("aqt", [64, 512], BF16, 2)
    cx.sm, cx.b_sm = mk("sm", [64, 64], F32, 4)
    cx.sm2, cx.b_sm2 = mk("sm2", [128, 8], F32, 4)
    cx.vb, cx.b_vb = mk("vb", [64, 512], BF16, 1)
    cx.kbe, cx.b_kbe = mk("kbe", [64, 512], BF16, 1)
    cx.kdg, cx.b_kdg = mk("kdg", [64, 512], BF16, 1)
    cx.wtn, cx.b_wtn = mk("wtn", [128, 256], BF16, 1)
    cx.vnew, cx.b_vnew = mk("vnew", [64, 512], BF16, 1)
    cx.oi, cx.b_oi = mk("oi", [64, 512], F32, 1)
    cx.oo, cx.b_oo = mk("oo", [64, 512], F32, 1)
    cx.msk = sb(cx, "msk", [64, 1024], F32)
    cx.b_msk = Buf("msk")


def gdn_load_params(cx, gp_d, mp_d):
    s = cx.s
    s.add("sp", lambda e: e.dma_start(out=cx.gp[:], in_=gp_d), writes=[cx.b_gp], dma=True)
    s.add("sp", lambda e: e.dma_start(out=cx.mparam[:], in_=mp_d), writes=[cx.b_mparam], dma=True)
    s.add("act", lambda e: e.activation(out=cx.mparam[:, 516:520], in_=cx.mparam[:, 516:520], func=AF.Exp),
          reads=[cx.b_mparam], writes=[cx.b_mparam])
    for h in range(4):
        s.add("dve", lambda e, h=h: e.memset(cx.st_f[h][:], 0.0), writes=[cx.b_st_f[h]])
        s.add("dve", lambda e, h=h: e.memset(cx.st_b[h][:], 0.0), writes=[cx.b_st_b[h]])
    for g in range(8):
        s.add("dve", lambda e, g=g: e.memset(cx.ub[g][:, 0:4], 0.0), writes=[cx.b_ub[g]])
    for h in range(4, 8):
        s.add("dve", lambda e, h=h: e.tensor_copy(out=cx.msk[:, 512 + h * 64:512 + (h + 1) * 64], in_=cx.cst[0:64, 0:64]),
              reads=[cx.b_cst], writes=[cx.b_msk])
    for h in range(4):
        s.add("dve", lambda e, h=h: e.tensor_copy(out=cx.msk[:, h * 64:(h + 1) * 64], in_=cx.cst[0:64, 768:832]),
              reads=[cx.b_cst], writes=[cx.b_msk])
        s.add("dve", lambda e, h=h: e.tensor_copy(out=cx.msk[:, 256 + h * 64:256 + (h + 1) * 64], in_=cx.cst[0:64, 128:192]),
              reads=[cx.b_cst], writes=[cx.b_msk])
        s.add("dve", lambda e, h=h: e.tensor_copy(out=cx.msk[:, 512 + h * 64:512 + (h + 1) * 64], in_=cx.cst[0:64, 0:64]),
              reads=[cx.b_cst], writes=[cx.b_msk])


GDN_STOP = 99


def gdn_proj(cx):
    s, win = cx.s, cx.win
    U = cx.cst[0:64, 128:192]
    ONES = cx.cst[0:64, 512:640]
    IDF = cx.cst[0:64, 0:64]
    IDB = cx.cstb[0:64, 0:64]
    ps, bp = cx.ps, cx.b_ps
    LS4, UI4, ID8 = cx.msk[:, 0:256], cx.msk[:, 256:512], cx.msk[:, 512:1024]

    def prep_group(g, xT, b_xT, mt):
        ub, b_ub = cx.ub[g], cx.b_ub[g]
        acc, b_acc = cx.cacc[g % 2], cx.b_cacc[g % 2]
        pb = 2
        if mt > 0:
            s.add("dve", lambda e: e.tensor_copy(out=ub[:, 1:4], in_=ub[:, 513:516]), reads=[b_ub], writes=[b_ub])
        for kc in range(8):
            s.add("pe", lambda e, kc=kc: e.matmul(ps[pb][:, :], lhsT=win[:, kc, g * 128:(g + 1) * 128], rhs=xT[:, kc, :],
                                                  start=(kc == 0), stop=(kc == 7)), reads=[cx.b_win, b_xT], writes=[bp[pb]])
        s.add("act", lambda e: e.copy(out=ub[:, 4:516], in_=ps[pb][:, :]), reads=[bp[pb]], writes=[b_ub])
        s.add("dve", lambda e: e.tensor_scalar(out=acc[:], in0=ub[:, 4:516], scalar1=cx.gp[:, g * 4 + 3:g * 4 + 4], scalar2=None,
                                               op0=ALU.mult), reads=[b_ub, cx.b_gp], writes=[b_acc])
        for j in range(3):
            s.add("dve", lambda e, j=j: e.scalar_tensor_tensor(
                out=acc[:], in0=ub[:, 1 + j:513 + j], scalar=cx.gp[:, g * 4 + j:g * 4 + j + 1], in1=acc[:],
                op0=ALU.mult, op1=ALU.add), reads=[b_ub, cx.b_gp, b_acc], writes=[b_acc])
        if g >= 4:
            vc, b_vc = cx.vc[g - 4], cx.b_vc[g - 4]
            s.add("act", lambda e: e.activation(out=vc[:], in_=acc[:], func=AF.Silu), reads=[b_acc], writes=[b_vc])
            return
        s.add("act", lambda e: e.activation(out=acc[:], in_=acc[:], func=AF.Silu), reads=[b_acc], writes=[b_acc])
        sq, b_sq = cx.junk[0][:, 0:512], cx.b_junk[0]
        rs, b_rs = cx.grs[g % 2], cx.b_grs[g % 2]
        s.add("act", lambda e: e.activation(out=sq, in_=acc[:], func=AF.Square), reads=[b_acc], writes=[b_sq])
        s.add("pe", lambda e: e.matmul(ps[3][:, :], lhsT=cx.ones_b, rhs=sq, start=True, stop=True),
              reads=[b_sq, cx.b_cstb], writes=[bp[3]])
        s.add("act", lambda e: e.activation(out=rs[:], in_=ps[3][:, :], func=AF.Ln, bias=cx.epsq[:, 3:4]),
              reads=[bp[3], cx.b_epsq], writes=[b_rs])
        s.add("act", lambda e: e.activation(out=rs[:], in_=rs[:], func=AF.Exp, scale=-0.5), reads=[b_rs], writes=[b_rs])
        qk, b_qk = cx.qkn[g], cx.b_qkn[g]
        sc = float(128 ** -0.5) if g < 2 else 1.0
        s.add("dve", lambda e: e.scalar_tensor_tensor(out=qk[:], in0=acc[:], scalar=sc, in1=rs[:], op0=ALU.mult, op1=ALU.mult),
              reads=[b_acc, b_rs], writes=[b_qk])

    def emit(mt, xT, b_xT):
        for g in range(8):
            prep_group(g, xT, b_xT, mt)
        def Y(pr):
            pre(mt, 2 * pr, xT, b_xT)
            pre(mt, 2 * pr + 1, xT, b_xT)
            neu(mt, 2 * pr, xT, b_xT)

        def X(pr):
            post(mt, 2 * pr, xT, b_xT)
            post(mt, 2 * pr + 1, xT, b_xT)

        Y(0)
        for pr in range(4):
            if pr + 1 < 4:
                s.begin_record()
                Y(pr + 1)
                ly = s.end_record()
                s.begin_record()
                X(pr)
                lx = s.end_record()
                s.replay_zip(lx, ly)
            else:
                X(pr)
        jj, c0 = (mt * 512) // cx.CW, (mt * 512) % cx.CW
        s.add("pool", lambda e: e.dma_start(out=cx.gohalf_d[jj][:, c0:c0 + 512].rearrange("(j p) t -> p j t", p=128),
                                            in_=cx.gomt[:]), reads=[cx.b_gomt], writes=[cx.b_gohalf_d[jj]], dma=True)

    def pre(mt, ci, xT, b_xT):
        c = mt * 8 + ci
        par = c % 2
        pp = (c // 2) % 2
        cs = slice(ci * 64, (ci + 1) * 64)
        sm, b_sm = cx.sm[c % 4], cx.b_sm[c % 4]
        sm2, b_sm2 = cx.sm2[c % 4], cx.b_sm2[c % 4]
        gz, b_gz = cx.w_gz[par], cx.b_w_gz[par]
        og, b_og = cx.w_og[par], cx.b_w_og[par]
        nR, b_nR = cx.nR[0], cx.b_nR[0]
        rbf, b_rbf = cx.rbf[pp], cx.b_rbf[pp]
        gb, b_gb = cx.gb[0], cx.b_gb[0]
        dA, b_dA = cx.dA[0], cx.b_dA[0]
        dT, b_dT = cx.dT[0], cx.b_dT[0]
        aqt, b_aqt = cx.aqt[pp], cx.b_aqt[pp]
        vb, b_vb = cx.vb[0], cx.b_vb[0]
        kbe, b_kbe = cx.kbe[0], cx.b_kbe[0]
        kdg, b_kdg = cx.kdg[0], cx.b_kdg[0]
        wtn, b_wtn = cx.wtn[0], cx.b_wtn[0]
        vnew, b_vnew = cx.vnew[0], cx.b_vnew[0]
        oi, b_oi = cx.oi[0], cx.b_oi[0]
        oo, b_oo = cx.oo[0], cx.b_oo[0]

        def hs(h):
            return slice(h * 64, (h + 1) * 64)

        def hv(h):
            return slice(h * 128, (h + 1) * 128)

        def hb(h):
            return slice((ci % 2) * 256 + h * 64, (ci % 2) * 256 + (h + 1) * 64)

        if GDN_STOP < 1:
            return
        if GDN_STOP < 2:
            return
        for kc in range(8):
            s.add("pe", lambda e, kc=kc: e.matmul(ps[2][0:64, 0:8], lhsT=xT[:, kc, cs], rhs=win[:, kc, 1536:1544],
                                                  start=(kc == 0), stop=(kc == 7)), reads=[cx.b_win, b_xT], writes=[bp[2]])
        s.add("dve", lambda e: e.tensor_tensor(out=sm[:, 0:4], in0=ps[2][0:64, 0:4], in1=cx.mparam[:, 512:516], op=ALU.add),
              reads=[bp[2], cx.b_mparam], writes=[b_sm])
        s.add("act", lambda e: e.activation(out=sm[:, 8:12], in_=ps[2][0:64, 4:8], func=AF.Exp, scale=-1.0), reads=[bp[2]], writes=[b_sm])
        s.add("act", lambda e: e.activation(out=sm[:, 0:4], in_=sm[:, 0:4], func=AF.Exp), reads=[b_sm], writes=[b_sm])
        s.add("act", lambda e: e.activation(out=sm[:, 0:4], in_=sm[:, 0:4], func=AF.Ln, bias=cx.epsq[0:64, 1:2]),
              reads=[b_sm, cx.b_epsq], writes=[b_sm])
        s.add("dve", lambda e: e.tensor_tensor(out=sm[:, 4:8], in0=sm[:, 0:4], in1=cx.mparam[:, 516:520], op=ALU.mult),
              reads=[b_sm, cx.b_mparam], writes=[b_sm])
        s.add("dve", lambda e: e.tensor_scalar(out=sm[:, 8:12], in0=sm[:, 8:12], scalar1=1.0, scalar2=None, op0=ALU.add),
              reads=[b_sm], writes=[b_sm])
        s.add("dve", lambda e: e.reciprocal(out=sm[:, 8:12], in_=sm[:, 8:12]), reads=[b_sm], writes=[b_sm])
        if GDN_STOP < 3:
            return
        s.add("pe", lambda e: e.matmul(ps[6][0:64, 0:4], lhsT=U, rhs=sm[:, 4:8], start=True, stop=True),
              reads=[b_sm, cx.b_cst], writes=[bp[6]])
        s.add("pe", lambda e: e.matmul(ps[6][:, 8:12], lhsT=ONES, rhs=sm[:, 4:8], start=True, stop=True),
              reads=[b_sm, cx.b_cst], writes=[bp[6]])
        s.add("dve", lambda e: e.tensor_copy(out=sm[:, 12:16], in_=ps[6][0:64, 0:4]), reads=[bp[6]], writes=[b_sm])
        s.add("act", lambda e: e.activation(out=sm[:, 16:20], in_=ps[6][0:64, 0:4], func=AF.Exp, scale=-1.0), reads=[bp[6]], writes=[b_sm])
        s.add("dve", lambda e: e.tensor_tensor(out=sm[:, 20:24], in0=ps[6][0:64, 8:12], in1=sm[:, 12:16], op=ALU.subtract),
              reads=[bp[6], b_sm], writes=[b_sm])
        s.add("act", lambda e: e.activation(out=sm[:, 20:24], in_=sm[:, 20:24], func=AF.Exp, scale=-1.0), reads=[b_sm], writes=[b_sm])
        s.add("act", lambda e: e.activation(out=sm2[:, 4:8], in_=ps[6][:, 8:12], func=AF.Exp, scale=-1.0), reads=[bp[6]], writes=[b_sm2])
        s.add("dve", lambda e: e.tensor_tensor(out=sm[:, 24:28], in0=sm[:, 8:12], in1=sm[:, 16:20], op=ALU.mult),
              reads=[b_sm], writes=[b_sm])
        if GDN_STOP < 4:
            return
        for h in range(4):
            s.add("dve", lambda e, h=h: e.tensor_copy(out=gb[:, hs(h)], in_=sm[:, 4 + h:5 + h].to_broadcast([64, 64])),
                  reads=[b_sm], writes=[b_gb])
        for h in range(4):
            s.add("pe", lambda e, h=h: e.matmul(ps[7][0:64, hs(h)], lhsT=gb[:, hs(h)], rhs=U, start=True, stop=True),
                  reads=[b_gb, cx.b_cst], writes=[bp[7]])
        for h in range(4):
            s.add("dve", lambda e, h=h: e.tensor_scalar(out=dA[:, hs(h)], in0=ps[7][0:64, hs(h)], scalar1=sm[:, 12 + h:13 + h], scalar2=0.0,
                                                        op0=ALU.subtract, op1=ALU.min), reads=[bp[7], b_sm], writes=[b_dA])
            s.add("dve", lambda e, h=h: e.tensor_scalar(out=dT[:, hs(h)], in0=ps[7][0:64, hs(h)], scalar1=sm[:, 12 + h:13 + h], scalar2=0.0,
                                                        op0=ALU.subtract, op1=ALU.max), reads=[bp[7], b_sm], writes=[b_dT])
        s.add("act", lambda e: e.activation(out=dA[:], in_=dA[:], func=AF.Exp), reads=[b_dA], writes=[b_dA])
        s.add("act", lambda e: e.activation(out=dT[:], in_=dT[:], func=AF.Exp, scale=-1.0), reads=[b_dT], writes=[b_dT])
        s.add("pool", lambda e: e.tensor_tensor(out=dA[:], in0=dA[:], in1=LS4, op=ALU.mult), reads=[b_dA, cx.b_msk], writes=[b_dA])
        s.add("pool", lambda e: e.tensor_tensor(out=dT[:], in0=dT[:], in1=UI4, op=ALU.mult), reads=[b_dT, cx.b_msk], writes=[b_dT])
        if GDN_STOP < 5:
            return
        for qh in range(2):
            kn, b_kn = cx.qkn[2 + qh], cx.b_qkn[2 + qh]
            qn, b_qn = cx.qkn[qh], cx.b_qkn[qh]
            s.add("pe", lambda e, qh=qh, kn=kn: e.matmul(ps[2][0:64, hs(qh)], lhsT=kn[:, cs], rhs=kn[:, cs], start=True, stop=True),
                  reads=[b_kn], writes=[bp[2]])
            s.add("pe", lambda e, qh=qh, kn=kn, qn=qn: e.matmul(ps[2][0:64, 128 + qh * 64:192 + qh * 64], lhsT=kn[:, cs], rhs=qn[:, cs],
                                                                start=True, stop=True), reads=[b_kn, b_qn], writes=[bp[2]])
        P0, b_P0 = cx.nP[0], cx.b_nP[0]
        for h in range(4):
            s.add("dve", lambda e, h=h: e.scalar_tensor_tensor(out=P0[:, hb(h)], in0=ps[2][0:64, hs(h // 2)], scalar=sm[:, 8 + h:9 + h],
                                                               in1=dA[:, hs(h)], op0=ALU.mult, op1=ALU.mult),
                  reads=[bp[2], b_sm, b_dA], writes=[b_P0])
            s.add("dve", lambda e, h=h: e.tensor_tensor(out=aqt[:, hb(h)], in0=ps[2][0:64, 128 + (h // 2) * 64:192 + (h // 2) * 64],
                                                        in1=dT[:, hs(h)], op=ALU.mult), reads=[bp[2], b_dT], writes=[b_aqt])

    def neu(mt, ci, xT, b_xT):
        P0, b_P0 = cx.nP[0], cx.b_nP[0]
        c = mt * 8 + ci
        par = c % 2
        pp = (c // 2) % 2
        cs = slice(ci * 64, (ci + 1) * 64)
        sm, b_sm = cx.sm[c % 4], cx.b_sm[c % 4]
        sm2, b_sm2 = cx.sm2[c % 4], cx.b_sm2[c % 4]
        gz, b_gz = cx.w_gz[par], cx.b_w_gz[par]
        og, b_og = cx.w_og[par], cx.b_w_og[par]
        nR, b_nR = cx.nR[0], cx.b_nR[0]
        rbf, b_rbf = cx.rbf[pp], cx.b_rbf[pp]
        gb, b_gb = cx.gb[0], cx.b_gb[0]
        dA, b_dA = cx.dA[0], cx.b_dA[0]
        dT, b_dT = cx.dT[0], cx.b_dT[0]
        aqt, b_aqt = cx.aqt[pp], cx.b_aqt[pp]
        vb, b_vb = cx.vb[0], cx.b_vb[0]
        kbe, b_kbe = cx.kbe[0], cx.b_kbe[0]
        kdg, b_kdg = cx.kdg[0], cx.b_kdg[0]
        wtn, b_wtn = cx.wtn[0], cx.b_wtn[0]
        vnew, b_vnew = cx.vnew[0], cx.b_vnew[0]
        oi, b_oi = cx.oi[0], cx.b_oi[0]
        oo, b_oo = cx.oo[0], cx.b_oo[0]

        def hs(h):
            return slice(h * 64, (h + 1) * 64)

        def hv(h):
            return slice(h * 128, (h + 1) * 128)

        def hb(h):
            return slice((ci % 2) * 256 + h * 64, (ci % 2) * 256 + (h + 1) * 64)

        if GDN_STOP < 1:
            return
        if GDN_STOP < 6:
            return
        Q0, b_Q0 = cx.nQ[0], cx.b_nQ[0]
        rsh, b_rsh = cx.rsh[0], cx.b_rsh[0]
        for h in range(8):
            s.add("pe", lambda e, h=h: e.matmul(ps[7][0:64, hs(h)], lhsT=P0[:, hs(h)], rhs=IDB, start=True, stop=True),
                  reads=[b_P0, cx.b_cstb], writes=[bp[7]])
        s.add("act", lambda e: e.copy(out=Q0[:], in_=ps[7][0:64, 0:512]), reads=[bp[7]], writes=[b_Q0])
        s.add("dve", lambda e: e.scalar_tensor_tensor(out=rsh[:], in0=ps[7][0:64, 0:512], scalar=-1.0, in1=ID8, op0=ALU.mult, op1=ALU.add),
              reads=[cx.b_msk, bp[7]], writes=[b_rsh])
        s.add("dve", lambda e: e.scalar_tensor_tensor(out=nR[:], in0=ps[7][0:64, 0:512], scalar=-1.0, in1=ID8, op0=ALU.mult, op1=ALU.add),
              reads=[cx.b_msk, bp[7]], writes=[b_nR])
        a = 0
        for lvl in range(1, 6):
            P, b_P, Q, b_Q = cx.nP[a], cx.b_nP[a], cx.nQ[a], cx.b_nQ[a]
            Pn, b_Pn, Qn, b_Qn = cx.nP[1 - a], cx.b_nP[1 - a], cx.nQ[1 - a], cx.b_nQ[1 - a]
            for h in range(8):
                s.add("pe", lambda e, h=h, P=P, Q=Q: e.matmul(ps[6][0:64, hs(h)], lhsT=Q[:, hs(h)], rhs=P[:, hs(h)], start=True, stop=True),
                      reads=[b_P, b_Q], writes=[bp[6]])
            s.add("act", lambda e, Pn=Pn: e.copy(out=Pn[:], in_=ps[6][0:64, 0:512]), reads=[bp[6]], writes=[b_Pn])
            if lvl < 5:
                for h in range(8):
                    s.add("pe", lambda e, h=h, P=P, Q=Q: e.matmul(ps[7][0:64, hs(h)], lhsT=P[:, hs(h)], rhs=Q[:, hs(h)], start=True, stop=True),
                          reads=[b_P, b_Q], writes=[bp[7]])
                s.add("dve", lambda e, Qn=Qn: e.tensor_copy(out=Qn[:], in_=ps[7][0:64, 0:512]), reads=[bp[7]], writes=[b_Qn])
            for h in range(8):
                s.add("pe", lambda e, h=h, Pn=Pn: e.matmul(ps[2][0:64, hs(h)], lhsT=Pn[:, hs(h)], rhs=rsh[:, hs(h)], start=True, stop=True),
                      reads=[b_Pn, b_rsh], writes=[bp[2]])
            if lvl < 5:
                s.add("dve", lambda e: e.tensor_tensor(out=rsh[:], in0=ps[2][0:64, 0:512], in1=nR[:], op=ALU.add),
                      reads=[b_nR, bp[2]], writes=[b_rsh])
                s.add("dve", lambda e: e.tensor_tensor(out=nR[:], in0=ps[2][0:64, 0:512], in1=nR[:], op=ALU.add),
                      reads=[b_nR, bp[2]], writes=[b_nR])
            else:
                s.add("dve", lambda e: e.tensor_tensor(out=rbf[:], in0=ps[2][0:64, 0:512], in1=nR[:], op=ALU.add),
                      reads=[b_nR, bp[2]], writes=[b_rbf])
            a = 1 - a

    def post(mt, ci, xT, b_xT):
        c = mt * 8 + ci
        par = c % 2
        pp = (c // 2) % 2
        cs = slice(ci * 64, (ci + 1) * 64)
        sm, b_sm = cx.sm[c % 4], cx.b_sm[c % 4]
        sm2, b_sm2 = cx.sm2[c % 4], cx.b_sm2[c % 4]
        gz, b_gz = cx.w_gz[par], cx.b_w_gz[par]
        og, b_og = cx.w_og[par], cx.b_w_og[par]
        nR, b_nR = cx.nR[0], cx.b_nR[0]
        rbf, b_rbf = cx.rbf[pp], cx.b_rbf[pp]
        gb, b_gb = cx.gb[0], cx.b_gb[0]
        dA, b_dA = cx.dA[0], cx.b_dA[0]
        dT, b_dT = cx.dT[0], cx.b_dT[0]
        aqt, b_aqt = cx.aqt[pp], cx.b_aqt[pp]
        vb, b_vb = cx.vb[0], cx.b_vb[0]
        kbe, b_kbe = cx.kbe[0], cx.b_kbe[0]
        kdg, b_kdg = cx.kdg[0], cx.b_kdg[0]
        wtn, b_wtn = cx.wtn[0], cx.b_wtn[0]
        vnew, b_vnew = cx.vnew[0], cx.b_vnew[0]
        oi, b_oi = cx.oi[0], cx.b_oi[0]
        oo, b_oo = cx.oo[0], cx.b_oo[0]

        def hs(h):
            return slice(h * 64, (h + 1) * 64)

        def hv(h):
            return slice(h * 128, (h + 1) * 128)

        def hb(h):
            return slice((ci % 2) * 256 + h * 64, (ci % 2) * 256 + (h + 1) * 64)

        if GDN_STOP < 1:
            return
        for kc in range(8):
            s.add("pe", lambda e, kc=kc: e.matmul(ps[5][0:64, :], lhsT=xT[:, kc, cs], rhs=win[:, kc, 1024:1536],
                                                  start=(kc == 0), stop=(kc == 7)), reads=[cx.b_win, b_xT], writes=[bp[5]])
        s.add("act", lambda e: e.activation(out=gz[:], in_=ps[5][0:64, :], func=AF.Silu), reads=[bp[5]], writes=[b_gz])
        s.add("pool", lambda e: e.tensor_tensor(out=gz[:], in0=gz[:], in1=cx.mparam[:, 0:512], op=ALU.mult),
              reads=[b_gz, cx.b_mparam], writes=[b_gz])
        if GDN_STOP < 8:
            return
        tpb = ps[0].bitcast(BF16)
        for g in range(6):
            src_, b_src = (cx.qkn[2 + g], cx.b_qkn[2 + g]) if g < 2 else (cx.vc[g - 2], cx.b_vc[g - 2])
            s.add("pe", lambda e, g=g, src_=src_: e.transpose(tpb[0:64, g * 128:(g + 1) * 128], src_[:, cs], cx.ident_b),
                  reads=[b_src, cx.b_cstb], writes=[bp[0]])
        for h in range(4):
            s.add("dve", lambda e, h=h: e.tensor_scalar(out=vb[:, hv(h)], in0=tpb[0:64, (2 + h) * 128:(3 + h) * 128], scalar1=sm[:, 8 + h:9 + h],
                                                        scalar2=None, op0=ALU.mult), reads=[bp[0], b_sm], writes=[b_vb])
            s.add("dve", lambda e, h=h: e.tensor_scalar(out=kbe[:, hv(h)], in0=tpb[0:64, (h // 2) * 128:(h // 2 + 1) * 128],
                                                        scalar1=sm[:, 24 + h:25 + h], scalar2=None, op0=ALU.mult),
                  reads=[bp[0], b_sm], writes=[b_kbe])
            s.add("dve", lambda e, h=h: e.tensor_scalar(out=kdg[:, hv(h)], in0=tpb[0:64, (h // 2) * 128:(h // 2 + 1) * 128],
                                                        scalar1=sm[:, 20 + h:21 + h], scalar2=None, op0=ALU.mult),
                  reads=[bp[0], b_sm], writes=[b_kdg])
        if GDN_STOP < 9:
            return
        for h in range(4):
            s.add("pe", lambda e, h=h: e.matmul(ps[1][:, hs(h)], lhsT=kbe[:, hv(h)], rhs=rbf[:, hb(h)], start=True, stop=True),
                  reads=[b_kbe, b_rbf], writes=[bp[1]])
        s.add("act", lambda e: e.mul(out=wtn[:], in_=ps[1][:, 0:256], mul=-1.0), reads=[bp[1]], writes=[b_wtn])
        for h in range(4):
            s.add("pe", lambda e, h=h: e.matmul(ps[4][0:64, hv(h)], lhsT=rbf[:, hb(h)], rhs=vb[:, hv(h)], start=True, stop=False),
                  reads=[b_rbf, b_vb], writes=[bp[4]])
            s.add("pe", lambda e, h=h: e.matmul(ps[4][0:64, hv(h)], lhsT=wtn[:, hs(h)], rhs=cx.st_b[h][:, 0:128], start=False, stop=True),
                  reads=[b_wtn, cx.b_st_b[h]], writes=[bp[4]])
        s.add("act", lambda e: e.copy(out=vnew[:], in_=ps[4][0:64, :]), reads=[bp[4]], writes=[b_vnew])
        for h in range(4):
            qn, b_qn = cx.qkn[h // 2], cx.b_qkn[h // 2]
            s.add("pe", lambda e, h=h, qn=qn: e.matmul(ps[5][0:64, hv(h)], lhsT=qn[:, cs], rhs=cx.st_b[h][:, 0:128], start=True, stop=True),
                  reads=[b_qn, cx.b_st_b[h]], writes=[bp[5]])
        for h in range(4):
            s.add("pe", lambda e, h=h: e.matmul(ps[3][0:64, hv(h)], lhsT=aqt[:, hb(h)], rhs=vnew[:, hv(h)], start=True, stop=True),
                  reads=[b_aqt, b_vnew], writes=[bp[3]])
        s.add("act", lambda e: e.copy(out=oi[:], in_=ps[3][0:64, :]), reads=[bp[3]], writes=[b_oi])
        for h in range(4):
            s.add("dve", lambda e, h=h: e.scalar_tensor_tensor(out=oo[:, hv(h)], in0=ps[5][0:64, hv(h)], scalar=sm[:, 16 + h:17 + h],
                                                               in1=oi[:, hv(h)], op0=ALU.mult, op1=ALU.add),
                  reads=[bp[5], b_sm, b_oi], writes=[b_oo])
        for h in range(4):
            s.add("pe", lambda e, h=h: e.matmul(ps[1][:, hv(h)], lhsT=kdg[:, hv(h)], rhs=vnew[:, hv(h)], start=True, stop=True),
                  reads=[b_kdg, b_vnew], writes=[bp[1]])
        for h in range(4):
            s.add("dve", lambda e, h=h: e.scalar_tensor_tensor(out=cx.st_f[h][:, 0:128], in0=cx.st_f[h][:, 0:128], scalar=sm2[:, 4 + h:5 + h],
                                                               in1=ps[1][:, hv(h)], op0=ALU.mult, op1=ALU.add),
                  reads=[cx.b_st_f[h], b_sm2, bp[1]], writes=[cx.b_st_f[h]])
            s.add("act", lambda e, h=h: e.copy(out=cx.st_b[h][:, 0:128], in_=cx.st_f[h][:, 0:128]), reads=[cx.b_st_f[h]], writes=[cx.b_st_b[h]])
        if GDN_STOP < 10:
            return
        for h in range(4):
            s.add("act", lambda e, h=h: e.activation(out=oi[:, hv(h)], in_=oo[:, hv(h)], func=AF.Square, accum_out=sm[:, 32 + h:33 + h]),
                  reads=[b_oo], writes=[b_oi, b_sm])
        s.add("act", lambda e: e.activation(out=sm[:, 36:40], in_=sm[:, 32:36], func=AF.Ln, scale=1.0 / 128, bias=cx.epsq[0:64, 3:4]),
              reads=[b_sm, cx.b_epsq], writes=[b_sm])
        s.add("act", lambda e: e.activation(out=sm[:, 40:44], in_=sm[:, 36:40], func=AF.Exp, scale=-0.5), reads=[b_sm], writes=[b_sm])
        for h in range(4):
            s.add("dve", lambda e, h=h: e.scalar_tensor_tensor(out=og[:, hv(h)], in0=oo[:, hv(h)], scalar=sm[:, 40 + h:41 + h],
                                                               in1=gz[:, hv(h)], op0=ALU.mult, op1=ALU.mult),
                  reads=[b_oo, b_sm, b_gz], writes=[b_og])
        for j in range(4):
            s.add("pe", lambda e, j=j: e.transpose(tpb[:, j * 64:(j + 1) * 64], og[:, j * 128:(j + 1) * 128], IDB),
                  reads=[b_og, cx.b_cstb], writes=[bp[0]])
        s.add("act", lambda e: e.copy(out=cx.gomt[:, :, cs], in_=tpb[:, 0:256].rearrange("p (j t) -> p j t", j=4)),
              reads=[bp[0]], writes=[cx.b_gomt])
    return emit


KINDS = ["fox", "gla", "gdn", "fox"]
NCOLS = {"fox": 2052, "gla": 1552, "gdn": 1544}


def host_params(inp, hh):
    f32 = np.float32
    P = {}
    nw = np.asarray(inp["norm_w"], f32)
    P["normw"] = np.ascontiguousarray(nw.reshape(4, 8, 128).transpose(2, 0, 1).reshape(128, 32))
    for li in range(2):
        w = np.asarray(inp["fox_w_in"][li], f32)
        s = slice(hh * 512, hh * 512 + 512)
        P[f"fox_win{li}"] = np.ascontiguousarray(np.concatenate(
            [w[:, 0:1024][:, s], w[:, 1024:2048][:, s], w[:, 3072:4096][:, s], w[:, 2048:3072][:, s],
             w[:, 4096 + hh * 4:4096 + hh * 4 + 4]], axis=1))
        fx = np.zeros((128, 16), f32)
        fx[:, 0] = np.asarray(inp["fox_q_gain"][li], f32)
        fx[:, 1] = np.asarray(inp["fox_k_gain"][li], f32)
        fx[:, 8:12] = np.asarray(inp["fox_b_f"][li], f32)[hh * 4:hh * 4 + 4][None, :]
        P[f"fox_fx{li}"] = fx
    w = np.asarray(inp["gla_w_in"][0], f32)
    P["gla_win"] = np.ascontiguousarray(np.concatenate(
        [w[:, hh * 256:hh * 256 + 256], w[:, 512 + hh * 256:512 + hh * 256 + 256], w[:, 3072:3088],
         w[:, 1024 + hh * 512:1024 + hh * 512 + 512], w[:, 2048 + hh * 512:2048 + hh * 512 + 512]], axis=1))
    P["gla_wup"] = np.ascontiguousarray(np.concatenate(
        [np.asarray(inp["gla_w_gate_up"][0], f32)[:, hh * 256:hh * 256 + 256],
         np.asarray(inp["gla_b_gate"][0], f32)[None, hh * 256:hh * 256 + 256]], axis=0))
    P["gla_gain"] = np.ascontiguousarray(np.tile(np.asarray(inp["gla_o_gain"][0], f32)[None, :], (64, 2)))
    w = np.asarray(inp["gdn_w_in"][0], f32)
    qc = slice(hh * 256, hh * 256 + 256)
    kc = slice(512 + hh * 256, 512 + hh * 256 + 256)
    vs = slice(1024 + hh * 512, 1024 + hh * 512 + 512)
    P["gdn_win"] = np.ascontiguousarray(np.concatenate(
        [w[:, qc], w[:, kc], w[:, vs], w[:, 2048 + hh * 512:2048 + hh * 512 + 512],
         w[:, 3072 + hh * 4:3072 + hh * 4 + 4], w[:, 3080 + hh * 4:3080 + hh * 4 + 4]], axis=1))
    cw = np.asarray(inp["gdn_conv_w"][0], f32)
    cwc = np.concatenate([cw[:, qc], cw[:, kc], cw[:, vs]], axis=1)
    P["gdn_gp"] = np.ascontiguousarray(cwc.reshape(4, 8, 128).transpose(2, 1, 0).reshape(128, 32))
    mp = np.zeros((64, 1024), f32)
    mp[:, 0:512] = np.tile(np.asarray(inp["gdn_o_gain"][0], f32)[None, :], (64, 4))
    mp[:, 512:516] = np.asarray(inp["gdn_dt_bias"][0], f32)[hh * 4:hh * 4 + 4][None, :]
    mp[:, 516:520] = np.asarray(inp["gdn_a_log"][0], f32)[hh * 4:hh * 4 + 4][None, :]
    P["gdn_mp"] = mp
    P["wout"] = [np.ascontiguousarray(np.asarray(inp["fox_w_out"][0], f32)), np.ascontiguousarray(np.asarray(inp["gla_w_out"][0], f32)),
                 np.ascontiguousarray(np.asarray(inp["gdn_w_out"][0], f32)), np.ascontiguousarray(np.asarray(inp["fox_w_out"][1], f32))]
    return P


GROUPS = [[0, 1], [2, 3], [4, 5], [6, 7]]


def exchange(cx):
    for j in range(cx.NCH):
        cx.s.add("pool", lambda e, j=j: e.collective_compute("AllGather", ALU.bypass, replica_groups=cx.groups,
                                                             ins=[cx.gohalf_d[j]], outs=[cx.gofull_d[j]]),
                 reads=[cx.b_gohalf_d[j]], writes=[cx.b_gofull_d[j]], dma=True, cc=True)


def build_fused(S, groups=None):
    from contextlib import ExitStack
    nc = bass.Bass("TRN2", target_bir_lowering=False)
    cx = Ctx()
    cx.nc, cx.S = nc, S
    cx.s = Sched(nc)
    cx.groups = groups or GROUPS
    setup_consts(cx)
    common_setup(cx)
    out_d = nc.dram_tensor("out", [S, 1024], F32, kind="ExternalOutput").ap()
    cx.gohalf_d = [nc.dram_tensor(f"gohalf{j}", [512, cx.CW], BF16, kind="Internal").ap() for j in range(cx.NCH)]
    cx.b_gohalf_d = [Buf(f"gohalf{j}") for j in range(cx.NCH)]

    def din(name, shape):
        return nc.dram_tensor(name, shape, F32, kind="ExternalInput").ap()
    fox_w = [din("fox_win0", [1024, 2052]), din("fox_win1", [1024, 2052])]
    cx.fx_d = [din("fox_fx0", [128, 16]), din("fox_fx1", [128, 16])]
    gla_w, wup_d, gain_d = din("gla_win", [1024, 1552]), din("gla_wup", [17, 256]), din("gla_gain", [64, 512])
    gdn_w, gp_d, mp_d = din("gdn_win", [1024, 1544]), din("gdn_gp", [128, 32]), din("gdn_mp", [64, 1024])
    wout = [din(f"wout{i}", [1024, 1024]) for i in range(4)]
    with ExitStack() as es:
        cx.es = es
        fox_setup(cx)
    cx.es = None
    mixer_common_setup(cx)
    with ExitStack() as es:
        cx.es = es
        gla_setup(cx)
    cx.es = None
    gdn_setup(cx)

    cx.s.scopes = getattr(build_fused, "scopes", False)
    cx.s.phase = "L0_proj"
    fox_load_params(cx, 0)
    boundary(cx, 0, "fox", cx.x_in, cx.xres_d, None, fox_w[0], 2052, fox_proj(cx))
    cx.s.phase = "L0_attn"
    fox_attn(cx)
    exchange(cx)
    cx.s.barrier()
    cx.s.phase = "L1_gla"
    gla_load_params(cx, wup_d, gain_d)
    boundary(cx, 1, "gla", cx.x_in, cx.xres_d, wout[0], gla_w, 1552, gla_proj(cx))
    exchange(cx)
    cx.s.barrier()
    cx.s.phase = "L2_gdn"
    gdn_load_params(cx, gp_d, mp_d)
    boundary(cx, 2, "gdn", cx.xres_d, cx.xres_d, wout[1], gdn_w, 1544, gdn_proj(cx))
    exchange(cx)
    cx.s.barrier()
    cx.s.phase = "L3_proj"
    fox_load_params(cx, 1)
    boundary(cx, 3, "fox", cx.xres_d, cx.xres_d, wout[2], fox_w[1], 2052, fox_proj(cx))
    cx.s.phase = "L3_attn"
    fox_attn(cx)
    exchange(cx)
    cx.s.phase = "L4_final"
    boundary(cx, 4, None, cx.xres_d, cx.xres_d, wout[3], None, 0, None, final_out=out_d)
    cx.s.emit()
    return nc


def kernel(**inp):
    x = np.asarray(inp["x"], np.float32)
    B, S, D = x.shape
    cst = make_consts()
    params = [host_params(inp, hh) for hh in range(2)]
    nc = build_fused(S)
    in_maps = []
    for c in range(8):
        P = params[c % 2]
        m = {"x": np.ascontiguousarray(x[c // 2]), "cst": cst, "normw_in": P["normw"],
             "fox_win0": P["fox_win0"], "fox_win1": P["fox_win1"], "fox_fx0": P["fox_fx0"], "fox_fx1": P["fox_fx1"],
             "gla_win": P["gla_win"], "gla_wup": P["gla_wup"], "gla_gain": P["gla_gain"],
             "gdn_win": P["gdn_win"], "gdn_gp": P["gdn_gp"], "gdn_mp": P["gdn_mp"]}
        for i in range(4):
            m[f"wout{i}"] = P["wout"][i]
        in_maps.append(m)
    res = run_bass_kernel_spmd(nc, in_maps, core_ids=list(range(8)))
    out = np.empty((B, S, D), np.float32)
    for b in range(B):
        out[b] = np.asarray(res.results[2 * b]["out"])
    return out
```

```python
import numpy as np
import concourse.bass as bass
import concourse.mybir as mybir
from concourse.bass_utils import run_bass_kernel_spmd

F32 = mybir.dt.float32
BF16 = mybir.dt.bfloat16
AF = mybir.ActivationFunctionType
ALU = mybir.AluOpType
AX = mybir.AxisListType

ENGS = ["pe", "act", "dve", "pool", "sp"]
NDSEM = 8
SEM_ROLL = 20000


class Buf:
    __slots__ = ("name", "lw", "rd", "rd_dma", "psum", "wr_dma")

    def __init__(self, name="", psum=False):
        self.name = name
        self.psum = psum
        self.lw = None
        self.rd = {}
        self.rd_dma = []
        self.wr_dma = []


class Op:
    __slots__ = ("eng", "fn", "deps", "sig", "idx", "dma", "seq", "sem", "val", "barred", "cc", "phase")


class Sched:
    def __init__(self, nc):
        self.nc = nc
        self.phase = None
        self.scopes = False
        self.q = {e: [] for e in ENGS}

    def begin_record(self):
        self._rec = []

    def end_record(self):
        r, self._rec = self._rec, None
        return r

    def replay_zip(self, a, b):
        ia = ib = 0
        while ia < len(a) or ib < len(b):
            if ib >= len(b) or (ia < len(a) and ia * len(b) <= ib * len(a)):
                self.add(*a[ia])
                ia += 1
            else:
                self.add(*b[ib])
                ib += 1

    def add(self, eng, fn, reads=(), writes=(), dma=False, cc=False):
        if getattr(self, "_rec", None) is not None:
            self._rec.append((eng, fn, tuple(reads), tuple(writes), dma, cc))
            return None
        op = Op()
        op.eng, op.fn, op.dma, op.sig, op.cc = eng, fn, dma, False, cc
        op.seq = op.sem = op.val = None
        op.barred = False
        op.phase = self.phase
        deps, seen = [], set()

        def adddep(d):
            if d is None or id(d) in seen:
                return
            seen.add(id(d))
            deps.append(d)

        for b in reads:
            adddep(b.lw)
            for w in b.wr_dma:
                adddep(w)
            if b.psum:
                for e2, r in b.rd.items():
                    if e2 != eng:
                        adddep(r)
        for b in writes:
            had_readers = bool(b.rd) or bool(b.rd_dma)
            for r in b.rd.values():
                adddep(r)
            for r in b.rd_dma:
                adddep(r)
            if dma and not cc:
                if had_readers or (b.lw is not None and not b.lw.dma):
                    adddep(b.lw)
                    b.wr_dma = []
            else:
                adddep(b.lw)
                for w in b.wr_dma:
                    adddep(w)
        op.deps = [d for d in deps
                   if not (eng == "pe" and d.eng == "pe" and not d.dma and not dma)]
        for b in reads:
            if dma:
                b.rd_dma.append(op)
            else:
                b.rd[eng] = op
        for b in writes:
            b.rd = {}
            b.rd_dma = []
            if dma and not cc:
                b.wr_dma.append(op)
                b.lw = None
            else:
                b.lw = op
                b.wr_dma = []
        op.idx = len(self.q[eng])
        self.q[eng].append(op)
        return op

    def barrier(self):
        pre = []
        for e in ENGS:
            last = None
            for op in self.q[e]:
                if op.dma:
                    if not getattr(op, "barred", False):
                        pre.append(op)
                        op.barred = True
                else:
                    last = op
            if last is not None:
                pre.append(last)
        for e in ENGS:
            op = self.add(e, lambda eng: eng.nop())
            op.deps = [d for d in pre if d is not op]

    def emit(self):
        nc = self.nc
        for e in ENGS:
            for op in self.q[e]:
                for d in op.deps:
                    d.sig = True
        csem = {}
        for e in ENGS:
            nsig = sum(1 for op in self.q[e] if op.sig and not op.dma)
            csem[e] = [nc.alloc_semaphore(f"c_{e}_{i}") for i in range(nsig // SEM_ROLL + 1)]
            s = 0
            for op in self.q[e]:
                if op.sig and not op.dma:
                    op.sem = csem[e][s // SEM_ROLL]
                    op.val = s % SEM_ROLL + 1
                    op.seq = s
                    s += 1
        dsem = {}
        for e in ENGS:
            nd = sum(1 for op in self.q[e] if op.dma)
            if nd == 0:
                continue
            dsem[e] = [nc.alloc_semaphore(f"d_{e}_{i}") for i in range(NDSEM)]
            n = 0
            for op in self.q[e]:
                if op.dma and op.cc:
                    op.sem = nc.alloc_semaphore(f"cc_{e}_{op.idx}")
                    op.val = 1
                elif op.dma:
                    op.sem = dsem[e][n % NDSEM]
                    op.val = 16 * (n // NDSEM + 1)
                    n += 1

        def run_engine(e, eng):
            waited = {}

            def wait(sem, val):
                key = sem.num
                if waited.get(key, 0) >= val:
                    return
                waited[key] = val
                eng.wait_ge(sem, val)

            cur = [None, None]

            def scope(ph):
                if not self.scopes or ph == cur[0]:
                    return
                if cur[1] is not None:
                    cur[1].__exit__(None, None, None)
                    cur[1] = None
                cur[0] = ph
                if ph is not None:
                    cur[1] = nc.named_scope(ph)
                    cur[1].__enter__()

            for op in self.q[e]:
                scope(op.phase)
                for d in op.deps:
                    wait(d.sem, d.val)
                if op.dma and op.cc:
                    ins = op.fn(eng)
                    ins.then_inc(op.sem)
                elif op.dma:
                    if op.val > 16:
                        wait(op.sem, op.val - 16)
                    ins = op.fn(eng)
                    ins.then_inc(op.sem, 16)
                else:
                    ins = op.fn(eng)
                    if op.sig:
                        ins.then_inc(op.sem, 1)
            scope(None)
            if e in dsem:
                last = {}
                for op in self.q[e]:
                    if op.dma:
                        last[op.sem.num] = (op.sem, max(op.val, last.get(op.sem.num, (None, 0))[1]))
                for sem, val in last.values():
                    wait(sem, val)

        with nc.Block() as block:
            @block.tensor
            def _(eng):
                run_engine("pe", eng)

            @block.scalar
            def _(eng):
                run_engine("act", eng)

            @block.vector
            def _(eng):
                run_engine("dve", eng)

            @block.gpsimd
            def _(eng):
                run_engine("pool", eng)

            @block.sync
            def _(eng):
                run_engine("sp", eng)


D_MODEL = 1024
NB = 4
RMS_EPS = 1e-6
NCST = 7 * 128


def make_consts():
    c = np.zeros((128, NCST), np.float32)
    i = np.arange(128)
    c[:, 0:128] = np.eye(128)
    c[:, 128:256] = (i[:, None] <= i[None, :])
    same = (i[:, None] // 64) == (i[None, :] // 64)
    c[:, 256:384] = (i[:, None] <= i[None, :]) & same
    c[:, 384:512] = (i[:, None] < i[None, :]) & same
    c[:, 512:640] = 1.0
    c[:, 640:768] = same
    c[:, 768:896] = (i[:, None] > i[None, :]) & same
    return c


class Ctx:
    pass


def sb(cx, name, shape, dt):
    es = getattr(cx, "es", None)
    if es is not None:
        return es.enter_context(cx.nc.sbuf_tensor(name, shape, dt))
    return cx.nc.alloc_sbuf_tensor(name, shape, dt)


def setup_consts(cx):
    nc, S = cx.nc, cx.S
    cx.cst_d = nc.dram_tensor("cst", [128, NCST], F32, kind="ExternalInput").ap()
    cx.cst = sb(cx, "cst_sb", [128, NCST], F32)
    cx.cstb = sb(cx, "cstb_sb", [128, NCST], BF16)
    cx.b_cst = Buf("cst")
    cx.b_cstb = Buf("cstb")
    cx.s.add("sp", lambda e: e.dma_start(out=cx.cst[:], in_=cx.cst_d), writes=[cx.b_cst], dma=True)
    cx.s.add("dve", lambda e: e.tensor_copy(out=cx.cstb[:], in_=cx.cst[:]), reads=[cx.b_cst], writes=[cx.b_cstb])
    cx.ident_b = cx.cstb[:, 0:128]
    cx.ones_b = cx.cstb[:, 512:640]
    cx.tri_f = cx.cst[:, 128:256]
    cx.tri_b = cx.cstb[:, 128:256]
    cx.epsq = sb(cx, "epsq", [128, 4], F32)
    cx.b_epsq = Buf("epsq")
    cx.s.add("dve", lambda e: e.memset(cx.epsq[:, 0:1], float(128 * RMS_EPS)), writes=[cx.b_epsq])
    cx.s.add("dve", lambda e: e.memset(cx.epsq[:, 1:2], 1.0), writes=[cx.b_epsq])
    cx.s.add("dve", lambda e: e.memset(cx.epsq[:, 2:3], float(D_MODEL * RMS_EPS)), writes=[cx.b_epsq])
    cx.s.add("dve", lambda e: e.memset(cx.epsq[:, 3:4], float(RMS_EPS)), writes=[cx.b_epsq])
    cx.ident_f = cx.cst[:, 0:128]
    cx.ones_f = cx.cst[:, 512:640]
    cx.ps = [nc.alloc_psum_tensor(f"ps{i}", [128, 512], F32) for i in range(8)]
    cx.b_ps = [Buf(f"ps{i}", psum=True) for i in range(8)]


def load_weights(cx, tag, w_d, ncols, nw_row, wbuf, b_w):
    s = cx.s
    n = 0
    for kc in range(8):
        for c0 in range(0, ncols, 516):
            c1 = min(ncols, c0 + 516)
            st = cx.wst[n % 2]
            b_st = cx.b_wst[n % 2]
            n += 1
            s.add("sp", lambda e, st=st, kc=kc, c0=c0, c1=c1: e.dma_start(out=st[:, 0:c1 - c0], in_=w_d[kc * 128:(kc + 1) * 128, c0:c1]),
                  writes=[b_st], dma=True)
            if nw_row is not None:
                s.add("dve", lambda e, st=st, kc=kc, c0=c0, c1=c1: e.tensor_scalar(
                    out=wbuf[:, kc, c0:c1], in0=st[:, 0:c1 - c0], scalar1=cx.normw[:, nw_row * 8 + kc: nw_row * 8 + kc + 1],
                    scalar2=None, op0=ALU.mult), reads=[b_st, cx.b_normw], writes=[b_w])
            else:
                s.add("dve", lambda e, st=st, kc=kc, c0=c0, c1=c1: e.tensor_copy(out=wbuf[:, kc, c0:c1], in_=st[:, 0:c1 - c0]),
                      reads=[b_st], writes=[b_w])


def boundary(cx, L, kind, x_src, x_dst, wout_d, win_d, ncols, emit_proj, final_out=None, tok_range=None):
    nc, s, S = cx.nc, cx.s, cx.S
    MT = S // 512
    has_out = wout_d is not None
    if has_out:
        load_weights(cx, f"wo{L}", wout_d, 1024, None, cx.wout, cx.b_wout)
    if win_d is not None:
        load_weights(cx, f"wi{L}", win_d, ncols, L, cx.win, cx.b_win)
    mts = list(range(MT) if tok_range is None else tok_range)
    dst = final_out if final_out is not None else x_dst

    def phase_a(mt):
        if has_out:
            go, b_go = cx.goT[0], cx.b_goT[0]
            jj, c0 = (mt * 512) // cx.CW, (mt * 512) % cx.CW
            s.add("sp", lambda e: e.dma_start(
                out=go[:], in_=cx.gofull_d[jj][:, c0:c0 + 512].rearrange("(c p) t -> p c t", p=128)),
                reads=[cx.b_gofull_d[jj]], writes=[b_go], dma=True)
        for sub in range(4):
            sub_a(mt, sub)

    def sub_a(mt, sub):
        tt = mt * 4 + sub
        xt, b_xt = cx.xt[tt % 2], cx.b_xt[tt % 2]
        s.add("sp", lambda e: e.dma_start(out=xt[:], in_=x_src[tt * 128:(tt + 1) * 128, :]),
              reads=[cx.b_xres_d[tt]] if x_src is cx.xres_d else [], writes=[b_xt], dma=True)
        if has_out:
            go, b_go = cx.goT[0], cx.b_goT[0]
            for half in range(2):
                yp, b_yp = cx.ps[1], cx.b_ps[1]
                for kc in range(8):
                    s.add("pe", lambda e, kc=kc, half=half: e.matmul(
                        yp[:, :], lhsT=go[:, kc, sub * 128:(sub + 1) * 128],
                        rhs=cx.wout[:, kc, half * 512:(half + 1) * 512], start=(kc == 0), stop=(kc == 7)),
                        reads=[b_go, cx.b_wout], writes=[b_yp])
                s.add("dve", lambda e, half=half: e.tensor_tensor(
                    out=xt[:, half * 512:(half + 1) * 512], in0=yp[:, :], in1=xt[:, half * 512:(half + 1) * 512],
                    op=ALU.add), reads=[b_yp, b_xt], writes=[b_xt])
            s.add("pool", lambda e: e.dma_start(out=dst[tt * 128:(tt + 1) * 128, :], in_=xt[:]),
                  reads=[b_xt], writes=[cx.b_xres_d[tt]], dma=True)
        if win_d is None:
            return
        sq, b_sq = cx.junk[tt % 2], cx.b_junk[tt % 2]
        ssq, b_ssq = cx.stat[tt % 4], cx.b_stat[tt % 4]
        s.add("act", lambda e: e.activation(out=sq[:], in_=xt[:], func=AF.Square, accum_out=ssq[:, 0:1]),
              reads=[b_xt], writes=[b_sq, b_ssq])
        s.add("act", lambda e: e.activation(out=ssq[:, 2:3], in_=ssq[:, 0:1], func=AF.Ln, bias=cx.epsq[:, 2:3]),
              reads=[b_ssq, cx.b_epsq], writes=[b_ssq])
        s.add("act", lambda e: e.activation(out=ssq[:, 1:2], in_=ssq[:, 2:3], func=AF.Exp, scale=-0.5),
              reads=[b_ssq], writes=[b_ssq])
        xs, b_xs = cx.xs[tt % 4], cx.b_xs[tt % 4]
        s.add("dve", lambda e: e.tensor_scalar(
            out=xs[:], in0=xt[:], scalar1=ssq[:, 1:2], scalar2=float(np.sqrt(D_MODEL)),
            op0=ALU.mult, op1=ALU.mult), reads=[b_xt, b_ssq], writes=[b_xs])

    def phase_t(mt):
        xT, b_xT = cx.xT[mt % 2], cx.b_xT[mt % 2]
        tp = cx.ps[0].bitcast(BF16)
        b_tp = cx.b_ps[0]
        for sub in range(4):
            tt = mt * 4 + sub
            xs, b_xs = cx.xs[tt % 4], cx.b_xs[tt % 4]
            for kc in range(8):
                s.add("pe", lambda e, kc=kc, xs=xs: e.transpose(
                    tp[:, kc * 128:(kc + 1) * 128], xs[:, kc * 128:(kc + 1) * 128], cx.ident_b),
                    reads=[b_xs, cx.b_cstb], writes=[b_tp])
            s.add("act", lambda e, sub=sub: e.copy(
                out=xT[:, :, sub * 128:(sub + 1) * 128], in_=tp[:, :].rearrange("p (c t) -> p c t", c=8)),
                reads=[b_tp], writes=[b_xT])

    if win_d is None:
        for mt in mts:
            phase_a(mt)
        return
    phase_a(mts[0])
    phase_t(mts[0])
    for i, mt in enumerate(mts):
        nxt = mts[i + 1] if i + 1 < len(mts) else None
        if nxt is not None:
            phase_a(nxt)
        emit_proj(mt, cx.xT[mt % 2], cx.b_xT[mt % 2])
        if nxt is not None:
            phase_t(nxt)
    if hasattr(emit_proj, "flush"):
        emit_proj.flush()


def fox_setup(cx):
    nc, S = cx.nc, cx.S
    cx.qT_d = nc.dram_tensor("qT_d", [4, 128, S], BF16, kind="Internal").ap()
    cx.kT_d = nc.dram_tensor("kT_d", [4, 128, S], BF16, kind="Internal").ap()
    cx.gT_d = nc.dram_tensor("gT_d", [4, 128, S], BF16, kind="Internal").ap()
    cx.v_d = nc.dram_tensor("v_d", [S, 512], BF16, kind="Internal").ap()
    cx.crow_d = nc.dram_tensor("crow_d", [4, S], F32, kind="Internal").ap()
    cx.b_qT_d, cx.b_kT_d, cx.b_gT_d, cx.b_v_d, cx.b_crow_d = (Buf("qT_d"), Buf("kT_d"), Buf("gT_d"), Buf("v_d"), Buf("crow_d"))
    cx.fxp = sb(cx, "fxp", [128, 16], F32)
    cx.b_fxp = Buf("fxp")
    cx.cumcol = sb(cx, "cumcol", [128, S // 128, 4], F32)
    cx.b_cumcol = Buf("cumcol")
    cx.carry = sb(cx, "carry", [128, 4], F32)
    cx.b_carry = Buf("carry")
    cx.fl = [sb(cx, f"fl{i}", [128, 16], F32) for i in range(2)]
    cx.b_fl = [Buf(f"fl{i}") for i in range(2)]
    cx.crow_sb = [sb(cx, f"crow{i}", [4, 128], F32) for i in range(2)]
    cx.b_crow_sb = [Buf(f"crow{i}") for i in range(2)]
    cx.sqb = [sb(cx, f"sqb{i}", [128, 512], BF16) for i in range(2)]
    cx.b_sqb = [Buf(f"sqb{i}") for i in range(2)]
    cx.rs = [sb(cx, f"rs{i}", [128, 512], F32) for i in range(2)]
    cx.b_rs = [Buf(f"rs{i}") for i in range(2)]
    cx.ob = [sb(cx, f"ob{i}", [128, 512], BF16) for i in range(4)]
    cx.b_ob = [Buf(f"ob{i}") for i in range(4)]
    cx.b_p2 = [Buf(f"p2_{i}") for i in range(4)]
    cx.kT_sb = sb(cx, "kT_sb", [128, S], BF16)
    cx.b_kT_sb = Buf("kT_sb")
    cx.v_sb = sb(cx, "v_sb", [128, S // 128, 128], BF16)
    cx.b_v_sb = Buf("v_sb")
    cx.q_sb = [sb(cx, f"q_sb{i}", [128, 512], BF16) for i in range(2)]
    cx.b_q_sb = [Buf(f"q_sb{i}") for i in range(2)]
    cx.g_sb = [sb(cx, f"g_sb{i}", [128, 512], BF16) for i in range(2)]
    cx.b_g_sb = [Buf(f"g_sb{i}") for i in range(2)]
    cx.cnq = [sb(cx, f"cnq{i}", [128, 512], F32) for i in range(2)]
    cx.b_cnq = [Buf(f"cnq{i}") for i in range(2)]
    cx.tt_sb = [sb(cx, f"tt_sb{i}", [128, 512], F32) for i in range(4)]
    cx.b_tt_sb = [Buf(f"tt_sb{i}") for i in range(4)]
    cx.p_sb = [sb(cx, f"p_sb{i}", [128, 512], BF16) for i in range(4)]
    cx.b_p_sb = [Buf(f"p_sb{i}") for i in range(4)]
    cx.rl = [sb(cx, f"rl{i}", [128, 512], F32) for i in range(2)]
    cx.b_rl = [Buf(f"rl{i}") for i in range(2)]
    cx.o1 = [sb(cx, f"o1{i}", [128, 512], F32) for i in range(2)]
    cx.b_o1 = [Buf(f"o1{i}") for i in range(2)]
    cx.go_sb = [sb(cx, f"go_sb{i}", [128, 512], BF16) for i in range(2)]
    cx.b_go_sb = [Buf(f"go_sb{i}") for i in range(2)]


def fox_load_params(cx, li):
    s = cx.s
    d = cx.fx_d[li]
    s.add("sp", lambda e: e.dma_start(out=cx.fxp[:], in_=d), writes=[cx.b_fxp], dma=True)
    s.add("dve", lambda e: e.tensor_scalar(out=cx.fxp[:, 1:2], in0=cx.fxp[:, 1:2], scalar1=float(np.sqrt(128.0)),
                                           scalar2=None, op0=ALU.mult), reads=[cx.b_fxp], writes=[cx.b_fxp])
    s.add("dve", lambda e: e.memset(cx.carry[:], 0.0), writes=[cx.b_carry])


def fox_proj(cx):
    s = cx.s
    win = cx.win

    p2 = cx.ps[2]
    b_p2 = cx.b_ps[2]
    pend = []

    def phaseA(tt, sub, xT, b_xT):
        for kc in range(8):
            s.add("pe", lambda e, kc=kc: e.matmul(
                p2[:, 0:4], lhsT=xT[:, kc, sub * 128:(sub + 1) * 128], rhs=win[:, kc, 2048:2052],
                start=(kc == 0), stop=(kc == 7)), reads=[cx.b_win, b_xT], writes=[b_p2])
        fl, b_fl = cx.fl[tt % 2], cx.b_fl[tt % 2]
        s.add("dve", lambda e: e.tensor_tensor(out=fl[:, 0:4], in0=p2[:, 0:4], in1=cx.fxp[:, 8:12], op=ALU.add),
              reads=[b_p2, cx.b_fxp], writes=[b_fl])
        s.add("act", lambda e: e.activation(out=fl[:, 4:8], in_=fl[:, 0:4], func=AF.Exp, scale=-1.0),
              reads=[b_fl], writes=[b_fl])
        s.add("act", lambda e: e.activation(out=fl[:, 8:12], in_=fl[:, 4:8], func=AF.Ln, bias=cx.epsq[:, 1:2]),
              reads=[b_fl, cx.b_epsq], writes=[b_fl])

    def phaseB(tt):
        fl, b_fl = cx.fl[tt % 2], cx.b_fl[tt % 2]
        s.add("pe", lambda e: e.matmul(p2[:, 8:12], lhsT=cx.tri_f, rhs=fl[:, 8:12], start=True, stop=True),
              reads=[b_fl, cx.b_cst], writes=[b_p2])
        s.add("pe", lambda e: e.matmul(p2[:, 16:20], lhsT=cx.ones_f, rhs=fl[:, 8:12], start=True, stop=True),
              reads=[b_fl, cx.b_cst], writes=[b_p2])
        s.add("dve", lambda e: e.tensor_tensor(out=cx.cumcol[:, tt, :], in0=p2[:, 8:12], in1=cx.carry[:], op=ALU.add),
              reads=[b_p2, cx.b_carry], writes=[cx.b_cumcol])
        s.add("dve", lambda e: e.tensor_tensor(out=cx.carry[:], in0=p2[:, 16:20], in1=cx.carry[:], op=ALU.add),
              reads=[b_p2, cx.b_carry], writes=[cx.b_carry])

    def phaseC(tt):
        s.add("pe", lambda e: e.matmul(p2[0:4, 32:160], lhsT=cx.cumcol[:, tt, :], rhs=cx.ident_f, start=True, stop=True),
              reads=[cx.b_cumcol, cx.b_cst], writes=[b_p2])
        cr, b_cr = cx.crow_sb[tt % 2], cx.b_crow_sb[tt % 2]
        s.add("dve", lambda e: e.tensor_copy(out=cr[:], in_=p2[0:4, 32:160]), reads=[b_p2], writes=[b_cr])
        s.add("pool", lambda e: e.dma_start(out=cx.crow_d[:, tt * 128:(tt + 1) * 128], in_=cr[:]),
              reads=[b_cr], writes=[cx.b_crow_d], dma=True)

    def fchain(tt, sub, xT, b_xT):
        phaseA(tt, sub, xT, b_xT)
        if tt >= 1:
            phaseB(tt - 1)
        if tt >= 2:
            phaseC(tt - 2)

    def flush():
        TT = cx.S // 128
        phaseB(TT - 1)
        phaseC(TT - 2)
        phaseC(TT - 1)

    def emit(mt, xT, b_xT):
        for g in range(12):
            typ, h = g // 4, g % 4
            pb = 3 + (g % 2)
            ps, b_p = cx.ps[pb], cx.b_ps[pb]
            for kc in range(8):
                s.add("pe", lambda e, kc=kc, g=g, ps=ps: e.matmul(
                    ps[:, :], lhsT=win[:, kc, g * 128:(g + 1) * 128], rhs=xT[:, kc, :], start=(kc == 0), stop=(kc == 7)),
                    reads=[cx.b_win, b_xT], writes=[b_p])
            ob, b_ob = cx.ob[g % 4], cx.b_ob[g % 4]
            if typ < 2:
                sq, b_sq = cx.sqb[g % 2], cx.b_sqb[g % 2]
                rs, b_rs = cx.rs[g % 2], cx.b_rs[g % 2]
                s.add("act", lambda e, sq=sq, ps=ps: e.activation(out=sq[:], in_=ps[:, :], func=AF.Square),
                      reads=[b_p], writes=[b_sq])
                p5, b_p5 = cx.ps[5], cx.b_ps[5]
                s.add("pe", lambda e, sq=sq, p5=p5: e.matmul(p5[:, :], lhsT=cx.ones_b, rhs=sq[:], start=True, stop=True),
                      reads=[b_sq, cx.b_cstb], writes=[b_p5])
                s.add("act", lambda e, rs=rs, p5=p5: e.activation(
                    out=rs[:], in_=p5[:, :], func=AF.Ln, bias=cx.epsq[:, 0:1]),
                    reads=[b_p5, cx.b_epsq], writes=[b_rs])
                s.add("act", lambda e, rs=rs: e.activation(out=rs[:], in_=rs[:], func=AF.Exp, scale=-0.5),
                      reads=[b_rs], writes=[b_rs])
                s.add("dve", lambda e, ob=ob, ps=ps, rs=rs, typ=typ: e.scalar_tensor_tensor(
                    out=ob[:], in0=ps[:, :], scalar=cx.fxp[:, typ:typ + 1], in1=rs[:], op0=ALU.mult, op1=ALU.mult),
                    reads=[b_p, b_rs, cx.b_fxp], writes=[b_ob])
                dst, b_dst = (cx.qT_d, cx.b_qT_d) if typ == 0 else (cx.kT_d, cx.b_kT_d)
            else:
                s.add("act", lambda e, ob=ob, ps=ps: e.activation(out=ob[:], in_=ps[:, :], func=AF.Silu),
                      reads=[b_p], writes=[b_ob])
                dst, b_dst = cx.gT_d, cx.b_gT_d
            s.add("pool", lambda e, ob=ob, dst=dst, h=h, mt=mt: e.dma_start(
                out=dst[h, :, mt * 512:(mt + 1) * 512], in_=ob[:]), reads=[b_ob], writes=[b_dst], dma=True)
        for sub in range(4):
            tt = mt * 4 + sub
            pb = 6 + (sub % 2)
            ps, b_p = cx.ps[pb], cx.b_ps[pb]
            for kc in range(8):
                s.add("pe", lambda e, kc=kc, sub=sub, ps=ps: e.matmul(
                    ps[:, :], lhsT=xT[:, kc, sub * 128:(sub + 1) * 128], rhs=win[:, kc, 1536:2048],
                    start=(kc == 0), stop=(kc == 7)), reads=[cx.b_win, b_xT], writes=[b_p])
            ob, b_ob = cx.ob[sub % 4], cx.b_ob[sub % 4]
            s.add("act", lambda e, ob=ob, ps=ps: e.copy(out=ob[:], in_=ps[:, :]), reads=[b_p], writes=[b_ob])
            s.add("pool", lambda e, ob=ob, tt=tt: e.dma_start(out=cx.v_d[tt * 128:(tt + 1) * 128, :], in_=ob[:]),
                  reads=[b_ob], writes=[cx.b_v_d], dma=True)
            fchain(tt, sub, xT, b_xT)
    emit.flush = flush
    return emit


def fox_attn(cx):
    s, S = cx.s, cx.S
    NQ = S // 512
    for h in range(4):
        s.add("sp", lambda e, h=h: e.dma_start(out=cx.kT_sb[:], in_=cx.kT_d[h]), reads=[cx.b_kT_d], writes=[cx.b_kT_sb], dma=True)
        s.add("sp", lambda e, h=h: e.dma_start(
            out=cx.v_sb[:], in_=cx.v_d[:, h * 128:(h + 1) * 128].rearrange("(kb p) d -> p kb d", p=128)),
            reads=[cx.b_v_d], writes=[cx.b_v_sb], dma=True)
        steps = [(T, kb) for T in range(NQ) for kb in range(4 * T + 4)]
        LOOK = 3

        def front(i, h=h):
            T, kb = steps[i]
            if kb == 0:
                q, b_q = cx.q_sb[T % 2], cx.b_q_sb[T % 2]
                g, b_g = cx.g_sb[T % 2], cx.b_g_sb[T % 2]
                cn, b_cn = cx.cnq[T % 2], cx.b_cnq[T % 2]
                s.add("sp", lambda e: e.dma_start(out=q[:], in_=cx.qT_d[h, :, T * 512:(T + 1) * 512]),
                      reads=[cx.b_qT_d], writes=[b_q], dma=True)
                s.add("sp", lambda e: e.dma_start(out=g[:], in_=cx.gT_d[h, :, T * 512:(T + 1) * 512]),
                      reads=[cx.b_gT_d], writes=[b_g], dma=True)
                s.add("sp", lambda e: e.dma_start(out=cn[:], in_=cx.crow_d[h:h + 1, T * 512:(T + 1) * 512].partition_broadcast(128)),
                      reads=[cx.b_crow_d], writes=[b_cn], dma=True)
            q, b_q = cx.q_sb[T % 2], cx.b_q_sb[T % 2]
            cn, b_cn = cx.cnq[T % 2], cx.b_cnq[T % 2]
            j = kb - 4 * T
            c0 = max(0, j) * 128
            sp_, b_sp = cx.ps[i % 4], cx.b_ps[i % 4]
            tt_, b_tt = cx.tt_sb[i % 4], cx.b_tt_sb[i % 4]
            p_, b_pp = cx.p_sb[i % 4], cx.b_p_sb[i % 4]
            s.add("pe", lambda e: e.matmul(sp_[:, c0:512], lhsT=cx.kT_sb[:, kb * 128:(kb + 1) * 128], rhs=q[:, c0:512],
                                           start=True, stop=True), reads=[cx.b_kT_sb, b_q], writes=[b_sp])
            s.add("dve", lambda e: e.scalar_tensor_tensor(
                out=tt_[:, c0:512], in0=sp_[:, c0:512], scalar=cx.cumcol[:, kb, h:h + 1], in1=cn[:, c0:512],
                op0=ALU.add, op1=ALU.subtract), reads=[b_sp, cx.b_cumcol, b_cn], writes=[b_tt])
            s.add("act", lambda e: e.activation(out=p_[:, c0:512], in_=tt_[:, c0:512], func=AF.Exp),
                  reads=[b_tt], writes=[b_pp])
            if j >= 0:
                s.add("pool", lambda e: e.tensor_tensor(out=p_[:, c0:c0 + 128], in0=p_[:, c0:c0 + 128], in1=cx.tri_b, op=ALU.mult),
                      reads=[b_pp, cx.b_cstb], writes=[b_pp])

        def back(i, h=h):
            T, kb = steps[i]
            j = kb - 4 * T
            c0 = max(0, j) * 128
            p_, b_pp = cx.p_sb[i % 4], cx.b_p_sb[i % 4]
            op_, b_op = cx.ps[4 + T % 2], cx.b_ps[4 + T % 2]
            lp_, b_lp = cx.ps[6 + T % 2], cx.b_ps[6 + T % 2]
            last = (kb == 4 * T + 3)
            s.add("pe", lambda e: e.matmul(op_[:, c0:512], lhsT=cx.v_sb[:, kb, :], rhs=p_[:, c0:512],
                                           start=(kb == 0), stop=last, skip_group_check=True), reads=[cx.b_v_sb, b_pp], writes=[b_op])
            s.add("pe", lambda e: e.matmul(lp_[:, c0:512], lhsT=cx.ones_b, rhs=p_[:, c0:512],
                                           start=(kb == 0), stop=last, skip_group_check=True), reads=[cx.b_cstb, b_pp], writes=[b_lp])
            if last:
                rl, b_rl = cx.rl[T % 2], cx.b_rl[T % 2]
                o1, b_o1 = cx.o1[T % 2], cx.b_o1[T % 2]
                go, b_go = cx.go_sb[T % 2], cx.b_go_sb[T % 2]
                g, b_g = cx.g_sb[T % 2], cx.b_g_sb[T % 2]
                s.add("act", lambda e: e.activation(out=rl[:], in_=lp_[:, :], func=AF.Ln), reads=[b_lp], writes=[b_rl])
                s.add("act", lambda e: e.activation(out=rl[:], in_=rl[:], func=AF.Exp, scale=-1.0), reads=[b_rl], writes=[b_rl])
                s.add("dve", lambda e: e.tensor_tensor(out=o1[:], in0=op_[:, :], in1=rl[:], op=ALU.mult),
                      reads=[b_op, b_rl], writes=[b_o1])
                s.add("pool", lambda e: e.tensor_tensor(out=go[:], in0=o1[:], in1=g[:], op=ALU.mult),
                      reads=[b_o1, b_g], writes=[b_go])
                jj, gc0 = (T * 512) // cx.CW, (T * 512) % cx.CW
                s.add("pool", lambda e: e.dma_start(out=cx.gohalf_d[jj][h * 128:(h + 1) * 128, gc0:gc0 + 512], in_=go[:]),
                      reads=[b_go], writes=[cx.b_gohalf_d[jj]], dma=True)

        n = len(steps)
        for i in range(n + LOOK):
            if i < n:
                front(i)
            if i >= LOOK:
                back(i - LOOK)


def common_setup(cx):
    nc, S = cx.nc, cx.S
    cx.wst = [sb(cx, f"wst{i}", [128, 516], F32) for i in range(2)]
    cx.b_wst = [Buf(f"wst{i}") for i in range(2)]
    cx.win = sb(cx, "win", [128, 8, 2064], BF16)
    cx.b_win = Buf("win")
    cx.wout = sb(cx, "wout", [128, 8, 1024], BF16)
    cx.b_wout = Buf("wout")
    cx.normw = sb(cx, "normw", [128, 32], F32)
    cx.b_normw = Buf("normw")
    cx.normw_d = nc.dram_tensor("normw_in", [128, 32], F32, kind="ExternalInput").ap()
    cx.s.add("sp", lambda e: e.dma_start(out=cx.normw[:], in_=cx.normw_d), writes=[cx.b_normw], dma=True)
    cx.xT = [sb(cx, f"xT{i}", [128, 8, 512], BF16) for i in range(2)]
    cx.b_xT = [Buf(f"xT{i}") for i in range(2)]
    cx.goT = [sb(cx, f"goT{i}", [128, 8, 512], BF16) for i in range(1)]
    cx.b_goT = [Buf(f"goT{i}") for i in range(1)]
    cx.xt = [sb(cx, f"xt{i}", [128, 1024], F32) for i in range(2)]
    cx.b_xt = [Buf(f"xt{i}") for i in range(2)]
    cx.junk = [sb(cx, f"junk{i}", [128, 1024], BF16) for i in range(2)]
    cx.b_junk = [Buf(f"junk{i}") for i in range(2)]
    cx.stat = [sb(cx, f"stat{i}", [128, 4], F32) for i in range(4)]
    cx.b_stat = [Buf(f"stat{i}") for i in range(4)]
    cx.xs = [sb(cx, f"xs{i}", [128, 1024], BF16) for i in range(4)]
    cx.b_xs = [Buf(f"xs{i}") for i in range(4)]
    cx.x_in = nc.dram_tensor("x", [S, 1024], F32, kind="ExternalInput").ap()
    cx.xres_d = nc.dram_tensor("xres_d", [S, 1024], F32, kind=getattr(cx, "xres_kind", "Internal")).ap()
    cx.b_xres_d = [Buf(f"xres{t}") for t in range(S // 128)]
    cx.CW = min(1024, S)
    cx.NCH = S // cx.CW
    cx.gofull_d = [nc.dram_tensor(f"gofull_d{j}", [1024, cx.CW], BF16, kind="Internal").ap() for j in range(cx.NCH)]
    cx.b_gofull_d = [Buf(f"gofull_d{j}") for j in range(cx.NCH)]


def mixer_common_setup(cx):
    cx.gomt = sb(cx, "gomt", [128, 4, 512], BF16)
    cx.b_gomt = Buf("gomt")
    cx.st_f = [sb(cx, f"st_f{i}", [128, 256], F32) for i in range(4)]
    cx.b_st_f = [Buf(f"st_f{i}") for i in range(4)]
    cx.st_b = [sb(cx, f"st_b{i}", [128, 256], BF16) for i in range(4)]
    cx.b_st_b = [Buf(f"st_b{i}") for i in range(4)]
    cx.mparam = sb(cx, "mparam", [64, 1024], F32)
    cx.b_mparam = Buf("mparam")

    def two(name, shape, dt):
        return [sb(cx, f"{name}{i}", shape, dt) for i in range(2)], [Buf(f"{name}{i}") for i in range(2)]
    cx.w_gz, cx.b_w_gz = two("w_gz", [64, 512], F32)
    cx.w_og, cx.b_w_og = two("w_og", [64, 512], BF16)


def gla_setup(cx):
    cx.cmraw = [sb(cx, f"cmraw{i}", [128, 512], F32) for i in range(4)]
    cx.b_cmraw = [Buf(f"cmraw{i}") for i in range(4)]
    cx.glT = sb(cx, "glT", [32, 512], F32)
    cx.b_glT = Buf("glT")
    cx.wup = sb(cx, "wup", [32, 256], F32)
    cx.b_wup = Buf("wup")

    def two(name, shape, dt):
        return [sb(cx, f"{name}{i}", shape, dt) for i in range(2)], [Buf(f"{name}{i}") for i in range(2)]
    cx.w_a, cx.b_w_a = two("w_a", [64, 512], F32)
    cx.w_b, cx.b_w_b = two("w_b", [64, 512], F32)
    cx.w_c, cx.b_w_c = two("w_c", [64, 512], F32)
    cx.w_v, cx.b_w_v = two("w_v", [64, 512], BF16)
    cx.w_kd, cx.b_w_kd = two("w_kd", [64, 512], BF16)
    cx.w_e1, cx.b_w_e1 = two("w_e1", [128, 256], F32)
    cx.w_e2, cx.b_w_e2 = two("w_e2", [128, 256], F32)
    cx.w_qd, cx.b_w_qd = two("w_qd", [128, 256], BF16)
    cx.w_ki, cx.b_w_ki = two("w_ki", [128, 256], BF16)
    cx.w_at, cx.b_w_at = two("w_at", [64, 256], BF16)
    cx.w_st, cx.b_w_st = two("w_st", [64, 16], F32)
    cx.w_nl, cx.b_w_nl = two("w_nl", [64, 256], BF16)
    cx.kb16 = [sb(cx, f"kb16_{i}", [128, 512], BF16) for i in range(2)]
    cx.b_kb16 = [Buf(f"kb16_{i}") for i in range(2)]
    cx.vcT = [sb(cx, f"vcT{i}", [128, 512], BF16) for i in range(4)]
    cx.b_vcT = [Buf(f"vcT{i}") for i in range(4)]
    cx.zcT = [sb(cx, f"zcT{i}", [128, 512], BF16) for i in range(4)]
    cx.b_zcT = [Buf(f"zcT{i}") for i in range(4)]
    cx.glTb = sb(cx, "glTb", [32, 512], BF16)
    cx.b_glTb = Buf("glTb")
    cx.wupb = sb(cx, "wupb", [32, 256], BF16)
    cx.b_wupb = Buf("wupb")


def gla_load_params(cx, wup_d, gain_d):
    s = cx.s
    s.add("dve", lambda e: e.memset(cx.glT[:], 1.0), writes=[cx.b_glT])
    s.add("sp", lambda e: e.dma_start(out=cx.wup[0:17, :], in_=wup_d), writes=[cx.b_wup], dma=True)
    s.add("dve", lambda e: e.tensor_copy(out=cx.wupb[0:17, :], in_=cx.wup[0:17, :]), reads=[cx.b_wup], writes=[cx.b_wupb])
    s.add("dve", lambda e: e.memset(cx.glTb[:], 1.0), writes=[cx.b_glTb])
    s.add("sp", lambda e: e.dma_start(out=cx.mparam[:, 0:512], in_=gain_d), writes=[cx.b_mparam], dma=True)
    for h in range(2):
        s.add("dve", lambda e, h=h: e.memset(cx.st_f[h][:], 0.0), writes=[cx.b_st_f[h]])
        s.add("dve", lambda e, h=h: e.memset(cx.st_b[h][:], 0.0), writes=[cx.b_st_b[h]])


def gla_proj(cx):
    s, win = cx.s, cx.win
    U = cx.cst[0:64, 128:192]
    Ub = cx.cstb[0:64, 128:192]
    ONES = cx.cst[0:64, 512:576]
    IDB = cx.cstb[0:64, 0:64]
    ps, bp = cx.ps, cx.b_ps

    def emit(mt, xT, b_xT):
        for g in range(4):
            pb = 2
            for kc in range(8):
                s.add("pe", lambda e, kc=kc, g=g: e.matmul(
                    ps[pb][:, :], lhsT=win[:, kc, g * 128:(g + 1) * 128], rhs=xT[:, kc, :], start=(kc == 0), stop=(kc == 7)),
                    reads=[cx.b_win, b_xT], writes=[bp[pb]])
            s.add("act", lambda e, g=g: e.copy(out=cx.cmraw[g][:], in_=ps[pb][:, :]), reads=[bp[pb]], writes=[cx.b_cmraw[g]])
            if g >= 2:
                s.add("act", lambda e, g=g: e.copy(out=cx.kb16[g - 2][:], in_=ps[pb][:, :]), reads=[bp[pb]], writes=[cx.b_kb16[g - 2]])
        for kc in range(8):
            s.add("pe", lambda e, kc=kc: e.matmul(
                ps[2][0:16, :], lhsT=win[:, kc, 512:528], rhs=xT[:, kc, :], start=(kc == 0), stop=(kc == 7)),
                reads=[cx.b_win, b_xT], writes=[bp[2]])
        s.add("act", lambda e: e.copy(out=cx.glTb[0:16, :], in_=ps[2][0:16, :]), reads=[bp[2]], writes=[cx.b_glTb])
        for g in range(8):
            pb = 2 + (g % 2)
            c0 = (528 if g < 4 else 1040) + (g % 4) * 128
            for kc in range(8):
                s.add("pe", lambda e, kc=kc, c0=c0, pb=pb: e.matmul(
                    ps[pb][:, :], lhsT=win[:, kc, c0:c0 + 128], rhs=xT[:, kc, :], start=(kc == 0), stop=(kc == 7)),
                    reads=[cx.b_win, b_xT], writes=[bp[pb]])
            if g < 4:
                s.add("act", lambda e, g=g, pb=pb: e.copy(out=cx.vcT[g][:], in_=ps[pb][:, :]), reads=[bp[pb]], writes=[cx.b_vcT[g]])
            else:
                s.add("act", lambda e, g=g, pb=pb: e.activation(out=cx.zcT[g - 4][:], in_=ps[pb][:, :], func=AF.Silu),
                      reads=[bp[pb]], writes=[cx.b_zcT[g - 4]])
        chunk(mt, 0, xT, b_xT, 'Y')
        for ci in range(8):
            if ci + 1 < 8:
                s.begin_record()
                chunk(mt, ci, xT, b_xT, 'X')
                lx = s.end_record()
                s.begin_record()
                chunk(mt, ci + 1, xT, b_xT, 'Y')
                ly = s.end_record()
                s.replay_zip(lx, ly)
            else:
                chunk(mt, ci, xT, b_xT, 'X')
        jj, c0 = (mt * 512) // cx.CW, (mt * 512) % cx.CW
        s.add("pool", lambda e: e.dma_start(out=cx.gohalf_d[jj][:, c0:c0 + 512].rearrange("(j p) t -> p j t", p=128),
                                            in_=cx.gomt[:]), reads=[cx.b_gomt], writes=[cx.b_gohalf_d[jj]], dma=True)

    def chunk(mt, ci, xT, b_xT, part):
        if True:
            c = mt * 8 + ci
            par = c % 2
            cs = slice(ci * 64, (ci + 1) * 64)
            wa, b_wa = cx.w_a[par], cx.b_w_a[par]
            wb, b_wb = cx.w_b[par], cx.b_w_b[par]
            wc, b_wc = cx.w_c[par], cx.b_w_c[par]
            wv, b_wv = cx.w_v[par], cx.b_w_v[par]
            gz, b_gz = cx.w_gz[par], cx.b_w_gz[par]
            kd, b_kd = cx.w_kd[par], cx.b_w_kd[par]
            e1, b_e1 = cx.w_e1[par], cx.b_w_e1[par]
            e2, b_e2 = cx.w_e2[par], cx.b_w_e2[par]
            qd, b_qd = cx.w_qd[par], cx.b_w_qd[par]
            ki, b_ki = cx.w_ki[par], cx.b_w_ki[par]
            at, b_at = cx.w_at[par], cx.b_w_at[par]
            og, b_og = cx.w_og[par], cx.b_w_og[par]
            st, b_st = cx.w_st[par], cx.b_w_st[par]
            if part == 'Y':
                tpv = ps[4].bitcast(BF16)
                tpz = ps[5].bitcast(BF16)
                for g in range(4):
                    s.add("pe", lambda e, g=g: e.transpose(tpv[0:64, g * 128:(g + 1) * 128], cx.vcT[g][:, cs], cx.ident_b),
                          reads=[cx.b_vcT[g], cx.b_cstb], writes=[bp[4]])
                for h in range(2):
                    s.add("pe", lambda e, h=h: e.transpose(tpv[0:64, 512 + h * 128:512 + (h + 1) * 128], cx.kb16[h][:, cs], cx.ident_b),
                          reads=[cx.b_kb16[h], cx.b_cstb], writes=[bp[4]])
                s.add("act", lambda e: e.copy(out=wv[:], in_=tpv[0:64, 0:512]), reads=[bp[4]], writes=[b_wv])
                for g in range(4):
                    s.add("pe", lambda e, g=g: e.transpose(tpz[0:64, g * 128:(g + 1) * 128], cx.zcT[g][:, cs], cx.ident_b),
                          reads=[cx.b_zcT[g], cx.b_cstb], writes=[bp[5]])
                s.add("dve", lambda e: e.tensor_tensor(out=gz[:], in0=tpz[0:64, 0:512], in1=cx.mparam[:, 0:512], op=ALU.mult),
                      reads=[bp[5], cx.b_mparam], writes=[b_gz])
                nl, b_nl = cx.w_nl[par], cx.b_w_nl[par]
                s.add("pe", lambda e: e.matmul(ps[6][0:64, 0:256], lhsT=cx.glTb[0:17, cs], rhs=cx.wupb[0:17, :], start=True, stop=True),
                      reads=[cx.b_glTb, cx.b_wupb], writes=[bp[6]])
                s.add("act", lambda e: e.activation(out=wa[:, 0:256], in_=ps[6][0:64, 0:256], func=AF.Exp, scale=-1.0),
                      reads=[bp[6]], writes=[b_wa])
                s.add("act", lambda e: e.activation(out=nl[:], in_=wa[:, 0:256], func=AF.Ln, bias=cx.epsq[0:64, 1:2]),
                      reads=[b_wa, cx.b_epsq], writes=[b_nl])
                s.add("pe", lambda e: e.matmul(ps[6][0:64, 0:256], lhsT=Ub, rhs=nl[:], start=True, stop=True),
                      reads=[b_nl, cx.b_cstb], writes=[bp[6]])
                s.add("pe", lambda e: e.matmul(ps[6][0:64, 256:512], lhsT=cx.cstb[0:64, 512:576], rhs=nl[:], start=True, stop=True),
                      reads=[b_nl, cx.b_cstb], writes=[bp[6]])
                for h in range(2):
                    s.add("pe", lambda e, h=h: e.matmul(ps[7][:, h * 64:(h + 1) * 64], lhsT=nl[:, h * 128:(h + 1) * 128], rhs=Ub,
                                                        start=True, stop=True), reads=[b_nl, cx.b_cstb], writes=[bp[7]])
                s.add("act", lambda e: e.copy(out=wb[:, 0:256], in_=ps[6][0:64, 0:256]), reads=[bp[6]], writes=[b_wb])
                s.add("dve", lambda e: e.tensor_tensor(out=wb[:, 256:512], in0=ps[6][0:64, 256:512], in1=wb[:, 0:256], op=ALU.subtract),
                      reads=[bp[6], b_wb], writes=[b_wb])
                s.add("act", lambda e: e.activation(out=wb[:, 256:512], in_=wb[:, 256:512], func=AF.Exp, scale=-1.0 / 16),
                      reads=[b_wb], writes=[b_wb])
                s.add("dve", lambda e: e.tensor_tensor(out=kd[:, 0:256], in0=ps[4].bitcast(BF16)[0:64, 512:768], in1=wb[:, 256:512], op=ALU.mult),
                      reads=[bp[4], b_wb], writes=[b_kd])
                s.add("act", lambda e: e.activation(out=e1[:, 0:128], in_=ps[7][:, 0:128], func=AF.Exp, scale=-1.0 / 16),
                      reads=[bp[7]], writes=[b_e1])
                s.add("act", lambda e: e.activation(out=e2[:, 0:128], in_=ps[7][:, 0:128], func=AF.Exp, scale=1.0 / 16),
                      reads=[bp[7]], writes=[b_e2])
                for h in range(2):
                    s.add("dve", lambda e, h=h: e.scalar_tensor_tensor(
                        out=qd[:, h * 64:(h + 1) * 64], in0=cx.cmraw[h][:, cs], scalar=float(128 ** -0.5),
                        in1=e1[:, h * 64:(h + 1) * 64], op0=ALU.mult, op1=ALU.mult),
                        reads=[cx.b_cmraw[h], b_e1], writes=[b_qd])
                    s.add("dve", lambda e, h=h: e.tensor_tensor(
                        out=ki[:, h * 64:(h + 1) * 64], in0=cx.cmraw[2 + h][:, cs], in1=e2[:, h * 64:(h + 1) * 64], op=ALU.mult),
                        reads=[cx.b_cmraw[2 + h], b_e2], writes=[b_ki])
                for h in range(2):
                    s.add("pe", lambda e, h=h: e.matmul(ps[7][0:64, 128 + h * 64:128 + (h + 1) * 64], lhsT=ki[:, h * 64:(h + 1) * 64],
                                                        rhs=qd[:, h * 64:(h + 1) * 64], start=True, stop=True),
                          reads=[b_ki, b_qd], writes=[bp[7]])
                for h in range(2):
                    s.add("dve", lambda e, h=h: e.tensor_tensor(out=at[:, h * 64:(h + 1) * 64], in0=ps[7][0:64, 128 + h * 64:128 + (h + 1) * 64],
                                                                in1=U, op=ALU.mult), reads=[bp[7], cx.b_cst], writes=[b_at])
                return
            for h in range(2):
                s.add("pe", lambda e, h=h: e.matmul(ps[3][0:64, h * 256:(h + 1) * 256], lhsT=at[:, h * 64:(h + 1) * 64],
                                                    rhs=wv[:, h * 256:(h + 1) * 256], start=True, stop=False),
                      reads=[b_at, b_wv], writes=[bp[3]])
                s.add("pe", lambda e, h=h: e.matmul(ps[3][0:64, h * 256:(h + 1) * 256], lhsT=qd[:, h * 64:(h + 1) * 64],
                                                    rhs=cx.st_b[h][:], start=False, stop=True),
                      reads=[b_qd, cx.b_st_b[h]], writes=[bp[3]])
            for h in range(2):
                s.add("pe", lambda e, h=h: e.matmul(ps[1][:, h * 256:(h + 1) * 256], lhsT=kd[:, h * 128:(h + 1) * 128],
                                                    rhs=wv[:, h * 256:(h + 1) * 256], start=True, stop=True),
                      reads=[b_kd, b_wv], writes=[bp[1]])
            for h in range(2):
                s.add("dve", lambda e, h=h: e.scalar_tensor_tensor(
                    out=cx.st_f[h][:], in0=cx.st_f[h][:], scalar=e1[:, h * 64 + 63:h * 64 + 64], in1=ps[1][:, h * 256:(h + 1) * 256],
                    op0=ALU.mult, op1=ALU.add), reads=[cx.b_st_f[h], b_e1, bp[1]], writes=[cx.b_st_f[h]])
                s.add("act", lambda e, h=h: e.copy(out=cx.st_b[h][:], in_=cx.st_f[h][:]), reads=[cx.b_st_f[h]], writes=[cx.b_st_b[h]])
            for h in range(2):
                s.add("act", lambda e, h=h: e.activation(out=wc[:, h * 256:(h + 1) * 256], in_=ps[3][0:64, h * 256:(h + 1) * 256],
                                                         func=AF.Square, accum_out=st[:, h:h + 1]), reads=[bp[3]], writes=[b_wc, b_st])
            s.add("act", lambda e: e.activation(out=st[:, 2:4], in_=st[:, 0:2], func=AF.Ln, scale=1.0 / 256, bias=cx.epsq[0:64, 3:4]),
                  reads=[b_st, cx.b_epsq], writes=[b_st])
            s.add("act", lambda e: e.activation(out=st[:, 4:6], in_=st[:, 2:4], func=AF.Exp, scale=-0.5), reads=[b_st], writes=[b_st])
            for h in range(2):
                s.add("dve", lambda e, h=h: e.scalar_tensor_tensor(
                    out=og[:, h * 256:(h + 1) * 256], in0=ps[3][0:64, h * 256:(h + 1) * 256], scalar=st[:, 4 + h:5 + h],
                    in1=gz[:, h * 256:(h + 1) * 256], op0=ALU.mult, op1=ALU.mult), reads=[bp[3], b_st, b_gz], writes=[b_og])
            tp = cx.ps[0].bitcast(BF16)
            for j in range(4):
                s.add("pe", lambda e, j=j: e.transpose(tp[:, j * 64:(j + 1) * 64], og[:, j * 128:(j + 1) * 128], IDB),
                      reads=[b_og, cx.b_cstb], writes=[bp[0]])
            s.add("act", lambda e: e.copy(out=cx.gomt[:, :, cs], in_=tp[:, 0:256].rearrange("p (j t) -> p j t", j=4)),
                  reads=[bp[0]], writes=[cx.b_gomt])
    return emit


def gdn_setup(cx):
    def mk(name, shape, dt, n):
        return [sb(cx, f"{name}{i}", shape, dt) for i in range(n)], [Buf(f"{name}{i}") for i in range(n)]
    cx.ub, cx.b_ub = mk("ub", [128, 516], F32, 8)
    cx.qkn, cx.b_qkn = mk("qkn", [128, 512], BF16, 4)
    cx.vc, cx.b_vc = mk("vc", [128, 512], BF16, 4)
    cx.cacc, cx.b_cacc = mk("cacc", [128, 512], F32, 2)
    cx.grs, cx.b_grs = mk("grs", [128, 512], F32, 2)
    cx.zc, cx.b_zc = mk("zc", [128, 512], BF16, 4)
    cx.gp = sb(cx, "gp", [128, 32], F32)
    cx.b_gp = Buf("gp")
    cx.nP, cx.b_nP = mk("nP", [64, 512], BF16, 2)
    cx.nQ, cx.b_nQ = mk("nQ", [64, 512], BF16, 2)
    cx.rsh, cx.b_rsh = mk("rsh", [64, 512], BF16, 1)
    cx.nR, cx.b_nR = mk("nR", [64, 512], F32, 1)
    cx.rbf, cx.b_rbf = mk("rbf", [64, 512], BF16, 2)
    cx.gb, cx.b_gb = mk("gb", [64, 256], F32, 1)
    cx.dA, cx.b_dA = mk("dA", [64, 256], F32, 1)
    cx.dT, cx.b_dT = mk("dT", [64, 256], F32, 1)
    cx.aqt, cx.b_aqt = mk("aqt", [64, 512], BF16, 2)
    cx.sm, cx.b_sm = mk("sm", [64, 64], F32, 4)
    cx.sm2, cx.b_sm2 = mk("sm2", [128, 8], F32, 4)
    cx.vb, cx.b_vb = mk("vb", [64, 512], BF16, 1)
    cx.kbe, cx.b_kbe = mk("kbe", [64, 512], BF16, 1)
    cx.kdg, cx.b_kdg = mk("kdg", [64, 512], BF16, 1)
    cx.wtn, cx.b_wtn = mk("wtn", [128, 256], BF16, 1)
    cx.vnew, cx.b_vnew = mk("vnew", [64, 512], BF16, 1)
    cx.oi, cx.b_oi = mk("oi", [64, 512], F32, 1)
    cx.oo, cx.b_oo = mk("oo", [64, 512], F32, 1)
    cx.msk = sb(cx, "msk", [64, 1024], F32)
    cx.b_msk = Buf("msk")


def gdn_load_params(cx, gp_d, mp_d):
    s = cx.s
    s.add("sp", lambda e: e.dma_start(out=cx.gp[:], in_=gp_d), writes=[cx.b_gp], dma=True)
    s.add("sp", lambda e: e.dma_start(out=cx.mparam[:], in_=mp_d), writes=[cx.b_mparam], dma=True)
    s.add("act", lambda e: e.activation(out=cx.mparam[:, 516:520], in_=cx.mparam[:, 516:520], func=AF.Exp),
          reads=[cx.b_mparam], writes=[cx.b_mparam])
    for h in range(4):
        s.add("dve", lambda e, h=h: e.memset(cx.st_f[h][:], 0.0), writes=[cx.b_st_f[h]])
        s.add("dve", lambda e, h=h: e.memset(cx.st_b[h][:], 0.0), writes=[cx.b_st_b[h]])
    for g in range(8):
        s.add("dve", lambda e, g=g: e.memset(cx.ub[g][:, 0:4], 0.0), writes=[cx.b_ub[g]])
    for h in range(4, 8):
        s.add("dve", lambda e, h=h: e.tensor_copy(out=cx.msk[:, 512 + h * 64:512 + (h + 1) * 64], in_=cx.cst[0:64, 0:64]),
              reads=[cx.b_cst], writes=[cx.b_msk])
    for h in range(4):
        s.add("dve", lambda e, h=h: e.tensor_copy(out=cx.msk[:, h * 64:(h + 1) * 64], in_=cx.cst[0:64, 768:832]),
              reads=[cx.b_cst], writes=[cx.b_msk])
        s.add("dve", lambda e, h=h: e.tensor_copy(out=cx.msk[:, 256 + h * 64:256 + (h + 1) * 64], in_=cx.cst[0:64, 128:192]),
              reads=[cx.b_cst], writes=[cx.b_msk])
        s.add("dve", lambda e, h=h: e.tensor_copy(out=cx.msk[:, 512 + h * 64:512 + (h + 1) * 64], in_=cx.cst[0:64, 0:64]),
              reads=[cx.b_cst], writes=[cx.b_msk])


GDN_STOP = 99


def gdn_proj(cx):
    s, win = cx.s, cx.win
    U = cx.cst[0:64, 128:192]
    ONES = cx.cst[0:64, 512:640]
    IDF = cx.cst[0:64, 0:64]
    IDB = cx.cstb[0:64, 0:64]
    ps, bp = cx.ps, cx.b_ps
    LS4, UI4, ID8 = cx.msk[:, 0:256], cx.msk[:, 256:512], cx.msk[:, 512:1024]

    def prep_group(g, xT, b_xT, mt):
        ub, b_ub = cx.ub[g], cx.b_ub[g]
        acc, b_acc = cx.cacc[g % 2], cx.b_cacc[g % 2]
        pb = 2
        if mt > 0:
            s.add("dve", lambda e: e.tensor_copy(out=ub[:, 1:4], in_=ub[:, 513:516]), reads=[b_ub], writes=[b_ub])
        for kc in range(8):
            s.add("pe", lambda e, kc=kc: e.matmul(ps[pb][:, :], lhsT=win[:, kc, g * 128:(g + 1) * 128], rhs=xT[:, kc, :],
                                                  start=(kc == 0), stop=(kc == 7)), reads=[cx.b_win, b_xT], writes=[bp[pb]])
        s.add("act", lambda e: e.copy(out=ub[:, 4:516], in_=ps[pb][:, :]), reads=[bp[pb]], writes=[b_ub])
        s.add("dve", lambda e: e.tensor_scalar(out=acc[:], in0=ub[:, 4:516], scalar1=cx.gp[:, g * 4 + 3:g * 4 + 4], scalar2=None,
                                               op0=ALU.mult), reads=[b_ub, cx.b_gp], writes=[b_acc])
        for j in range(3):
            s.add("dve", lambda e, j=j: e.scalar_tensor_tensor(
                out=acc[:], in0=ub[:, 1 + j:513 + j], scalar=cx.gp[:, g * 4 + j:g * 4 + j + 1], in1=acc[:],
                op0=ALU.mult, op1=ALU.add), reads=[b_ub, cx.b_gp, b_acc], writes=[b_acc])
        if g >= 4:
            vc, b_vc = cx.vc[g - 4], cx.b_vc[g - 4]
            s.add("act", lambda e: e.activation(out=vc[:], in_=acc[:], func=AF.Silu), reads=[b_acc], writes=[b_vc])
            return
        s.add("act", lambda e: e.activation(out=acc[:], in_=acc[:], func=AF.Silu), reads=[b_acc], writes=[b_acc])
        sq, b_sq = cx.junk[0][:, 0:512], cx.b_junk[0]
        rs, b_rs = cx.grs[g % 2], cx.b_grs[g % 2]
        s.add("act", lambda e: e.activation(out=sq, in_=acc[:], func=AF.Square), reads=[b_acc], writes=[b_sq])
        s.add("pe", lambda e: e.matmul(ps[3][:, :], lhsT=cx.ones_b, rhs=sq, start=True, stop=True),
              reads=[b_sq, cx.b_cstb], writes=[bp[3]])
        s.add("act", lambda e: e.activation(out=rs[:], in_=ps[3][:, :], func=AF.Ln, bias=cx.epsq[:, 3:4]),
              reads=[bp[3], cx.b_epsq], writes=[b_rs])
        s.add("act", lambda e: e.activation(out=rs[:], in_=rs[:], func=AF.Exp, scale=-0.5), reads=[b_rs], writes=[b_rs])
        qk, b_qk = cx.qkn[g], cx.b_qkn[g]
        sc = float(128 ** -0.5) if g < 2 else 1.0
        s.add("dve", lambda e: e.scalar_tensor_tensor(out=qk[:], in0=acc[:], scalar=sc, in1=rs[:], op0=ALU.mult, op1=ALU.mult),
              reads=[b_acc, b_rs], writes=[b_qk])

    def emit(mt, xT, b_xT):
        for g in range(8):
            prep_group(g, xT, b_xT, mt)
        for g in range(4):
            for kc in range(8):
                s.add("pe", lambda e, kc=kc, g=g: e.matmul(ps[2][:, :], lhsT=win[:, kc, 1024 + g * 128:1024 + (g + 1) * 128], rhs=xT[:, kc, :],
                                                           start=(kc == 0), stop=(kc == 7)), reads=[cx.b_win, b_xT], writes=[bp[2]])
            s.add("act", lambda e, g=g: e.activation(out=cx.zc[g][:], in_=ps[2][:, :], func=AF.Silu), reads=[bp[2]], writes=[cx.b_zc[g]])
        def Y(pr):
            pre(mt, 2 * pr, xT, b_xT)
            pre(mt, 2 * pr + 1, xT, b_xT)
            neu(mt, 2 * pr, xT, b_xT)

        def X(pr):
            post(mt, 2 * pr, xT, b_xT)
            post(mt, 2 * pr + 1, xT, b_xT)

        Y(0)
        for pr in range(4):
            if pr + 1 < 4:
                s.begin_record()
                Y(pr + 1)
                ly = s.end_record()
                s.begin_record()
                X(pr)
                lx = s.end_record()
                s.replay_zip(lx, ly)
            else:
                X(pr)
        jj, c0 = (mt * 512) // cx.CW, (mt * 512) % cx.CW
        s.add("pool", lambda e: e.dma_start(out=cx.gohalf_d[jj][:, c0:c0 + 512].rearrange("(j p) t -> p j t", p=128),
                                            in_=cx.gomt[:]), reads=[cx.b_gomt], writes=[cx.b_gohalf_d[jj]], dma=True)

    def pre(mt, ci, xT, b_xT):
        c = mt * 8 + ci
        par = c % 2
        pp = (c // 2) % 2
        cs = slice(ci * 64, (ci + 1) * 64)
        sm, b_sm = cx.sm[c % 4], cx.b_sm[c % 4]
        sm2, b_sm2 = cx.sm2[c % 4], cx.b_sm2[c % 4]
        gz, b_gz = cx.w_gz[par], cx.b_w_gz[par]
        og, b_og = cx.w_og[par], cx.b_w_og[par]
        nR, b_nR = cx.nR[0], cx.b_nR[0]
        rbf, b_rbf = cx.rbf[pp], cx.b_rbf[pp]
        gb, b_gb = cx.gb[0], cx.b_gb[0]
        dA, b_dA = cx.dA[0], cx.b_dA[0]
        dT, b_dT = cx.dT[0], cx.b_dT[0]
        aqt, b_aqt = cx.aqt[pp], cx.b_aqt[pp]
        vb, b_vb = cx.vb[0], cx.b_vb[0]
        kbe, b_kbe = cx.kbe[0], cx.b_kbe[0]
        kdg, b_kdg = cx.kdg[0], cx.b_kdg[0]
        wtn, b_wtn = cx.wtn[0], cx.b_wtn[0]
        vnew, b_vnew = cx.vnew[0], cx.b_vnew[0]
        oi, b_oi = cx.oi[0], cx.b_oi[0]
        oo, b_oo = cx.oo[0], cx.b_oo[0]

        def hs(h):
            return slice(h * 64, (h + 1) * 64)

        def hv(h):
            return slice(h * 128, (h + 1) * 128)

        def hb(h):
            return slice((ci % 2) * 256 + h * 64, (ci % 2) * 256 + (h + 1) * 64)

        if GDN_STOP < 1:
            return
        if GDN_STOP < 2:
            return
        for kc in range(8):
            s.add("pe", lambda e, kc=kc: e.matmul(ps[2][0:64, 0:8], lhsT=xT[:, kc, cs], rhs=win[:, kc, 1536:1544],
                                                  start=(kc == 0), stop=(kc == 7)), reads=[cx.b_win, b_xT], writes=[bp[2]])
        s.add("dve", lambda e: e.tensor_tensor(out=sm[:, 0:4], in0=ps[2][0:64, 0:4], in1=cx.mparam[:, 512:516], op=ALU.add),
              reads=[bp[2], cx.b_mparam], writes=[b_sm])
        s.add("act", lambda e: e.activation(out=sm[:, 8:12], in_=ps[2][0:64, 4:8], func=AF.Exp, scale=-1.0), reads=[bp[2]], writes=[b_sm])
        s.add("act", lambda e: e.activation(out=sm[:, 0:4], in_=sm[:, 0:4], func=AF.Exp), reads=[b_sm], writes=[b_sm])
        s.add("act", lambda e: e.activation(out=sm[:, 0:4], in_=sm[:, 0:4], func=AF.Ln, bias=cx.epsq[0:64, 1:2]),
              reads=[b_sm, cx.b_epsq], writes=[b_sm])
        s.add("dve", lambda e: e.tensor_tensor(out=sm[:, 4:8], in0=sm[:, 0:4], in1=cx.mparam[:, 516:520], op=ALU.mult),
              reads=[b_sm, cx.b_mparam], writes=[b_sm])
        s.add("dve", lambda e: e.tensor_scalar(out=sm[:, 8:12], in0=sm[:, 8:12], scalar1=1.0, scalar2=None, op0=ALU.add),
              reads=[b_sm], writes=[b_sm])
        s.add("dve", lambda e: e.reciprocal(out=sm[:, 8:12], in_=sm[:, 8:12]), reads=[b_sm], writes=[b_sm])
        if GDN_STOP < 3:
            return
        s.add("pe", lambda e: e.matmul(ps[6][0:64, 0:4], lhsT=U, rhs=sm[:, 4:8], start=True, stop=True),
              reads=[b_sm, cx.b_cst], writes=[bp[6]])
        s.add("pe", lambda e: e.matmul(ps[6][:, 8:12], lhsT=ONES, rhs=sm[:, 4:8], start=True, stop=True),
              reads=[b_sm, cx.b_cst], writes=[bp[6]])
        s.add("dve", lambda e: e.tensor_copy(out=sm[:, 12:16], in_=ps[6][0:64, 0:4]), reads=[bp[6]], writes=[b_sm])
        s.add("act", lambda e: e.activation(out=sm[:, 16:20], in_=ps[6][0:64, 0:4], func=AF.Exp, scale=-1.0), reads=[bp[6]], writes=[b_sm])
        s.add("dve", lambda e: e.tensor_tensor(out=sm[:, 20:24], in0=ps[6][0:64, 8:12], in1=sm[:, 12:16], op=ALU.subtract),
              reads=[bp[6], b_sm], writes=[b_sm])
        s.add("act", lambda e: e.activation(out=sm[:, 20:24], in_=sm[:, 20:24], func=AF.Exp, scale=-1.0), reads=[b_sm], writes=[b_sm])
        s.add("act", lambda e: e.activation(out=sm2[:, 4:8], in_=ps[6][:, 8:12], func=AF.Exp, scale=-1.0), reads=[bp[6]], writes=[b_sm2])
        s.add("dve", lambda e: e.tensor_tensor(out=sm[:, 24:28], in0=sm[:, 8:12], in1=sm[:, 16:20], op=ALU.mult),
              reads=[b_sm], writes=[b_sm])
        if GDN_STOP < 4:
            return
        for h in range(4):
            s.add("dve", lambda e, h=h: e.tensor_copy(out=gb[:, hs(h)], in_=sm[:, 4 + h:5 + h].to_broadcast([64, 64])),
                  reads=[b_sm], writes=[b_gb])
        for h in range(4):
            s.add("pe", lambda e, h=h: e.matmul(ps[7][0:64, hs(h)], lhsT=gb[:, hs(h)], rhs=U, start=True, stop=True),
                  reads=[b_gb, cx.b_cst], writes=[bp[7]])
        for h in range(4):
            s.add("dve", lambda e, h=h: e.tensor_scalar(out=dA[:, hs(h)], in0=ps[7][0:64, hs(h)], scalar1=sm[:, 12 + h:13 + h], scalar2=0.0,
                                                        op0=ALU.subtract, op1=ALU.min), reads=[bp[7], b_sm], writes=[b_dA])
            s.add("dve", lambda e, h=h: e.tensor_scalar(out=dT[:, hs(h)], in0=ps[7][0:64, hs(h)], scalar1=sm[:, 12 + h:13 + h], scalar2=0.0,
                                                        op0=ALU.subtract, op1=ALU.max), reads=[bp[7], b_sm], writes=[b_dT])
        s.add("act", lambda e: e.activation(out=dA[:], in_=dA[:], func=AF.Exp), reads=[b_dA], writes=[b_dA])
        s.add("act", lambda e: e.activation(out=dT[:], in_=dT[:], func=AF.Exp, scale=-1.0), reads=[b_dT], writes=[b_dT])
        s.add("pool", lambda e: e.tensor_tensor(out=dA[:], in0=dA[:], in1=LS4, op=ALU.mult), reads=[b_dA, cx.b_msk], writes=[b_dA])
        s.add("pool", lambda e: e.tensor_tensor(out=dT[:], in0=dT[:], in1=UI4, op=ALU.mult), reads=[b_dT, cx.b_msk], writes=[b_dT])
        if GDN_STOP < 5:
            return
        for qh in range(2):
            kn, b_kn = cx.qkn[2 + qh], cx.b_qkn[2 + qh]
            qn, b_qn = cx.qkn[qh], cx.b_qkn[qh]
            s.add("pe", lambda e, qh=qh, kn=kn: e.matmul(ps[2][0:64, hs(qh)], lhsT=kn[:, cs], rhs=kn[:, cs], start=True, stop=True),
                  reads=[b_kn], writes=[bp[2]])
            s.add("pe", lambda e, qh=qh, kn=kn, qn=qn: e.matmul(ps[2][0:64, 128 + qh * 64:192 + qh * 64], lhsT=kn[:, cs], rhs=qn[:, cs],
                                                                start=True, stop=True), reads=[b_kn, b_qn], writes=[bp[2]])
        P0, b_P0 = cx.nP[0], cx.b_nP[0]
        for h in range(4):
            s.add("dve", lambda e, h=h: e.scalar_tensor_tensor(out=P0[:, hb(h)], in0=ps[2][0:64, hs(h // 2)], scalar=sm[:, 8 + h:9 + h],
                                                               in1=dA[:, hs(h)], op0=ALU.mult, op1=ALU.mult),
                  reads=[bp[2], b_sm, b_dA], writes=[b_P0])
            s.add("dve", lambda e, h=h: e.tensor_tensor(out=aqt[:, hb(h)], in0=ps[2][0:64, 128 + (h // 2) * 64:192 + (h // 2) * 64],
                                                        in1=dT[:, hs(h)], op=ALU.mult), reads=[bp[2], b_dT], writes=[b_aqt])

    def neu(mt, ci, xT, b_xT):
        P0, b_P0 = cx.nP[0], cx.b_nP[0]
        c = mt * 8 + ci
        par = c % 2
        pp = (c // 2) % 2
        cs = slice(ci * 64, (ci + 1) * 64)
        sm, b_sm = cx.sm[c % 4], cx.b_sm[c % 4]
        sm2, b_sm2 = cx.sm2[c % 4], cx.b_sm2[c % 4]
        gz, b_gz = cx.w_gz[par], cx.b_w_gz[par]
        og, b_og = cx.w_og[par], cx.b_w_og[par]
        nR, b_nR = cx.nR[0], cx.b_nR[0]
        rbf, b_rbf = cx.rbf[pp], cx.b_rbf[pp]
        gb, b_gb = cx.gb[0], cx.b_gb[0]
        dA, b_dA = cx.dA[0], cx.b_dA[0]
        dT, b_dT = cx.dT[0], cx.b_dT[0]
        aqt, b_aqt = cx.aqt[pp], cx.b_aqt[pp]
        vb, b_vb = cx.vb[0], cx.b_vb[0]
        kbe, b_kbe = cx.kbe[0], cx.b_kbe[0]
        kdg, b_kdg = cx.kdg[0], cx.b_kdg[0]
        wtn, b_wtn = cx.wtn[0], cx.b_wtn[0]
        vnew, b_vnew = cx.vnew[0], cx.b_vnew[0]
        oi, b_oi = cx.oi[0], cx.b_oi[0]
        oo, b_oo = cx.oo[0], cx.b_oo[0]

        def hs(h):
            return slice(h * 64, (h + 1) * 64)

        def hv(h):
            return slice(h * 128, (h + 1) * 128)

        def hb(h):
            return slice((ci % 2) * 256 + h * 64, (ci % 2) * 256 + (h + 1) * 64)

        if GDN_STOP < 1:
            return
        if GDN_STOP < 6:
            return
        Q0, b_Q0 = cx.nQ[0], cx.b_nQ[0]
        rsh, b_rsh = cx.rsh[0], cx.b_rsh[0]
        for h in range(8):
            s.add("pe", lambda e, h=h: e.matmul(ps[7][0:64, hs(h)], lhsT=P0[:, hs(h)], rhs=IDB, start=True, stop=True),
                  reads=[b_P0, cx.b_cstb], writes=[bp[7]])
        s.add("act", lambda e: e.copy(out=Q0[:], in_=ps[7][0:64, 0:512]), reads=[bp[7]], writes=[b_Q0])
        s.add("dve", lambda e: e.scalar_tensor_tensor(out=rsh[:], in0=ps[7][0:64, 0:512], scalar=-1.0, in1=ID8, op0=ALU.mult, op1=ALU.add),
              reads=[cx.b_msk, bp[7]], writes=[b_rsh])
        s.add("dve", lambda e: e.scalar_tensor_tensor(out=nR[:], in0=ps[7][0:64, 0:512], scalar=-1.0, in1=ID8, op0=ALU.mult, op1=ALU.add),
              reads=[cx.b_msk, bp[7]], writes=[b_nR])
        a = 0
        for lvl in range(1, 6):
            P, b_P, Q, b_Q = cx.nP[a], cx.b_nP[a], cx.nQ[a], cx.b_nQ[a]
            Pn, b_Pn, Qn, b_Qn = cx.nP[1 - a], cx.b_nP[1 - a], cx.nQ[1 - a], cx.b_nQ[1 - a]
            for h in range(8):
                s.add("pe", lambda e, h=h, P=P, Q=Q: e.matmul(ps[6][0:64, hs(h)], lhsT=Q[:, hs(h)], rhs=P[:, hs(h)], start=True, stop=True),
                      reads=[b_P, b_Q], writes=[bp[6]])
            s.add("act", lambda e, Pn=Pn: e.copy(out=Pn[:], in_=ps[6][0:64, 0:512]), reads=[bp[6]], writes=[b_Pn])
            if lvl < 5:
                for h in range(8):
                    s.add("pe", lambda e, h=h, P=P, Q=Q: e.matmul(ps[7][0:64, hs(h)], lhsT=P[:, hs(h)], rhs=Q[:, hs(h)], start=True, stop=True),
                          reads=[b_P, b_Q], writes=[bp[7]])
                s.add("dve", lambda e, Qn=Qn: e.tensor_copy(out=Qn[:], in_=ps[7][0:64, 0:512]), reads=[bp[7]], writes=[b_Qn])
            for h in range(8):
                s.add("pe", lambda e, h=h, Pn=Pn: e.matmul(ps[2][0:64, hs(h)], lhsT=Pn[:, hs(h)], rhs=rsh[:, hs(h)], start=True, stop=True),
                      reads=[b_Pn, b_rsh], writes=[bp[2]])
            if lvl < 5:
                s.add("dve", lambda e: e.tensor_tensor(out=rsh[:], in0=ps[2][0:64, 0:512], in1=nR[:], op=ALU.add),
                      reads=[b_nR, bp[2]], writes=[b_rsh])
                s.add("dve", lambda e: e.tensor_tensor(out=nR[:], in0=ps[2][0:64, 0:512], in1=nR[:], op=ALU.add),
                      reads=[b_nR, bp[2]], writes=[b_nR])
            else:
                s.add("dve", lambda e: e.tensor_tensor(out=rbf[:], in0=ps[2][0:64, 0:512], in1=nR[:], op=ALU.add),
                      reads=[b_nR, bp[2]], writes=[b_rbf])
            a = 1 - a

    def post(mt, ci, xT, b_xT):
        c = mt * 8 + ci
        par = c % 2
        pp = (c // 2) % 2
        cs = slice(ci * 64, (ci + 1) * 64)
        sm, b_sm = cx.sm[c % 4], cx.b_sm[c % 4]
        sm2, b_sm2 = cx.sm2[c % 4], cx.b_sm2[c % 4]
        gz, b_gz = cx.w_gz[par], cx.b_w_gz[par]
        og, b_og = cx.w_og[par], cx.b_w_og[par]
        nR, b_nR = cx.nR[0], cx.b_nR[0]
        rbf, b_rbf = cx.rbf[pp], cx.b_rbf[pp]
        gb, b_gb = cx.gb[0], cx.b_gb[0]
        dA, b_dA = cx.dA[0], cx.b_dA[0]
        dT, b_dT = cx.dT[0], cx.b_dT[0]
        aqt, b_aqt = cx.aqt[pp], cx.b_aqt[pp]
        vb, b_vb = cx.vb[0], cx.b_vb[0]
        kbe, b_kbe = cx.kbe[0], cx.b_kbe[0]
        kdg, b_kdg = cx.kdg[0], cx.b_kdg[0]
        wtn, b_wtn = cx.wtn[0], cx.b_wtn[0]
        vnew, b_vnew = cx.vnew[0], cx.b_vnew[0]
        oi, b_oi = cx.oi[0], cx.b_oi[0]
        oo, b_oo = cx.oo[0], cx.b_oo[0]

        def hs(h):
            return slice(h * 64, (h + 1) * 64)

        def hv(h):
            return slice(h * 128, (h + 1) * 128)

        def hb(h):
            return slice((ci % 2) * 256 + h * 64, (ci % 2) * 256 + (h + 1) * 64)

        if GDN_STOP < 1:
            return
        tpz = ps[0].bitcast(BF16)
        for g in range(4):
            s.add("pe", lambda e, g=g: e.transpose(tpz[0:64, g * 128:(g + 1) * 128], cx.zc[g][:, cs], cx.ident_b),
                  reads=[cx.b_zc[g], cx.b_cstb], writes=[bp[0]])
        s.add("dve", lambda e: e.tensor_tensor(out=gz[:], in0=tpz[0:64, 0:512], in1=cx.mparam[:, 0:512], op=ALU.mult),
              reads=[bp[0], cx.b_mparam], writes=[b_gz])
        if GDN_STOP < 8:
            return
        tpb = ps[0].bitcast(BF16)
        for g in range(6):
            src_, b_src = (cx.qkn[2 + g], cx.b_qkn[2 + g]) if g < 2 else (cx.vc[g - 2], cx.b_vc[g - 2])
            s.add("pe", lambda e, g=g, src_=src_: e.transpose(tpb[0:64, g * 128:(g + 1) * 128], src_[:, cs], cx.ident_b),
                  reads=[b_src, cx.b_cstb], writes=[bp[0]])
        for h in range(4):
            s.add("dve", lambda e, h=h: e.tensor_scalar(out=vb[:, hv(h)], in0=tpb[0:64, (2 + h) * 128:(3 + h) * 128], scalar1=sm[:, 8 + h:9 + h],
                                                        scalar2=None, op0=ALU.mult), reads=[bp[0], b_sm], writes=[b_vb])
            s.add("dve", lambda e, h=h: e.tensor_scalar(out=kbe[:, hv(h)], in0=tpb[0:64, (h // 2) * 128:(h // 2 + 1) * 128],
                                                        scalar1=sm[:, 24 + h:25 + h], scalar2=None, op0=ALU.mult),
                  reads=[bp[0], b_sm], writes=[b_kbe])
            s.add("dve", lambda e, h=h: e.tensor_scalar(out=kdg[:, hv(h)], in0=tpb[0:64, (h // 2) * 128:(h // 2 + 1) * 128],
                                                        scalar1=sm[:, 20 + h:21 + h], scalar2=None, op0=ALU.mult),
                  reads=[bp[0], b_sm], writes=[b_kdg])
        if GDN_STOP < 9:
            return
        for h in range(4):
            s.add("pe", lambda e, h=h: e.matmul(ps[1][:, hs(h)], lhsT=kbe[:, hv(h)], rhs=rbf[:, hb(h)], start=True, stop=True),
                  reads=[b_kbe, b_rbf], writes=[bp[1]])
        s.add("act", lambda e: e.mul(out=wtn[:], in_=ps[1][:, 0:256], mul=-1.0), reads=[bp[1]], writes=[b_wtn])
        for h in range(4):
            s.add("pe", lambda e, h=h: e.matmul(ps[4][0:64, hv(h)], lhsT=rbf[:, hb(h)], rhs=vb[:, hv(h)], start=True, stop=False),
                  reads=[b_rbf, b_vb], writes=[bp[4]])
            s.add("pe", lambda e, h=h: e.matmul(ps[4][0:64, hv(h)], lhsT=wtn[:, hs(h)], rhs=cx.st_b[h][:, 0:128], start=False, stop=True),
                  reads=[b_wtn, cx.b_st_b[h]], writes=[bp[4]])
        s.add("act", lambda e: e.copy(out=vnew[:], in_=ps[4][0:64, :]), reads=[bp[4]], writes=[b_vnew])
        for h in range(4):
            qn, b_qn = cx.qkn[h // 2], cx.b_qkn[h // 2]
            s.add("pe", lambda e, h=h, qn=qn: e.matmul(ps[5][0:64, hv(h)], lhsT=qn[:, cs], rhs=cx.st_b[h][:, 0:128], start=True, stop=True),
                  reads=[b_qn, cx.b_st_b[h]], writes=[bp[5]])
        for h in range(4):
            s.add("pe", lambda e, h=h: e.matmul(ps[3][0:64, hv(h)], lhsT=aqt[:, hb(h)], rhs=vnew[:, hv(h)], start=True, stop=True),
                  reads=[b_aqt, b_vnew], writes=[bp[3]])
        s.add("act", lambda e: e.copy(out=oi[:], in_=ps[3][0:64, :]), reads=[bp[3]], writes=[b_oi])
        for h in range(4):
            s.add("dve", lambda e, h=h: e.scalar_tensor_tensor(out=oo[:, hv(h)], in0=ps[5][0:64, hv(h)], scalar=sm[:, 16 + h:17 + h],
                                                               in1=oi[:, hv(h)], op0=ALU.mult, op1=ALU.add),
                  reads=[bp[5], b_sm, b_oi], writes=[b_oo])
        for h in range(4):
            s.add("pe", lambda e, h=h: e.matmul(ps[1][:, hv(h)], lhsT=kdg[:, hv(h)], rhs=vnew[:, hv(h)], start=True, stop=True),
                  reads=[b_kdg, b_vnew], writes=[bp[1]])
        for h in range(4):
            s.add("dve", lambda e, h=h: e.scalar_tensor_tensor(out=cx.st_f[h][:, 0:128], in0=cx.st_f[h][:, 0:128], scalar=sm2[:, 4 + h:5 + h],
                                                               in1=ps[1][:, hv(h)], op0=ALU.mult, op1=ALU.add),
                  reads=[cx.b_st_f[h], b_sm2, bp[1]], writes=[cx.b_st_f[h]])
            s.add("act", lambda e, h=h: e.copy(out=cx.st_b[h][:, 0:128], in_=cx.st_f[h][:, 0:128]), reads=[cx.b_st_f[h]], writes=[cx.b_st_b[h]])
        if GDN_STOP < 10:
            return
        for h in range(4):
            s.add("act", lambda e, h=h: e.activation(out=oi[:, hv(h)], in_=oo[:, hv(h)], func=AF.Square, accum_out=sm[:, 32 + h:33 + h]),
                  reads=[b_oo], writes=[b_oi, b_sm])
        s.add("act", lambda e: e.activation(out=sm[:, 36:40], in_=sm[:, 32:36], func=AF.Ln, scale=1.0 / 128, bias=cx.epsq[0:64, 3:4]),
              reads=[b_sm, cx.b_epsq], writes=[b_sm])
        s.add("act", lambda e: e.activation(out=sm[:, 40:44], in_=sm[:, 36:40], func=AF.Exp, scale=-0.5), reads=[b_sm], writes=[b_sm])
        for h in range(4):
            s.add("dve", lambda e, h=h: e.scalar_tensor_tensor(out=og[:, hv(h)], in0=oo[:, hv(h)], scalar=sm[:, 40 + h:41 + h],
                                                               in1=gz[:, hv(h)], op0=ALU.mult, op1=ALU.mult),
                  reads=[b_oo, b_sm, b_gz], writes=[b_og])
        for j in range(4):
            s.add("pe", lambda e, j=j: e.transpose(tpb[:, j * 64:(j + 1) * 64], og[:, j * 128:(j + 1) * 128], IDB),
                  reads=[b_og, cx.b_cstb], writes=[bp[0]])
        s.add("act", lambda e: e.copy(out=cx.gomt[:, :, cs], in_=tpb[:, 0:256].rearrange("p (j t) -> p j t", j=4)),
              reads=[bp[0]], writes=[cx.b_gomt])
    return emit


KINDS = ["fox", "gla", "gdn", "fox"]
NCOLS = {"fox": 2052, "gla": 1552, "gdn": 1544}


def host_params(inp, hh):
    f32 = np.float32
    P = {}
    nw = np.asarray(inp["norm_w"], f32)
    P["normw"] = np.ascontiguousarray(nw.reshape(4, 8, 128).transpose(2, 0, 1).reshape(128, 32))
    for li in range(2):
        w = np.asarray(inp["fox_w_in"][li], f32)
        s = slice(hh * 512, hh * 512 + 512)
        P[f"fox_win{li}"] = np.ascontiguousarray(np.concatenate(
            [w[:, 0:1024][:, s], w[:, 1024:2048][:, s], w[:, 3072:4096][:, s], w[:, 2048:3072][:, s],
             w[:, 4096 + hh * 4:4096 + hh * 4 + 4]], axis=1))
        fx = np.zeros((128, 16), f32)
        fx[:, 0] = np.asarray(inp["fox_q_gain"][li], f32)
        fx[:, 1] = np.asarray(inp["fox_k_gain"][li], f32)
        fx[:, 8:12] = np.asarray(inp["fox_b_f"][li], f32)[hh * 4:hh * 4 + 4][None, :]
        P[f"fox_fx{li}"] = fx
    w = np.asarray(inp["gla_w_in"][0], f32)
    P["gla_win"] = np.ascontiguousarray(np.concatenate(
        [w[:, hh * 256:hh * 256 + 256], w[:, 512 + hh * 256:512 + hh * 256 + 256], w[:, 3072:3088],
         w[:, 1024 + hh * 512:1024 + hh * 512 + 512], w[:, 2048 + hh * 512:2048 + hh * 512 + 512]], axis=1))
    P["gla_wup"] = np.ascontiguousarray(np.concatenate(
        [np.asarray(inp["gla_w_gate_up"][0], f32)[:, hh * 256:hh * 256 + 256],
         np.asarray(inp["gla_b_gate"][0], f32)[None, hh * 256:hh * 256 + 256]], axis=0))
    P["gla_gain"] = np.ascontiguousarray(np.tile(np.asarray(inp["gla_o_gain"][0], f32)[None, :], (64, 2)))
    w = np.asarray(inp["gdn_w_in"][0], f32)
    qc = slice(hh * 256, hh * 256 + 256)
    kc = slice(512 + hh * 256, 512 + hh * 256 + 256)
    vs = slice(1024 + hh * 512, 1024 + hh * 512 + 512)
    P["gdn_win"] = np.ascontiguousarray(np.concatenate(
        [w[:, qc], w[:, kc], w[:, vs], w[:, 2048 + hh * 512:2048 + hh * 512 + 512],
         w[:, 3072 + hh * 4:3072 + hh * 4 + 4], w[:, 3080 + hh * 4:3080 + hh * 4 + 4]], axis=1))
    cw = np.asarray(inp["gdn_conv_w"][0], f32)
    cwc = np.concatenate([cw[:, qc], cw[:, kc], cw[:, vs]], axis=1)
    P["gdn_gp"] = np.ascontiguousarray(cwc.reshape(4, 8, 128).transpose(2, 1, 0).reshape(128, 32))
    mp = np.zeros((64, 1024), f32)
    mp[:, 0:512] = np.tile(np.asarray(inp["gdn_o_gain"][0], f32)[None, :], (64, 4))
    mp[:, 512:516] = np.asarray(inp["gdn_dt_bias"][0], f32)[hh * 4:hh * 4 + 4][None, :]
    mp[:, 516:520] = np.asarray(inp["gdn_a_log"][0], f32)[hh * 4:hh * 4 + 4][None, :]
    P["gdn_mp"] = mp
    P["wout"] = [np.ascontiguousarray(np.asarray(inp["fox_w_out"][0], f32)), np.ascontiguousarray(np.asarray(inp["gla_w_out"][0], f32)),
                 np.ascontiguousarray(np.asarray(inp["gdn_w_out"][0], f32)), np.ascontiguousarray(np.asarray(inp["fox_w_out"][1], f32))]
    return P


GROUPS = [[0, 1], [2, 3], [4, 5], [6, 7]]


def exchange(cx):
    for j in range(cx.NCH):
        cx.s.add("pool", lambda e, j=j: e.collective_compute("AllGather", ALU.bypass, replica_groups=cx.groups,
                                                             ins=[cx.gohalf_d[j]], outs=[cx.gofull_d[j]]),
                 reads=[cx.b_gohalf_d[j]], writes=[cx.b_gofull_d[j]], dma=True, cc=True)


def build_fused(S, groups=None):
    from contextlib import ExitStack
    nc = bass.Bass("TRN2", target_bir_lowering=False)
    cx = Ctx()
    cx.nc, cx.S = nc, S
    cx.s = Sched(nc)
    cx.groups = groups or GROUPS
    setup_consts(cx)
    common_setup(cx)
    out_d = nc.dram_tensor("out", [S, 1024], F32, kind="ExternalOutput").ap()
    cx.gohalf_d = [nc.dram_tensor(f"gohalf{j}", [512, cx.CW], BF16, kind="Internal").ap() for j in range(cx.NCH)]
    cx.b_gohalf_d = [Buf(f"gohalf{j}") for j in range(cx.NCH)]

    def din(name, shape):
        return nc.dram_tensor(name, shape, F32, kind="ExternalInput").ap()
    fox_w = [din("fox_win0", [1024, 2052]), din("fox_win1", [1024, 2052])]
    cx.fx_d = [din("fox_fx0", [128, 16]), din("fox_fx1", [128, 16])]
    gla_w, wup_d, gain_d = din("gla_win", [1024, 1552]), din("gla_wup", [17, 256]), din("gla_gain", [64, 512])
    gdn_w, gp_d, mp_d = din("gdn_win", [1024, 1544]), din("gdn_gp", [128, 32]), din("gdn_mp", [64, 1024])
    wout = [din(f"wout{i}", [1024, 1024]) for i in range(4)]
    with ExitStack() as es:
        cx.es = es
        fox_setup(cx)
    cx.es = None
    mixer_common_setup(cx)
    with ExitStack() as es:
        cx.es = es
        gla_setup(cx)
    cx.es = None
    gdn_setup(cx)

    cx.s.scopes = getattr(build_fused, "scopes", False)
    cx.s.phase = "L0_proj"
    fox_load_params(cx, 0)
    boundary(cx, 0, "fox", cx.x_in, cx.xres_d, None, fox_w[0], 2052, fox_proj(cx))
    cx.s.phase = "L0_attn"
    fox_attn(cx)
    exchange(cx)
    cx.s.barrier()
    cx.s.phase = "L1_gla"
    gla_load_params(cx, wup_d, gain_d)
    boundary(cx, 1, "gla", cx.x_in, cx.xres_d, wout[0], gla_w, 1552, gla_proj(cx))
    exchange(cx)
    cx.s.barrier()
    cx.s.phase = "L2_gdn"
    gdn_load_params(cx, gp_d, mp_d)
    boundary(cx, 2, "gdn", cx.xres_d, cx.xres_d, wout[1], gdn_w, 1544, gdn_proj(cx))
    exchange(cx)
    cx.s.barrier()
    cx.s.phase = "L3_proj"
    fox_load_params(cx, 1)
    boundary(cx, 3, "fox", cx.xres_d, cx.xres_d, wout[2], fox_w[1], 2052, fox_proj(cx))
    cx.s.phase = "L3_attn"
    fox_attn(cx)
    exchange(cx)
    cx.s.phase = "L4_final"
    boundary(cx, 4, None, cx.xres_d, cx.xres_d, wout[3], None, 0, None, final_out=out_d)
    cx.s.emit()
    return nc


def kernel(**inp):
    x = np.asarray(inp["x"], np.float32)
    B, S, D = x.shape
    cst = make_consts()
    params = [host_params(inp, hh) for hh in range(2)]
    nc = build_fused(S)
    in_maps = []
    for c in range(8):
        P = params[c % 2]
        m = {"x": np.ascontiguousarray(x[c // 2]), "cst": cst, "normw_in": P["normw"],
             "fox_win0": P["fox_win0"], "fox_win1": P["fox_win1"], "fox_fx0": P["fox_fx0"], "fox_fx1": P["fox_fx1"],
             "gla_win": P["gla_win"], "gla_wup": P["gla_wup"], "gla_gain": P["gla_gain"],
             "gdn_win": P["gdn_win"], "gdn_gp": P["gdn_gp"], "gdn_mp": P["gdn_mp"]}
        for i in range(4):
            m[f"wout{i}"] = P["wout"][i]
        in_maps.append(m)
    res = run_bass_kernel_spmd(nc, in_maps, core_ids=list(range(8)))
    out = np.empty((B, S, D), np.float32)
    for b in range(B):
        out[b] = np.asarray(res.results[2 * b]["out"])
    return out
```

```python
import numpy as np
import concourse.bass as bass
import concourse.mybir as mybir
from concourse.bass_utils import run_bass_kernel_spmd

F32 = mybir.dt.float32
BF16 = mybir.dt.bfloat16
AF = mybir.ActivationFunctionType
ALU = mybir.AluOpType
AX = mybir.AxisListType

ENGS = ["pe", "act", "dve", "pool", "sp"]
NDSEM = 8
SEM_ROLL = 20000


class Buf:
    __slots__ = ("name", "lw", "rd", "rd_dma", "psum", "wr_dma")

    def __init__(self, name="", psum=False):
        self.name = name
        self.psum = psum
        self.lw = None
        self.rd = {}
        self.rd_dma = []
        self.wr_dma = []


class Op:
    __slots__ = ("eng", "fn", "deps", "sig", "idx", "dma", "seq", "sem", "val", "barred", "cc", "phase")


class Sched:
    def __init__(self, nc):
        self.nc = nc
        self.phase = None
        self.scopes = False
        self.q = {e: [] for e in ENGS}

    def begin_record(self):
        self._rec = []

    def end_record(self):
        r, self._rec = self._rec, None
        return r

    def replay_zip(self, a, b):
        ia = ib = 0
        while ia < len(a) or ib < len(b):
            if ib >= len(b) or (ia < len(a) and ia * len(b) <= ib * len(a)):
                self.add(*a[ia])
                ia += 1
            else:
                self.add(*b[ib])
                ib += 1

    def add(self, eng, fn, reads=(), writes=(), dma=False, cc=False):
        if getattr(self, "_rec", None) is not None:
            self._rec.append((eng, fn, tuple(reads), tuple(writes), dma, cc))
            return None
        op = Op()
        op.eng, op.fn, op.dma, op.sig, op.cc = eng, fn, dma, False, cc
        op.seq = op.sem = op.val = None
        op.barred = False
        op.phase = self.phase
        deps, seen = [], set()

        def adddep(d):
            if d is None or id(d) in seen:
                return
            seen.add(id(d))
            deps.append(d)

        for b in reads:
            adddep(b.lw)
            for w in b.wr_dma:
                adddep(w)
            if b.psum:
                for e2, r in b.rd.items():
                    if e2 != eng:
                        adddep(r)
        for b in writes:
            had_readers = bool(b.rd) or bool(b.rd_dma)
            for r in b.rd.values():
                adddep(r)
            for r in b.rd_dma:
                adddep(r)
            if dma and not cc:
                if had_readers or (b.lw is not None and not b.lw.dma):
                    adddep(b.lw)
                    b.wr_dma = []
            else:
                adddep(b.lw)
                for w in b.wr_dma:
                    adddep(w)
        op.deps = [d for d in deps
                   if not (eng == "pe" and d.eng == "pe" and not d.dma and not dma)]
        for b in reads:
            if dma:
                b.rd_dma.append(op)
            else:
                b.rd[eng] = op
        for b in writes:
            b.rd = {}
            b.rd_dma = []
            if dma and not cc:
                b.wr_dma.append(op)
                b.lw = None
            else:
                b.lw = op
                b.wr_dma = []
        op.idx = len(self.q[eng])
        self.q[eng].append(op)
        return op

    def barrier(self):
        pre = []
        for e in ENGS:
            last = None
            for op in self.q[e]:
                if op.dma:
                    if not getattr(op, "barred", False):
                        pre.append(op)
                        op.barred = True
                else:
                    last = op
            if last is not None:
                pre.append(last)
        for e in ENGS:
            op = self.add(e, lambda eng: eng.nop())
            op.deps = [d for d in pre if d is not op]

    def emit(self):
        nc = self.nc
        for e in ENGS:
            for op in self.q[e]:
                for d in op.deps:
                    d.sig = True
        csem = {}
        for e in ENGS:
            nsig = sum(1 for op in self.q[e] if op.sig and not op.dma)
            csem[e] = [nc.alloc_semaphore(f"c_{e}_{i}") for i in range(nsig // SEM_ROLL + 1)]
            s = 0
            for op in self.q[e]:
                if op.sig and not op.dma:
                    op.sem = csem[e][s // SEM_ROLL]
                    op.val = s % SEM_ROLL + 1
                    op.seq = s
                    s += 1
        dsem = {}
        for e in ENGS:
            nd = sum(1 for op in self.q[e] if op.dma)
            if nd == 0:
                continue
            dsem[e] = [nc.alloc_semaphore(f"d_{e}_{i}") for i in range(NDSEM)]
            n = 0
            for op in self.q[e]:
                if op.dma and op.cc:
                    op.sem = nc.alloc_semaphore(f"cc_{e}_{op.idx}")
                    op.val = 1
                elif op.dma:
                    op.sem = dsem[e][n % NDSEM]
                    op.val = 16 * (n // NDSEM + 1)
                    n += 1

        def run_engine(e, eng):
            waited = {}

            def wait(sem, val):
                key = sem.num
                if waited.get(key, 0) >= val:
                    return
                waited[key] = val
                eng.wait_ge(sem, val)

            cur = [None, None]

            def scope(ph):
                if not self.scopes or ph == cur[0]:
                    return
                if cur[1] is not None:
                    cur[1].__exit__(None, None, None)
                    cur[1] = None
                cur[0] = ph
                if ph is not None:
                    cur[1] = nc.named_scope(ph)
                    cur[1].__enter__()

            for op in self.q[e]:
                scope(op.phase)
                for d in op.deps:
                    wait(d.sem, d.val)
                if op.dma and op.cc:
                    ins = op.fn(eng)
                    ins.then_inc(op.sem)
                elif op.dma:
                    if op.val > 16:
                        wait(op.sem, op.val - 16)
                    ins = op.fn(eng)
                    ins.then_inc(op.sem, 16)
                else:
                    ins = op.fn(eng)
                    if op.sig:
                        ins.then_inc(op.sem, 1)
            scope(None)
            if e in dsem:
                last = {}
                for op in self.q[e]:
                    if op.dma:
                        last[op.sem.num] = (op.sem, max(op.val, last.get(op.sem.num, (None, 0))[1]))
                for sem, val in last.values():
                    wait(sem, val)

        with nc.Block() as block:
            @block.tensor
            def _(eng):
                run_engine("pe", eng)

            @block.scalar
            def _(eng):
                run_engine("act", eng)

            @block.vector
            def _(eng):
                run_engine("dve", eng)

            @block.gpsimd
            def _(eng):
                run_engine("pool", eng)

            @block.sync
            def _(eng):
                run_engine("sp", eng)


D_MODEL = 1024
NB = 4
RMS_EPS = 1e-6
NCST = 7 * 128


def make_consts():
    c = np.zeros((128, NCST), np.float32)
    i = np.arange(128)
    c[:, 0:128] = np.eye(128)
    c[:, 128:256] = (i[:, None] <= i[None, :])
    same = (i[:, None] // 64) == (i[None, :] // 64)
    c[:, 256:384] = (i[:, None] <= i[None, :]) & same
    c[:, 384:512] = (i[:, None] < i[None, :]) & same
    c[:, 512:640] = 1.0
    c[:, 640:768] = same
    c[:, 768:896] = (i[:, None] > i[None, :]) & same
    return c


class Ctx:
    pass


def sb(cx, name, shape, dt):
    es = getattr(cx, "es", None)
    if es is not None:
        return es.enter_context(cx.nc.sbuf_tensor(name, shape, dt))
    return cx.nc.alloc_sbuf_tensor(name, shape, dt)


def setup_consts(cx):
    nc, S = cx.nc, cx.S
    cx.cst_d = nc.dram_tensor("cst", [128, NCST], F32, kind="ExternalInput").ap()
    cx.cst = sb(cx, "cst_sb", [128, NCST], F32)
    cx.cstb = sb(cx, "cstb_sb", [128, NCST], BF16)
    cx.b_cst = Buf("cst")
    cx.b_cstb = Buf("cstb")
    cx.s.add("sp", lambda e: e.dma_start(out=cx.cst[:], in_=cx.cst_d), writes=[cx.b_cst], dma=True)
    cx.s.add("dve", lambda e: e.tensor_copy(out=cx.cstb[:], in_=cx.cst[:]), reads=[cx.b_cst], writes=[cx.b_cstb])
    cx.ident_b = cx.cstb[:, 0:128]
    cx.ones_b = cx.cstb[:, 512:640]
    cx.tri_f = cx.cst[:, 128:256]
    cx.tri_b = cx.cstb[:, 128:256]
    cx.epsq = sb(cx, "epsq", [128, 4], F32)
    cx.b_epsq = Buf("epsq")
    cx.s.add("dve", lambda e: e.memset(cx.epsq[:, 0:1], float(128 * RMS_EPS)), writes=[cx.b_epsq])
    cx.s.add("dve", lambda e: e.memset(cx.epsq[:, 1:2], 1.0), writes=[cx.b_epsq])
    cx.s.add("dve", lambda e: e.memset(cx.epsq[:, 2:3], float(D_MODEL * RMS_EPS)), writes=[cx.b_epsq])
    cx.s.add("dve", lambda e: e.memset(cx.epsq[:, 3:4], float(RMS_EPS)), writes=[cx.b_epsq])
    cx.ident_f = cx.cst[:, 0:128]
    cx.ones_f = cx.cst[:, 512:640]
    cx.ps = [nc.alloc_psum_tensor(f"ps{i}", [128, 512], F32) for i in range(8)]
    cx.b_ps = [Buf(f"ps{i}", psum=True) for i in range(8)]


def load_weights(cx, tag, w_d, ncols, nw_row, wbuf, b_w):
    s = cx.s
    n = 0
    for kc in range(8):
        for c0 in range(0, ncols, 516):
            c1 = min(ncols, c0 + 516)
            st = cx.wst[n % 2]
            b_st = cx.b_wst[n % 2]
            n += 1
            s.add("sp", lambda e, st=st, kc=kc, c0=c0, c1=c1: e.dma_start(out=st[:, 0:c1 - c0], in_=w_d[kc * 128:(kc + 1) * 128, c0:c1]),
                  writes=[b_st], dma=True)
            if nw_row is not None:
                s.add("dve", lambda e, st=st, kc=kc, c0=c0, c1=c1: e.tensor_scalar(
                    out=wbuf[:, kc, c0:c1], in0=st[:, 0:c1 - c0], scalar1=cx.normw[:, nw_row * 8 + kc: nw_row * 8 + kc + 1],
                    scalar2=None, op0=ALU.mult), reads=[b_st, cx.b_normw], writes=[b_w])
            else:
                s.add("dve", lambda e, st=st, kc=kc, c0=c0, c1=c1: e.tensor_copy(out=wbuf[:, kc, c0:c1], in_=st[:, 0:c1 - c0]),
                      reads=[b_st], writes=[b_w])


def boundary(cx, L, kind, x_src, x_dst, wout_d, win_d, ncols, emit_proj, final_out=None, tok_range=None):
    nc, s, S = cx.nc, cx.s, cx.S
    MT = S // 512
    has_out = wout_d is not None
    if has_out:
        load_weights(cx, f"wo{L}", wout_d, 1024, None, cx.wout, cx.b_wout)
    if win_d is not None:
        load_weights(cx, f"wi{L}", win_d, ncols, L, cx.win, cx.b_win)
    mts = list(range(MT) if tok_range is None else tok_range)
    dst = final_out if final_out is not None else x_dst

    def phase_a(mt):
        if has_out:
            go, b_go = cx.goT[mt % 2], cx.b_goT[mt % 2]
            jj, c0 = (mt * 512) // cx.CW, (mt * 512) % cx.CW
            s.add("sp", lambda e: e.dma_start(
                out=go[:], in_=cx.gofull_d[jj][:, c0:c0 + 512].rearrange("(c p) t -> p c t", p=128)),
                reads=[cx.b_gofull_d[jj]], writes=[b_go], dma=True)
        for sub in range(4):
            sub_a(mt, sub)

    def sub_a(mt, sub):
        tt = mt * 4 + sub
        xt, b_xt = cx.xt[tt % 3], cx.b_xt[tt % 3]
        s.add("sp", lambda e: e.dma_start(out=xt[:], in_=x_src[tt * 128:(tt + 1) * 128, :]),
              reads=[cx.b_xres_d[tt]] if x_src is cx.xres_d else [], writes=[b_xt], dma=True)
        if has_out:
            go, b_go = cx.goT[mt % 2], cx.b_goT[mt % 2]
            for half in range(2):
                yp, b_yp = cx.ps[1], cx.b_ps[1]
                for kc in range(8):
                    s.add("pe", lambda e, kc=kc, half=half: e.matmul(
                        yp[:, :], lhsT=go[:, kc, sub * 128:(sub + 1) * 128],
                        rhs=cx.wout[:, kc, half * 512:(half + 1) * 512], start=(kc == 0), stop=(kc == 7)),
                        reads=[b_go, cx.b_wout], writes=[b_yp])
                s.add("dve", lambda e, half=half: e.tensor_tensor(
                    out=xt[:, half * 512:(half + 1) * 512], in0=yp[:, :], in1=xt[:, half * 512:(half + 1) * 512],
                    op=ALU.add), reads=[b_yp, b_xt], writes=[b_xt])
            s.add("pool", lambda e: e.dma_start(out=dst[tt * 128:(tt + 1) * 128, :], in_=xt[:]),
                  reads=[b_xt], writes=[cx.b_xres_d[tt]], dma=True)
        if win_d is None:
            return
        sq, b_sq = cx.junk[tt % 2], cx.b_junk[tt % 2]
        ssq, b_ssq = cx.stat[tt % 4], cx.b_stat[tt % 4]
        s.add("act", lambda e: e.activation(out=sq[:], in_=xt[:], func=AF.Square, accum_out=ssq[:, 0:1]),
              reads=[b_xt], writes=[b_sq, b_ssq])
        s.add("act", lambda e: e.activation(out=ssq[:, 2:3], in_=ssq[:, 0:1], func=AF.Ln, bias=cx.epsq[:, 2:3]),
              reads=[b_ssq, cx.b_epsq], writes=[b_ssq])
        s.add("act", lambda e: e.activation(out=ssq[:, 1:2], in_=ssq[:, 2:3], func=AF.Exp, scale=-0.5),
              reads=[b_ssq], writes=[b_ssq])
        xs, b_xs = cx.xs[tt % 4], cx.b_xs[tt % 4]
        s.add("dve", lambda e: e.tensor_scalar(
            out=xs[:], in0=xt[:], scalar1=ssq[:, 1:2], scalar2=float(np.sqrt(D_MODEL)),
            op0=ALU.mult, op1=ALU.mult), reads=[b_xt, b_ssq], writes=[b_xs])

    def phase_t(mt):
        xT, b_xT = cx.xT[mt % 2], cx.b_xT[mt % 2]
        tp = cx.ps[0].bitcast(BF16)
        b_tp = cx.b_ps[0]
        for sub in range(4):
            tt = mt * 4 + sub
            xs, b_xs = cx.xs[tt % 4], cx.b_xs[tt % 4]
            for kc in range(8):
                s.add("pe", lambda e, kc=kc, xs=xs: e.transpose(
                    tp[:, kc * 128:(kc + 1) * 128], xs[:, kc * 128:(kc + 1) * 128], cx.ident_b),
                    reads=[b_xs, cx.b_cstb], writes=[b_tp])
            s.add("act", lambda e, sub=sub: e.copy(
                out=xT[:, :, sub * 128:(sub + 1) * 128], in_=tp[:, :].rearrange("p (c t) -> p c t", c=8)),
                reads=[b_tp], writes=[b_xT])

    if win_d is None:
        for mt in mts:
            phase_a(mt)
        return
    phase_a(mts[0])
    phase_t(mts[0])
    for i, mt in enumerate(mts):
        nxt = mts[i + 1] if i + 1 < len(mts) else None
        if nxt is not None:
            phase_a(nxt)
        emit_proj(mt, cx.xT[mt % 2], cx.b_xT[mt % 2])
        if nxt is not None:
            phase_t(nxt)
    if hasattr(emit_proj, "flush"):
        emit_proj.flush()


def fox_setup(cx):
    nc, S = cx.nc, cx.S
    cx.qT_d = nc.dram_tensor("qT_d", [4, 128, S], BF16, kind="Internal").ap()
    cx.kT_d = nc.dram_tensor("kT_d", [4, 128, S], BF16, kind="Internal").ap()
    cx.gT_d = nc.dram_tensor("gT_d", [4, 128, S], BF16, kind="Internal").ap()
    cx.v_d = nc.dram_tensor("v_d", [S, 512], BF16, kind="Internal").ap()
    cx.crow_d = nc.dram_tensor("crow_d", [4, S], F32, kind="Internal").ap()
    cx.b_qT_d, cx.b_kT_d, cx.b_gT_d, cx.b_v_d, cx.b_crow_d = (Buf("qT_d"), Buf("kT_d"), Buf("gT_d"), Buf("v_d"), Buf("crow_d"))
    cx.fxp = sb(cx, "fxp", [128, 16], F32)
    cx.b_fxp = Buf("fxp")
    cx.cumcol = sb(cx, "cumcol", [128, S // 128, 4], F32)
    cx.b_cumcol = Buf("cumcol")
    cx.carry = sb(cx, "carry", [128, 4], F32)
    cx.b_carry = Buf("carry")
    cx.fl = [sb(cx, f"fl{i}", [128, 16], F32) for i in range(2)]
    cx.b_fl = [Buf(f"fl{i}") for i in range(2)]
    cx.crow_sb = [sb(cx, f"crow{i}", [4, 128], F32) for i in range(2)]
    cx.b_crow_sb = [Buf(f"crow{i}") for i in range(2)]
    cx.sqb = [sb(cx, f"sqb{i}", [128, 512], BF16) for i in range(2)]
    cx.b_sqb = [Buf(f"sqb{i}") for i in range(2)]
    cx.rs = [sb(cx, f"rs{i}", [128, 512], F32) for i in range(2)]
    cx.b_rs = [Buf(f"rs{i}") for i in range(2)]
    cx.ob = [sb(cx, f"ob{i}", [128, 512], BF16) for i in range(4)]
    cx.b_ob = [Buf(f"ob{i}") for i in range(4)]
    cx.b_p2 = [Buf(f"p2_{i}") for i in range(4)]
    cx.kT_sb = sb(cx, "kT_sb", [128, S], BF16)
    cx.b_kT_sb = Buf("kT_sb")
    cx.v_sb = sb(cx, "v_sb", [128, S // 128, 128], BF16)
    cx.b_v_sb = Buf("v_sb")
    cx.q_sb = [sb(cx, f"q_sb{i}", [128, 512], BF16) for i in range(2)]
    cx.b_q_sb = [Buf(f"q_sb{i}") for i in range(2)]
    cx.g_sb = [sb(cx, f"g_sb{i}", [128, 512], BF16) for i in range(2)]
    cx.b_g_sb = [Buf(f"g_sb{i}") for i in range(2)]
    cx.cnq = [sb(cx, f"cnq{i}", [128, 512], F32) for i in range(2)]
    cx.b_cnq = [Buf(f"cnq{i}") for i in range(2)]
    cx.tt_sb = [sb(cx, f"tt_sb{i}", [128, 512], F32) for i in range(4)]
    cx.b_tt_sb = [Buf(f"tt_sb{i}") for i in range(4)]
    cx.p_sb = [sb(cx, f"p_sb{i}", [128, 512], BF16) for i in range(4)]
    cx.b_p_sb = [Buf(f"p_sb{i}") for i in range(4)]
    cx.rl = [sb(cx, f"rl{i}", [128, 512], F32) for i in range(2)]
    cx.b_rl = [Buf(f"rl{i}") for i in range(2)]
    cx.o1 = [sb(cx, f"o1{i}", [128, 512], F32) for i in range(2)]
    cx.b_o1 = [Buf(f"o1{i}") for i in range(2)]
    cx.go_sb = [sb(cx, f"go_sb{i}", [128, 512], BF16) for i in range(2)]
    cx.b_go_sb = [Buf(f"go_sb{i}") for i in range(2)]


def fox_load_params(cx, li):
    s = cx.s
    d = cx.fx_d[li]
    s.add("sp", lambda e: e.dma_start(out=cx.fxp[:], in_=d), writes=[cx.b_fxp], dma=True)
    s.add("dve", lambda e: e.tensor_scalar(out=cx.fxp[:, 1:2], in0=cx.fxp[:, 1:2], scalar1=float(np.sqrt(128.0)),
                                           scalar2=None, op0=ALU.mult), reads=[cx.b_fxp], writes=[cx.b_fxp])
    s.add("dve", lambda e: e.memset(cx.carry[:], 0.0), writes=[cx.b_carry])


def fox_proj(cx):
    s = cx.s
    win = cx.win

    p2 = cx.ps[2]
    b_p2 = cx.b_ps[2]
    pend = []

    def phaseA(tt, sub, xT, b_xT):
        for kc in range(8):
            s.add("pe", lambda e, kc=kc: e.matmul(
                p2[:, 0:4], lhsT=xT[:, kc, sub * 128:(sub + 1) * 128], rhs=win[:, kc, 2048:2052],
                start=(kc == 0), stop=(kc == 7)), reads=[cx.b_win, b_xT], writes=[b_p2])
        fl, b_fl = cx.fl[tt % 2], cx.b_fl[tt % 2]
        s.add("dve", lambda e: e.tensor_tensor(out=fl[:, 0:4], in0=p2[:, 0:4], in1=cx.fxp[:, 8:12], op=ALU.add),
              reads=[b_p2, cx.b_fxp], writes=[b_fl])
        s.add("act", lambda e: e.activation(out=fl[:, 4:8], in_=fl[:, 0:4], func=AF.Exp, scale=-1.0),
              reads=[b_fl], writes=[b_fl])
        s.add("act", lambda e: e.activation(out=fl[:, 8:12], in_=fl[:, 4:8], func=AF.Ln, bias=cx.epsq[:, 1:2]),
              reads=[b_fl, cx.b_epsq], writes=[b_fl])

    def phaseB(tt):
        fl, b_fl = cx.fl[tt % 2], cx.b_fl[tt % 2]
        s.add("pe", lambda e: e.matmul(p2[:, 8:12], lhsT=cx.tri_f, rhs=fl[:, 8:12], start=True, stop=True),
              reads=[b_fl, cx.b_cst], writes=[b_p2])
        s.add("pe", lambda e: e.matmul(p2[:, 16:20], lhsT=cx.ones_f, rhs=fl[:, 8:12], start=True, stop=True),
              reads=[b_fl, cx.b_cst], writes=[b_p2])
        s.add("dve", lambda e: e.tensor_tensor(out=cx.cumcol[:, tt, :], in0=p2[:, 8:12], in1=cx.carry[:], op=ALU.add),
              reads=[b_p2, cx.b_carry], writes=[cx.b_cumcol])
        s.add("dve", lambda e: e.tensor_tensor(out=cx.carry[:], in0=p2[:, 16:20], in1=cx.carry[:], op=ALU.add),
              reads=[b_p2, cx.b_carry], writes=[cx.b_carry])

    def phaseC(tt):
        s.add("pe", lambda e: e.matmul(p2[0:4, 32:160], lhsT=cx.cumcol[:, tt, :], rhs=cx.ident_f, start=True, stop=True),
              reads=[cx.b_cumcol, cx.b_cst], writes=[b_p2])
        cr, b_cr = cx.crow_sb[tt % 2], cx.b_crow_sb[tt % 2]
        s.add("dve", lambda e: e.tensor_copy(out=cr[:], in_=p2[0:4, 32:160]), reads=[b_p2], writes=[b_cr])
        s.add("pool", lambda e: e.dma_start(out=cx.crow_d[:, tt * 128:(tt + 1) * 128], in_=cr[:]),
              reads=[b_cr], writes=[cx.b_crow_d], dma=True)

    def fchain(tt, sub, xT, b_xT):
        phaseA(tt, sub, xT, b_xT)
        if tt >= 1:
            phaseB(tt - 1)
        if tt >= 2:
            phaseC(tt - 2)

    def flush():
        TT = cx.S // 128
        phaseB(TT - 1)
        phaseC(TT - 2)
        phaseC(TT - 1)

    def emit(mt, xT, b_xT):
        for g in range(12):
            typ, h = g // 4, g % 4
            pb = 3 + (g % 2)
            ps, b_p = cx.ps[pb], cx.b_ps[pb]
            for kc in range(8):
                s.add("pe", lambda e, kc=kc, g=g, ps=ps: e.matmul(
                    ps[:, :], lhsT=win[:, kc, g * 128:(g + 1) * 128], rhs=xT[:, kc, :], start=(kc == 0), stop=(kc == 7)),
                    reads=[cx.b_win, b_xT], writes=[b_p])
            ob, b_ob = cx.ob[g % 4], cx.b_ob[g % 4]
            if typ < 2:
                sq, b_sq = cx.sqb[g % 2], cx.b_sqb[g % 2]
                rs, b_rs = cx.rs[g % 2], cx.b_rs[g % 2]
                s.add("act", lambda e, sq=sq, ps=ps: e.activation(out=sq[:], in_=ps[:, :], func=AF.Square),
                      reads=[b_p], writes=[b_sq])
                p5, b_p5 = cx.ps[5], cx.b_ps[5]
                s.add("pe", lambda e, sq=sq, p5=p5: e.matmul(p5[:, :], lhsT=cx.ones_b, rhs=sq[:], start=True, stop=True),
                      reads=[b_sq, cx.b_cstb], writes=[b_p5])
                s.add("act", lambda e, rs=rs, p5=p5: e.activation(
                    out=rs[:], in_=p5[:, :], func=AF.Ln, bias=cx.epsq[:, 0:1]),
                    reads=[b_p5, cx.b_epsq], writes=[b_rs])
                s.add("act", lambda e, rs=rs: e.activation(out=rs[:], in_=rs[:], func=AF.Exp, scale=-0.5),
                      reads=[b_rs], writes=[b_rs])
                s.add("dve", lambda e, ob=ob, ps=ps, rs=rs, typ=typ: e.scalar_tensor_tensor(
                    out=ob[:], in0=ps[:, :], scalar=cx.fxp[:, typ:typ + 1], in1=rs[:], op0=ALU.mult, op1=ALU.mult),
                    reads=[b_p, b_rs, cx.b_fxp], writes=[b_ob])
                dst, b_dst = (cx.qT_d, cx.b_qT_d) if typ == 0 else (cx.kT_d, cx.b_kT_d)
            else:
                s.add("act", lambda e, ob=ob, ps=ps: e.activation(out=ob[:], in_=ps[:, :], func=AF.Silu),
                      reads=[b_p], writes=[b_ob])
                dst, b_dst = cx.gT_d, cx.b_gT_d
            s.add("pool", lambda e, ob=ob, dst=dst, h=h, mt=mt: e.dma_start(
                out=dst[h, :, mt * 512:(mt + 1) * 512], in_=ob[:]), reads=[b_ob], writes=[b_dst], dma=True)
        for sub in range(4):
            tt = mt * 4 + sub
            pb = 6 + (sub % 2)
            ps, b_p = cx.ps[pb], cx.b_ps[pb]
            for kc in range(8):
                s.add("pe", lambda e, kc=kc, sub=sub, ps=ps: e.matmul(
                    ps[:, :], lhsT=xT[:, kc, sub * 128:(sub + 1) * 128], rhs=win[:, kc, 1536:2048],
                    start=(kc == 0), stop=(kc == 7)), reads=[cx.b_win, b_xT], writes=[b_p])
            ob, b_ob = cx.ob[sub % 4], cx.b_ob[sub % 4]
            s.add("act", lambda e, ob=ob, ps=ps: e.copy(out=ob[:], in_=ps[:, :]), reads=[b_p], writes=[b_ob])
            s.add("pool", lambda e, ob=ob, tt=tt: e.dma_start(out=cx.v_d[tt * 128:(tt + 1) * 128, :], in_=ob[:]),
                  reads=[b_ob], writes=[cx.b_v_d], dma=True)
            fchain(tt, sub, xT, b_xT)
    emit.flush = flush
    return emit


def fox_attn(cx):
    s, S = cx.s, cx.S
    NQ = S // 512
    for h in range(4):
        s.add("sp", lambda e, h=h: e.dma_start(out=cx.kT_sb[:], in_=cx.kT_d[h]), reads=[cx.b_kT_d], writes=[cx.b_kT_sb], dma=True)
        s.add("sp", lambda e, h=h: e.dma_start(
            out=cx.v_sb[:], in_=cx.v_d[:, h * 128:(h + 1) * 128].rearrange("(kb p) d -> p kb d", p=128)),
            reads=[cx.b_v_d], writes=[cx.b_v_sb], dma=True)
        steps = [(T, kb) for T in range(NQ) for kb in range(4 * T + 4)]
        LOOK = 3

        def front(i, h=h):
            T, kb = steps[i]
            if kb == 0:
                q, b_q = cx.q_sb[T % 2], cx.b_q_sb[T % 2]
                g, b_g = cx.g_sb[T % 2], cx.b_g_sb[T % 2]
                cn, b_cn = cx.cnq[T % 2], cx.b_cnq[T % 2]
                s.add("sp", lambda e: e.dma_start(out=q[:], in_=cx.qT_d[h, :, T * 512:(T + 1) * 512]),
                      reads=[cx.b_qT_d], writes=[b_q], dma=True)
                s.add("sp", lambda e: e.dma_start(out=g[:], in_=cx.gT_d[h, :, T * 512:(T + 1) * 512]),
                      reads=[cx.b_gT_d], writes=[b_g], dma=True)
                s.add("sp", lambda e: e.dma_start(out=cn[:], in_=cx.crow_d[h:h + 1, T * 512:(T + 1) * 512].partition_broadcast(128)),
                      reads=[cx.b_crow_d], writes=[b_cn], dma=True)
            q, b_q = cx.q_sb[T % 2], cx.b_q_sb[T % 2]
            cn, b_cn = cx.cnq[T % 2], cx.b_cnq[T % 2]
            j = kb - 4 * T
            c0 = max(0, j) * 128
            sp_, b_sp = cx.ps[i % 4], cx.b_ps[i % 4]
            tt_, b_tt = cx.tt_sb[i % 4], cx.b_tt_sb[i % 4]
            p_, b_pp = cx.p_sb[i % 4], cx.b_p_sb[i % 4]
            s.add("pe", lambda e: e.matmul(sp_[:, c0:512], lhsT=cx.kT_sb[:, kb * 128:(kb + 1) * 128], rhs=q[:, c0:512],
                                           start=True, stop=True), reads=[cx.b_kT_sb, b_q], writes=[b_sp])
            s.add("dve", lambda e: e.scalar_tensor_tensor(
                out=tt_[:, c0:512], in0=sp_[:, c0:512], scalar=cx.cumcol[:, kb, h:h + 1], in1=cn[:, c0:512],
                op0=ALU.add, op1=ALU.subtract), reads=[b_sp, cx.b_cumcol, b_cn], writes=[b_tt])
            s.add("act", lambda e: e.activation(out=p_[:, c0:512], in_=tt_[:, c0:512], func=AF.Exp),
                  reads=[b_tt], writes=[b_pp])
            if j >= 0:
                s.add("pool", lambda e: e.tensor_tensor(out=p_[:, c0:c0 + 128], in0=p_[:, c0:c0 + 128], in1=cx.tri_b, op=ALU.mult),
                      reads=[b_pp, cx.b_cstb], writes=[b_pp])

        def back(i, h=h):
            T, kb = steps[i]
            j = kb - 4 * T
            c0 = max(0, j) * 128
            p_, b_pp = cx.p_sb[i % 4], cx.b_p_sb[i % 4]
            op_, b_op = cx.ps[4 + T % 2], cx.b_ps[4 + T % 2]
            lp_, b_lp = cx.ps[6 + T % 2], cx.b_ps[6 + T % 2]
            last = (kb == 4 * T + 3)
            s.add("pe", lambda e: e.matmul(op_[:, c0:512], lhsT=cx.v_sb[:, kb, :], rhs=p_[:, c0:512],
                                           start=(kb == 0), stop=last, skip_group_check=True), reads=[cx.b_v_sb, b_pp], writes=[b_op])
            s.add("pe", lambda e: e.matmul(lp_[:, c0:512], lhsT=cx.ones_b, rhs=p_[:, c0:512],
                                           start=(kb == 0), stop=last, skip_group_check=True), reads=[cx.b_cstb, b_pp], writes=[b_lp])
            if last:
                rl, b_rl = cx.rl[T % 2], cx.b_rl[T % 2]
                o1, b_o1 = cx.o1[T % 2], cx.b_o1[T % 2]
                go, b_go = cx.go_sb[T % 2], cx.b_go_sb[T % 2]
                g, b_g = cx.g_sb[T % 2], cx.b_g_sb[T % 2]
                s.add("act", lambda e: e.activation(out=rl[:], in_=lp_[:, :], func=AF.Ln), reads=[b_lp], writes=[b_rl])
                s.add("act", lambda e: e.activation(out=rl[:], in_=rl[:], func=AF.Exp, scale=-1.0), reads=[b_rl], writes=[b_rl])
                s.add("dve", lambda e: e.tensor_tensor(out=o1[:], in0=op_[:, :], in1=rl[:], op=ALU.mult),
                      reads=[b_op, b_rl], writes=[b_o1])
                s.add("pool", lambda e: e.tensor_tensor(out=go[:], in0=o1[:], in1=g[:], op=ALU.mult),
                      reads=[b_o1, b_g], writes=[b_go])
                jj, gc0 = (T * 512) // cx.CW, (T * 512) % cx.CW
                s.add("pool", lambda e: e.dma_start(out=cx.gohalf_d[jj][h * 128:(h + 1) * 128, gc0:gc0 + 512], in_=go[:]),
                      reads=[b_go], writes=[cx.b_gohalf_d[jj]], dma=True)

        n = len(steps)
        for i in range(n + LOOK):
            if i < n:
                front(i)
            if i >= LOOK:
                back(i - LOOK)


def common_setup(cx):
    nc, S = cx.nc, cx.S
    cx.wst = [sb(cx, f"wst{i}", [128, 516], F32) for i in range(2)]
    cx.b_wst = [Buf(f"wst{i}") for i in range(2)]
    cx.win = sb(cx, "win", [128, 8, 2064], BF16)
    cx.b_win = Buf("win")
    cx.wout = sb(cx, "wout", [128, 8, 1024], BF16)
    cx.b_wout = Buf("wout")
    cx.normw = sb(cx, "normw", [128, 32], F32)
    cx.b_normw = Buf("normw")
    cx.normw_d = nc.dram_tensor("normw_in", [128, 32], F32, kind="ExternalInput").ap()
    cx.s.add("sp", lambda e: e.dma_start(out=cx.normw[:], in_=cx.normw_d), writes=[cx.b_normw], dma=True)
    cx.xT = [sb(cx, f"xT{i}", [128, 8, 512], BF16) for i in range(2)]
    cx.b_xT = [Buf(f"xT{i}") for i in range(2)]
    cx.goT = [sb(cx, f"goT{i}", [128, 8, 512], BF16) for i in range(2)]
    cx.b_goT = [Buf(f"goT{i}") for i in range(2)]
    cx.xt = [sb(cx, f"xt{i}", [128, 1024], F32) for i in range(3)]
    cx.b_xt = [Buf(f"xt{i}") for i in range(3)]
    cx.junk = [sb(cx, f"junk{i}", [128, 1024], BF16) for i in range(2)]
    cx.b_junk = [Buf(f"junk{i}") for i in range(2)]
    cx.stat = [sb(cx, f"stat{i}", [128, 4], F32) for i in range(4)]
    cx.b_stat = [Buf(f"stat{i}") for i in range(4)]
    cx.xs = [sb(cx, f"xs{i}", [128, 1024], BF16) for i in range(4)]
    cx.b_xs = [Buf(f"xs{i}") for i in range(4)]
    cx.x_in = nc.dram_tensor("x", [S, 1024], F32, kind="ExternalInput").ap()
    cx.xres_d = nc.dram_tensor("xres_d", [S, 1024], F32, kind=getattr(cx, "xres_kind", "Internal")).ap()
    cx.b_xres_d = [Buf(f"xres{t}") for t in range(S // 128)]
    cx.CW = min(1024, S)
    cx.NCH = S // cx.CW
    cx.gofull_d = [nc.dram_tensor(f"gofull_d{j}", [1024, cx.CW], BF16, kind="Internal").ap() for j in range(cx.NCH)]
    cx.b_gofull_d = [Buf(f"gofull_d{j}") for j in range(cx.NCH)]


def mixer_common_setup(cx):
    cx.gomt = sb(cx, "gomt", [128, 4, 512], BF16)
    cx.b_gomt = Buf("gomt")
    cx.st_f = [sb(cx, f"st_f{i}", [128, 256], F32) for i in range(4)]
    cx.b_st_f = [Buf(f"st_f{i}") for i in range(4)]
    cx.st_b = [sb(cx, f"st_b{i}", [128, 256], BF16) for i in range(4)]
    cx.b_st_b = [Buf(f"st_b{i}") for i in range(4)]
    cx.mparam = sb(cx, "mparam", [64, 1024], F32)
    cx.b_mparam = Buf("mparam")

    def two(name, shape, dt):
        return [sb(cx, f"{name}{i}", shape, dt) for i in range(2)], [Buf(f"{name}{i}") for i in range(2)]
    cx.w_gz, cx.b_w_gz = two("w_gz", [64, 512], F32)
    cx.w_og, cx.b_w_og = two("w_og", [64, 512], BF16)


def gla_setup(cx):
    cx.cmraw = [sb(cx, f"cmraw{i}", [128, 512], F32) for i in range(4)]
    cx.b_cmraw = [Buf(f"cmraw{i}") for i in range(4)]
    cx.glT = sb(cx, "glT", [32, 512], F32)
    cx.b_glT = Buf("glT")
    cx.wup = sb(cx, "wup", [32, 256], F32)
    cx.b_wup = Buf("wup")

    def two(name, shape, dt):
        return [sb(cx, f"{name}{i}", shape, dt) for i in range(2)], [Buf(f"{name}{i}") for i in range(2)]
    cx.w_a, cx.b_w_a = two("w_a", [64, 512], F32)
    cx.w_b, cx.b_w_b = two("w_b", [64, 512], F32)
    cx.w_c, cx.b_w_c = two("w_c", [64, 512], F32)
    cx.w_v, cx.b_w_v = two("w_v", [64, 512], BF16)
    cx.w_kd, cx.b_w_kd = two("w_kd", [64, 512], BF16)
    cx.w_e1, cx.b_w_e1 = two("w_e1", [128, 256], F32)
    cx.w_e2, cx.b_w_e2 = two("w_e2", [128, 256], F32)
    cx.w_qd, cx.b_w_qd = two("w_qd", [128, 256], BF16)
    cx.w_ki, cx.b_w_ki = two("w_ki", [128, 256], BF16)
    cx.w_at, cx.b_w_at = two("w_at", [64, 256], BF16)
    cx.w_st, cx.b_w_st = two("w_st", [64, 16], F32)
    cx.w_nl, cx.b_w_nl = two("w_nl", [64, 256], BF16)
    cx.kb16 = [sb(cx, f"kb16_{i}", [128, 512], BF16) for i in range(2)]
    cx.b_kb16 = [Buf(f"kb16_{i}") for i in range(2)]
    cx.vcT = [sb(cx, f"vcT{i}", [128, 512], BF16) for i in range(4)]
    cx.b_vcT = [Buf(f"vcT{i}") for i in range(4)]
    cx.zcT = [sb(cx, f"zcT{i}", [128, 512], BF16) for i in range(4)]
    cx.b_zcT = [Buf(f"zcT{i}") for i in range(4)]
    cx.glTb = sb(cx, "glTb", [32, 512], BF16)
    cx.b_glTb = Buf("glTb")
    cx.wupb = sb(cx, "wupb", [32, 256], BF16)
    cx.b_wupb = Buf("wupb")


def gla_load_params(cx, wup_d, gain_d):
    s = cx.s
    s.add("dve", lambda e: e.memset(cx.glT[:], 1.0), writes=[cx.b_glT])
    s.add("sp", lambda e: e.dma_start(out=cx.wup[0:17, :], in_=wup_d), writes=[cx.b_wup], dma=True)
    s.add("dve", lambda e: e.tensor_copy(out=cx.wupb[0:17, :], in_=cx.wup[0:17, :]), reads=[cx.b_wup], writes=[cx.b_wupb])
    s.add("dve", lambda e: e.memset(cx.glTb[:], 1.0), writes=[cx.b_glTb])
    s.add("sp", lambda e: e.dma_start(out=cx.mparam[:, 0:512], in_=gain_d), writes=[cx.b_mparam], dma=True)
    for h in range(2):
        s.add("dve", lambda e, h=h: e.memset(cx.st_f[h][:], 0.0), writes=[cx.b_st_f[h]])
        s.add("dve", lambda e, h=h: e.memset(cx.st_b[h][:], 0.0), writes=[cx.b_st_b[h]])


def gla_proj(cx):
    s, win = cx.s, cx.win
    U = cx.cst[0:64, 128:192]
    Ub = cx.cstb[0:64, 128:192]
    ONES = cx.cst[0:64, 512:576]
    IDB = cx.cstb[0:64, 0:64]
    ps, bp = cx.ps, cx.b_ps

    def emit(mt, xT, b_xT):
        for g in range(4):
            pb = 2
            for kc in range(8):
                s.add("pe", lambda e, kc=kc, g=g: e.matmul(
                    ps[pb][:, :], lhsT=win[:, kc, g * 128:(g + 1) * 128], rhs=xT[:, kc, :], start=(kc == 0), stop=(kc == 7)),
                    reads=[cx.b_win, b_xT], writes=[bp[pb]])
            s.add("act", lambda e, g=g: e.copy(out=cx.cmraw[g][:], in_=ps[pb][:, :]), reads=[bp[pb]], writes=[cx.b_cmraw[g]])
            if g >= 2:
                s.add("act", lambda e, g=g: e.copy(out=cx.kb16[g - 2][:], in_=ps[pb][:, :]), reads=[bp[pb]], writes=[cx.b_kb16[g - 2]])
        for kc in range(8):
            s.add("pe", lambda e, kc=kc: e.matmul(
                ps[2][0:16, :], lhsT=win[:, kc, 512:528], rhs=xT[:, kc, :], start=(kc == 0), stop=(kc == 7)),
                reads=[cx.b_win, b_xT], writes=[bp[2]])
        s.add("act", lambda e: e.copy(out=cx.glTb[0:16, :], in_=ps[2][0:16, :]), reads=[bp[2]], writes=[cx.b_glTb])
        for g in range(8):
            pb = 2 + (g % 2)
            c0 = (528 if g < 4 else 1040) + (g % 4) * 128
            for kc in range(8):
                s.add("pe", lambda e, kc=kc, c0=c0, pb=pb: e.matmul(
                    ps[pb][:, :], lhsT=win[:, kc, c0:c0 + 128], rhs=xT[:, kc, :], start=(kc == 0), stop=(kc == 7)),
                    reads=[cx.b_win, b_xT], writes=[bp[pb]])
            if g < 4:
                s.add("act", lambda e, g=g, pb=pb: e.copy(out=cx.vcT[g][:], in_=ps[pb][:, :]), reads=[bp[pb]], writes=[cx.b_vcT[g]])
            else:
                s.add("act", lambda e, g=g, pb=pb: e.activation(out=cx.zcT[g - 4][:], in_=ps[pb][:, :], func=AF.Silu),
                      reads=[bp[pb]], writes=[cx.b_zcT[g - 4]])
        chunk(mt, 0, xT, b_xT, 'Y')
        for ci in range(8):
            if ci + 1 < 8:
                s.begin_record()
                chunk(mt, ci, xT, b_xT, 'X')
                lx = s.end_record()
                s.begin_record()
                chunk(mt, ci + 1, xT, b_xT, 'Y')
                ly = s.end_record()
                s.replay_zip(lx, ly)
            else:
                chunk(mt, ci, xT, b_xT, 'X')
        jj, c0 = (mt * 512) // cx.CW, (mt * 512) % cx.CW
        s.add("pool", lambda e: e.dma_start(out=cx.gohalf_d[jj][:, c0:c0 + 512].rearrange("(j p) t -> p j t", p=128),
                                            in_=cx.gomt[:]), reads=[cx.b_gomt], writes=[cx.b_gohalf_d[jj]], dma=True)

    def chunk(mt, ci, xT, b_xT, part):
        if True:
            c = mt * 8 + ci
            par = c % 2
            cs = slice(ci * 64, (ci + 1) * 64)
            wa, b_wa = cx.w_a[par], cx.b_w_a[par]
            wb, b_wb = cx.w_b[par], cx.b_w_b[par]
            wc, b_wc = cx.w_c[par], cx.b_w_c[par]
            wv, b_wv = cx.w_v[par], cx.b_w_v[par]
            gz, b_gz = cx.w_gz[par], cx.b_w_gz[par]
            kd, b_kd = cx.w_kd[par], cx.b_w_kd[par]
            e1, b_e1 = cx.w_e1[par], cx.b_w_e1[par]
            e2, b_e2 = cx.w_e2[par], cx.b_w_e2[par]
            qd, b_qd = cx.w_qd[par], cx.b_w_qd[par]
            ki, b_ki = cx.w_ki[par], cx.b_w_ki[par]
            at, b_at = cx.w_at[par], cx.b_w_at[par]
            og, b_og = cx.w_og[par], cx.b_w_og[par]
            st, b_st = cx.w_st[par], cx.b_w_st[par]
            if part == 'Y':
                tpv = ps[4].bitcast(BF16)
                tpz = ps[5].bitcast(BF16)
                for g in range(4):
                    s.add("pe", lambda e, g=g: e.transpose(tpv[0:64, g * 128:(g + 1) * 128], cx.vcT[g][:, cs], cx.ident_b),
                          reads=[cx.b_vcT[g], cx.b_cstb], writes=[bp[4]])
                for h in range(2):
                    s.add("pe", lambda e, h=h: e.transpose(tpv[0:64, 512 + h * 128:512 + (h + 1) * 128], cx.kb16[h][:, cs], cx.ident_b),
                          reads=[cx.b_kb16[h], cx.b_cstb], writes=[bp[4]])
                s.add("act", lambda e: e.copy(out=wv[:], in_=tpv[0:64, 0:512]), reads=[bp[4]], writes=[b_wv])
                for g in range(4):
                    s.add("pe", lambda e, g=g: e.transpose(tpz[0:64, g * 128:(g + 1) * 128], cx.zcT[g][:, cs], cx.ident_b),
                          reads=[cx.b_zcT[g], cx.b_cstb], writes=[bp[5]])
                s.add("dve", lambda e: e.tensor_tensor(out=gz[:], in0=tpz[0:64, 0:512], in1=cx.mparam[:, 0:512], op=ALU.mult),
                      reads=[bp[5], cx.b_mparam], writes=[b_gz])
                nl, b_nl = cx.w_nl[par], cx.b_w_nl[par]
                s.add("pe", lambda e: e.matmul(ps[6][0:64, 0:256], lhsT=cx.glTb[0:17, cs], rhs=cx.wupb[0:17, :], start=True, stop=True),
                      reads=[cx.b_glTb, cx.b_wupb], writes=[bp[6]])
                s.add("act", lambda e: e.activation(out=wa[:, 0:256], in_=ps[6][0:64, 0:256], func=AF.Exp, scale=-1.0),
                      reads=[bp[6]], writes=[b_wa])
                s.add("act", lambda e: e.activation(out=nl[:], in_=wa[:, 0:256], func=AF.Ln, bias=cx.epsq[0:64, 1:2]),
                      reads=[b_wa, cx.b_epsq], writes=[b_nl])
                s.add("pe", lambda e: e.matmul(ps[6][0:64, 0:256], lhsT=Ub, rhs=nl[:], start=True, stop=True),
                      reads=[b_nl, cx.b_cstb], writes=[bp[6]])
                s.add("pe", lambda e: e.matmul(ps[6][0:64, 256:512], lhsT=cx.cstb[0:64, 512:576], rhs=nl[:], start=True, stop=True),
                      reads=[b_nl, cx.b_cstb], writes=[bp[6]])
                for h in range(2):
                    s.add("pe", lambda e, h=h: e.matmul(ps[7][:, h * 64:(h + 1) * 64], lhsT=nl[:, h * 128:(h + 1) * 128], rhs=Ub,
                                                        start=True, stop=True), reads=[b_nl, cx.b_cstb], writes=[bp[7]])
                s.add("act", lambda e: e.copy(out=wb[:, 0:256], in_=ps[6][0:64, 0:256]), reads=[bp[6]], writes=[b_wb])
                s.add("dve", lambda e: e.tensor_tensor(out=wb[:, 256:512], in0=ps[6][0:64, 256:512], in1=wb[:, 0:256], op=ALU.subtract),
                      reads=[bp[6], b_wb], writes=[b_wb])
                s.add("act", lambda e: e.activation(out=wb[:, 256:512], in_=wb[:, 256:512], func=AF.Exp, scale=-1.0 / 16),
                      reads=[b_wb], writes=[b_wb])
                s.add("dve", lambda e: e.tensor_tensor(out=kd[:, 0:256], in0=ps[4].bitcast(BF16)[0:64, 512:768], in1=wb[:, 256:512], op=ALU.mult),
                      reads=[bp[4], b_wb], writes=[b_kd])
                s.add("act", lambda e: e.activation(out=e1[:, 0:128], in_=ps[7][:, 0:128], func=AF.Exp, scale=-1.0 / 16),
                      reads=[bp[7]], writes=[b_e1])
                s.add("act", lambda e: e.activation(out=e2[:, 0:128], in_=ps[7][:, 0:128], func=AF.Exp, scale=1.0 / 16),
                      reads=[bp[7]], writes=[b_e2])
                for h in range(2):
                    s.add("dve", lambda e, h=h: e.scalar_tensor_tensor(
                        out=qd[:, h * 64:(h + 1) * 64], in0=cx.cmraw[h][:, cs], scalar=float(128 ** -0.5),
                        in1=e1[:, h * 64:(h + 1) * 64], op0=ALU.mult, op1=ALU.mult),
                        reads=[cx.b_cmraw[h], b_e1], writes=[b_qd])
                    s.add("dve", lambda e, h=h: e.tensor_tensor(
                        out=ki[:, h * 64:(h + 1) * 64], in0=cx.cmraw[2 + h][:, cs], in1=e2[:, h * 64:(h + 1) * 64], op=ALU.mult),
                        reads=[cx.b_cmraw[2 + h], b_e2], writes=[b_ki])
                for h in range(2):
                    s.add("pe", lambda e, h=h: e.matmul(ps[7][0:64, 128 + h * 64:128 + (h + 1) * 64], lhsT=ki[:, h * 64:(h + 1) * 64],
                                                        rhs=qd[:, h * 64:(h + 1) * 64], start=True, stop=True),
                          reads=[b_ki, b_qd], writes=[bp[7]])
                for h in range(2):
                    s.add("dve", lambda e, h=h: e.tensor_tensor(out=at[:, h * 64:(h + 1) * 64], in0=ps[7][0:64, 128 + h * 64:128 + (h + 1) * 64],
                                                                in1=U, op=ALU.mult), reads=[bp[7], cx.b_cst], writes=[b_at])
                return
            for h in range(2):
                s.add("pe", lambda e, h=h: e.matmul(ps[3][0:64, h * 256:(h + 1) * 256], lhsT=at[:, h * 64:(h + 1) * 64],
                                                    rhs=wv[:, h * 256:(h + 1) * 256], start=True, stop=False),
                      reads=[b_at, b_wv], writes=[bp[3]])
                s.add("pe", lambda e, h=h: e.matmul(ps[3][0:64, h * 256:(h + 1) * 256], lhsT=qd[:, h * 64:(h + 1) * 64],
                                                    rhs=cx.st_b[h][:], start=False, stop=True),
                      reads=[b_qd, cx.b_st_b[h]], writes=[bp[3]])
            for h in range(2):
                s.add("pe", lambda e, h=h: e.matmul(ps[1][:, h * 256:(h + 1) * 256], lhsT=kd[:, h * 128:(h + 1) * 128],
                                                    rhs=wv[:, h * 256:(h + 1) * 256], start=True, stop=True),
                      reads=[b_kd, b_wv], writes=[bp[1]])
            for h in range(2):
                s.add("dve", lambda e, h=h: e.scalar_tensor_tensor(
                    out=cx.st_f[h][:], in0=cx.st_f[h][:], scalar=e1[:, h * 64 + 63:h * 64 + 64], in1=ps[1][:, h * 256:(h + 1) * 256],
                    op0=ALU.mult, op1=ALU.add), reads=[cx.b_st_f[h], b_e1, bp[1]], writes=[cx.b_st_f[h]])
                s.add("act", lambda e, h=h: e.copy(out=cx.st_b[h][:], in_=cx.st_f[h][:]), reads=[cx.b_st_f[h]], writes=[cx.b_st_b[h]])
            for h in range(2):
                s.add("act", lambda e, h=h: e.activation(out=wc[:, h * 256:(h + 1) * 256], in_=ps[3][0:64, h * 256:(h + 1) * 256],
                                                         func=AF.Square, accum_out=st[:, h:h + 1]), reads=[bp[3]], writes=[b_wc, b_st])
            s.add("act", lambda e: e.activation(out=st[:, 2:4], in_=st[:, 0:2], func=AF.Ln, scale=1.0 / 256, bias=cx.epsq[0:64, 3:4]),
                  reads=[b_st, cx.b_epsq], writes=[b_st])
            s.add("act", lambda e: e.activation(out=st[:, 4:6], in_=st[:, 2:4], func=AF.Exp, scale=-0.5), reads=[b_st], writes=[b_st])
            for h in range(2):
                s.add("dve", lambda e, h=h: e.scalar_tensor_tensor(
                    out=og[:, h * 256:(h + 1) * 256], in0=ps[3][0:64, h * 256:(h + 1) * 256], scalar=st[:, 4 + h:5 + h],
                    in1=gz[:, h * 256:(h + 1) * 256], op0=ALU.mult, op1=ALU.mult), reads=[bp[3], b_st, b_gz], writes=[b_og])
            tp = cx.ps[0].bitcast(BF16)
            for j in range(4):
                s.add("pe", lambda e, j=j: e.transpose(tp[:, j * 64:(j + 1) * 64], og[:, j * 128:(j + 1) * 128], IDB),
                      reads=[b_og, cx.b_cstb], writes=[bp[0]])
            s.add("act", lambda e: e.copy(out=cx.gomt[:, :, cs], in_=tp[:, 0:256].rearrange("p (j t) -> p j t", j=4)),
                  reads=[bp[0]], writes=[cx.b_gomt])
    return emit


def gdn_setup(cx):
    def mk(name, shape, dt, n):
        return [sb(cx, f"{name}{i}", shape, dt) for i in range(n)], [Buf(f"{name}{i}") for i in range(n)]
    cx.ub, cx.b_ub = mk("ub", [128, 516], F32, 2)
    cx.uh, cx.b_uh = mk("uh", [128, 4], F32, 8)
    cx.qkn, cx.b_qkn = mk("qkn", [128, 512], BF16, 4)
    cx.vc, cx.b_vc = mk("vc", [128, 512], BF16, 4)
    cx.cacc, cx.b_cacc = mk("cacc", [128, 512], F32, 2)
    cx.grs, cx.b_grs = mk("grs", [128, 512], F32, 2)
    cx.zc, cx.b_zc = mk("zc", [128, 512], BF16, 4)
    cx.gp = sb(cx, "gp", [128, 32], F32)
    cx.b_gp = Buf("gp")
    cx.nP, cx.b_nP = mk("nP", [64, 512], BF16, 2)
    cx.nQ, cx.b_nQ = mk("nQ", [64, 512], BF16, 2)
    cx.rsh, cx.b_rsh = mk("rsh", [64, 512], BF16, 1)
    cx.nR, cx.b_nR = mk("nR", [64, 512], F32, 1)
    cx.rbf, cx.b_rbf = mk("rbf", [64, 512], BF16, 2)
    cx.gb, cx.b_gb = mk("gb", [64, 256], F32, 1)
    cx.dA, cx.b_dA = mk("dA", [64, 256], F32, 1)
    cx.dT, cx.b_dT = mk("dT", [64, 256], F32, 1)
    cx.aqt, cx.b_aqt = mk("aqt", [64, 512], BF16, 2)
    cx.sm, cx.b_sm = mk("sm", [64, 64], F32, 4)
    cx.sm2, cx.b_sm2 = mk("sm2", [128, 8], F32, 4)
    cx.vb, cx.b_vb = mk("vb", [64, 512], BF16, 1)
    cx.kbe, cx.b_kbe = mk("kbe", [64, 512], BF16, 1)
    cx.kdg, cx.b_kdg = mk("kdg", [64, 512], BF16, 1)
    cx.wtn, cx.b_wtn = mk("wtn", [128, 256], BF16, 1)
    cx.vnew, cx.b_vnew = mk("vnew", [64, 512], BF16, 1)
    cx.oi, cx.b_oi = mk("oi", [64, 512], F32, 1)
    cx.oo, cx.b_oo = mk("oo", [64, 512], F32, 1)
    cx.msk = sb(cx, "msk", [64, 1024], F32)
    cx.b_msk = Buf("msk")


def gdn_load_params(cx, gp_d, mp_d):
    s = cx.s
    s.add("sp", lambda e: e.dma_start(out=cx.gp[:], in_=gp_d), writes=[cx.b_gp], dma=True)
    s.add("sp", lambda e: e.dma_start(out=cx.mparam[:], in_=mp_d), writes=[cx.b_mparam], dma=True)
    s.add("act", lambda e: e.activation(out=cx.mparam[:, 516:520], in_=cx.mparam[:, 516:520], func=AF.Exp),
          reads=[cx.b_mparam], writes=[cx.b_mparam])
    for h in range(4):
        s.add("dve", lambda e, h=h: e.memset(cx.st_f[h][:], 0.0), writes=[cx.b_st_f[h]])
        s.add("dve", lambda e, h=h: e.memset(cx.st_b[h][:], 0.0), writes=[cx.b_st_b[h]])
    for g in range(8):
        s.add("dve", lambda e, g=g: e.memset(cx.uh[g][:], 0.0), writes=[cx.b_uh[g]])
    for h in range(4, 8):
        s.add("dve", lambda e, h=h: e.tensor_copy(out=cx.msk[:, 512 + h * 64:512 + (h + 1) * 64], in_=cx.cst[0:64, 0:64]),
              reads=[cx.b_cst], writes=[cx.b_msk])
    for h in range(4):
        s.add("dve", lambda e, h=h: e.tensor_copy(out=cx.msk[:, h * 64:(h + 1) * 64], in_=cx.cst[0:64, 768:832]),
              reads=[cx.b_cst], writes=[cx.b_msk])
        s.add("dve", lambda e, h=h: e.tensor_copy(out=cx.msk[:, 256 + h * 64:256 + (h + 1) * 64], in_=cx.cst[0:64, 128:192]),
              reads=[cx.b_cst], writes=[cx.b_msk])
        s.add("dve", lambda e, h=h: e.tensor_copy(out=cx.msk[:, 512 + h * 64:512 + (h + 1) * 64], in_=cx.cst[0:64, 0:64]),
              reads=[cx.b_cst], writes=[cx.b_msk])


GDN_STOP = 99


def gdn_proj(cx):
    s, win = cx.s, cx.win
    U = cx.cst[0:64, 128:192]
    ONES = cx.cst[0:64, 512:640]
    IDF = cx.cst[0:64, 0:64]
    IDB = cx.cstb[0:64, 0:64]
    ps, bp = cx.ps, cx.b_ps
    LS4, UI4, ID8 = cx.msk[:, 0:256], cx.msk[:, 256:512], cx.msk[:, 512:1024]

    def prep_group(g, xT, b_xT, mt):
        ub, b_ub = cx.ub[g % 2], cx.b_ub[g % 2]
        uh, b_uh = cx.uh[g], cx.b_uh[g]
        acc, b_acc = cx.cacc[g % 2], cx.b_cacc[g % 2]
        pb = 2
        s.add("dve", lambda e: e.tensor_copy(out=ub[:, 1:4], in_=uh[:, 1:4]), reads=[b_uh], writes=[b_ub])
        for kc in range(8):
            s.add("pe", lambda e, kc=kc: e.matmul(ps[pb][:, :], lhsT=win[:, kc, g * 128:(g + 1) * 128], rhs=xT[:, kc, :],
                                                  start=(kc == 0), stop=(kc == 7)), reads=[cx.b_win, b_xT], writes=[bp[pb]])
        s.add("act", lambda e: e.copy(out=ub[:, 4:516], in_=ps[pb][:, :]), reads=[bp[pb]], writes=[b_ub])
        s.add("act", lambda e: e.copy(out=uh[:, 1:4], in_=ps[pb][:, 509:512]), reads=[bp[pb]], writes=[b_uh])
        s.add("dve", lambda e: e.tensor_scalar(out=acc[:], in0=ub[:, 4:516], scalar1=cx.gp[:, g * 4 + 3:g * 4 + 4], scalar2=None,
                                               op0=ALU.mult), reads=[b_ub, cx.b_gp], writes=[b_acc])
        for j in range(3):
            s.add("dve", lambda e, j=j: e.scalar_tensor_tensor(
                out=acc[:], in0=ub[:, 1 + j:513 + j], scalar=cx.gp[:, g * 4 + j:g * 4 + j + 1], in1=acc[:],
                op0=ALU.mult, op1=ALU.add), reads=[b_ub, cx.b_gp, b_acc], writes=[b_acc])
        if g >= 4:
            vc, b_vc = cx.vc[g - 4], cx.b_vc[g - 4]
            s.add("act", lambda e: e.activation(out=vc[:], in_=acc[:], func=AF.Silu), reads=[b_acc], writes=[b_vc])
            return
        s.add("act", lambda e: e.activation(out=acc[:], in_=acc[:], func=AF.Silu), reads=[b_acc], writes=[b_acc])
        sq, b_sq = cx.junk[0][:, 0:512], cx.b_junk[0]
        rs, b_rs = cx.grs[g % 2], cx.b_grs[g % 2]
        s.add("act", lambda e: e.activation(out=sq, in_=acc[:], func=AF.Square), reads=[b_acc], writes=[b_sq])
        s.add("pe", lambda e: e.matmul(ps[3][:, :], lhsT=cx.ones_b, rhs=sq, start=True, stop=True),
              reads=[b_sq, cx.b_cstb], writes=[bp[3]])
        s.add("act", lambda e: e.activation(out=rs[:], in_=ps[3][:, :], func=AF.Ln, bias=cx.epsq[:, 3:4]),
              reads=[bp[3], cx.b_epsq], writes=[b_rs])
        s.add("act", lambda e: e.activation(out=rs[:], in_=rs[:], func=AF.Exp, scale=-0.5), reads=[b_rs], writes=[b_rs])
        qk, b_qk = cx.qkn[g], cx.b_qkn[g]
        sc = float(128 ** -0.5) if g < 2 else 1.0
        s.add("dve", lambda e: e.scalar_tensor_tensor(out=qk[:], in0=acc[:], scalar=sc, in1=rs[:], op0=ALU.mult, op1=ALU.mult),
              reads=[b_acc, b_rs], writes=[b_qk])

    def emit(mt, xT, b_xT):
        for g in range(8):
            prep_group(g, xT, b_xT, mt)
        for g in range(4):
            for kc in range(8):
                s.add("pe", lambda e, kc=kc, g=g: e.matmul(ps[2][:, :], lhsT=win[:, kc, 1024 + g * 128:1024 + (g + 1) * 128], rhs=xT[:, kc, :],
                                                           start=(kc == 0), stop=(kc == 7)), reads=[cx.b_win, b_xT], writes=[bp[2]])
            s.add("act", lambda e, g=g: e.activation(out=cx.zc[g][:], in_=ps[2][:, :], func=AF.Silu), reads=[bp[2]], writes=[cx.b_zc[g]])
        def Y(pr):
            pre(mt, 2 * pr, xT, b_xT)
            pre(mt, 2 * pr + 1, xT, b_xT)
            neu(mt, 2 * pr, xT, b_xT)

        def X(pr):
            post(mt, 2 * pr, xT, b_xT)
            post(mt, 2 * pr + 1, xT, b_xT)

        Y(0)
        for pr in range(4):
            if pr + 1 < 4:
                s.begin_record()
                Y(pr + 1)
                ly = s.end_record()
                s.begin_record()
                X(pr)
                lx = s.end_record()
                s.replay_zip(lx, ly)
            else:
                X(pr)
        jj, c0 = (mt * 512) // cx.CW, (mt * 512) % cx.CW
        s.add("pool", lambda e: e.dma_start(out=cx.gohalf_d[jj][:, c0:c0 + 512].rearrange("(j p) t -> p j t", p=128),
                                            in_=cx.gomt[:]), reads=[cx.b_gomt], writes=[cx.b_gohalf_d[jj]], dma=True)

    def pre(mt, ci, xT, b_xT):
        c = mt * 8 + ci
        par = c % 2
        pp = (c // 2) % 2
        cs = slice(ci * 64, (ci + 1) * 64)
        sm, b_sm = cx.sm[c % 4], cx.b_sm[c % 4]
        sm2, b_sm2 = cx.sm2[c % 4], cx.b_sm2[c % 4]
        gz, b_gz = cx.w_gz[par], cx.b_w_gz[par]
        og, b_og = cx.w_og[par], cx.b_w_og[par]
        nR, b_nR = cx.nR[0], cx.b_nR[0]
        rbf, b_rbf = cx.rbf[pp], cx.b_rbf[pp]
        gb, b_gb = cx.gb[0], cx.b_gb[0]
        dA, b_dA = cx.dA[0], cx.b_dA[0]
        dT, b_dT = cx.dT[0], cx.b_dT[0]
        aqt, b_aqt = cx.aqt[pp], cx.b_aqt[pp]
        vb, b_vb = cx.vb[0], cx.b_vb[0]
        kbe, b_kbe = cx.kbe[0], cx.b_kbe[0]
        kdg, b_kdg = cx.kdg[0], cx.b_kdg[0]
        wtn, b_wtn = cx.wtn[0], cx.b_wtn[0]
        vnew, b_vnew = cx.vnew[0], cx.b_vnew[0]
        oi, b_oi = cx.oi[0], cx.b_oi[0]
        oo, b_oo = cx.oo[0], cx.b_oo[0]

        def hs(h):
            return slice(h * 64, (h + 1) * 64)

        def hv(h):
            return slice(h * 128, (h + 1) * 128)

        def hb(h):
            return slice((ci % 2) * 256 + h * 64, (ci % 2) * 256 + (h + 1) * 64)

        if GDN_STOP < 1:
            return
        if GDN_STOP < 2:
            return
        for kc in range(8):
            s.add("pe", lambda e, kc=kc: e.matmul(ps[2][0:64, 0:8], lhsT=xT[:, kc, cs], rhs=win[:, kc, 1536:1544],
                                                  start=(kc == 0), stop=(kc == 7)), reads=[cx.b_win, b_xT], writes=[bp[2]])
        s.add("dve", lambda e: e.tensor_tensor(out=sm[:, 0:4], in0=ps[2][0:64, 0:4], in1=cx.mparam[:, 512:516], op=ALU.add),
              reads=[bp[2], cx.b_mparam], writes=[b_sm])
        s.add("act", lambda e: e.activation(out=sm[:, 8:12], in_=ps[2][0:64, 4:8], func=AF.Exp, scale=-1.0), reads=[bp[2]], writes=[b_sm])
        s.add("act", lambda e: e.activation(out=sm[:, 0:4], in_=sm[:, 0:4], func=AF.Exp), reads=[b_sm], writes=[b_sm])
        s.add("act", lambda e: e.activation(out=sm[:, 0:4], in_=sm[:, 0:4], func=AF.Ln, bias=cx.epsq[0:64, 1:2]),
              reads=[b_sm, cx.b_epsq], writes=[b_sm])
        s.add("dve", lambda e: e.tensor_tensor(out=sm[:, 4:8], in0=sm[:, 0:4], in1=cx.mparam[:, 516:520], op=ALU.mult),
              reads=[b_sm, cx.b_mparam], writes=[b_sm])
        s.add("dve", lambda e: e.tensor_scalar(out=sm[:, 8:12], in0=sm[:, 8:12], scalar1=1.0, scalar2=None, op0=ALU.add),
              reads=[b_sm], writes=[b_sm])
        s.add("dve", lambda e: e.reciprocal(out=sm[:, 8:12], in_=sm[:, 8:12]), reads=[b_sm], writes=[b_sm])
        if GDN_STOP < 3:
            return
        s.add("pe", lambda e: e.matmul(ps[6][0:64, 0:4], lhsT=U, rhs=sm[:, 4:8], start=True, stop=True),
              reads=[b_sm, cx.b_cst], writes=[bp[6]])
        s.add("pe", lambda e: e.matmul(ps[6][:, 8:12], lhsT=ONES, rhs=sm[:, 4:8], start=True, stop=True),
              reads=[b_sm, cx.b_cst], writes=[bp[6]])
        s.add("dve", lambda e: e.tensor_copy(out=sm[:, 12:16], in_=ps[6][0:64, 0:4]), reads=[bp[6]], writes=[b_sm])
        s.add("act", lambda e: e.activation(out=sm[:, 16:20], in_=ps[6][0:64, 0:4], func=AF.Exp, scale=-1.0), reads=[bp[6]], writes=[b_sm])
        s.add("dve", lambda e: e.tensor_tensor(out=sm[:, 20:24], in0=ps[6][0:64, 8:12], in1=sm[:, 12:16], op=ALU.subtract),
              reads=[bp[6], b_sm], writes=[b_sm])
        s.add("act", lambda e: e.activation(out=sm[:, 20:24], in_=sm[:, 20:24], func=AF.Exp, scale=-1.0), reads=[b_sm], writes=[b_sm])
        s.add("act", lambda e: e.activation(out=sm2[:, 4:8], in_=ps[6][:, 8:12], func=AF.Exp, scale=-1.0), reads=[bp[6]], writes=[b_sm2])
        s.add("dve", lambda e: e.tensor_tensor(out=sm[:, 24:28], in0=sm[:, 8:12], in1=sm[:, 16:20], op=ALU.mult),
              reads=[b_sm], writes=[b_sm])
        if GDN_STOP < 4:
            return
        for h in range(4):
            s.add("dve", lambda e, h=h: e.tensor_copy(out=gb[:, hs(h)], in_=sm[:, 4 + h:5 + h].to_broadcast([64, 64])),
                  reads=[b_sm], writes=[b_gb])
        for h in range(4):
            s.add("pe", lambda e, h=h: e.matmul(ps[7][0:64, hs(h)], lhsT=gb[:, hs(h)], rhs=U, start=True, stop=True),
                  reads=[b_gb, cx.b_cst], writes=[bp[7]])
        for h in range(4):
            s.add("dve", lambda e, h=h: e.tensor_scalar(out=dA[:, hs(h)], in0=ps[7][0:64, hs(h)], scalar1=sm[:, 12 + h:13 + h], scalar2=0.0,
                                                        op0=ALU.subtract, op1=ALU.min), reads=[bp[7], b_sm], writes=[b_dA])
            s.add("dve", lambda e, h=h: e.tensor_scalar(out=dT[:, hs(h)], in0=ps[7][0:64, hs(h)], scalar1=sm[:, 12 + h:13 + h], scalar2=0.0,
                                                        op0=ALU.subtract, op1=ALU.max), reads=[bp[7], b_sm], writes=[b_dT])
        s.add("act", lambda e: e.activation(out=dA[:], in_=dA[:], func=AF.Exp), reads=[b_dA], writes=[b_dA])
        s.add("act", lambda e: e.activation(out=dT[:], in_=dT[:], func=AF.Exp, scale=-1.0), reads=[b_dT], writes=[b_dT])
        s.add("pool", lambda e: e.tensor_tensor(out=dA[:], in0=dA[:], in1=LS4, op=ALU.mult), reads=[b_dA, cx.b_msk], writes=[b_dA])
        s.add("pool", lambda e: e.tensor_tensor(out=dT[:], in0=dT[:], in1=UI4, op=ALU.mult), reads=[b_dT, cx.b_msk], writes=[b_dT])
        if GDN_STOP < 5:
            return
        for qh in range(2):
            kn, b_kn = cx.qkn[2 + qh], cx.b_qkn[2 + qh]
            qn, b_qn = cx.qkn[qh], cx.b_qkn[qh]
            s.add("pe", lambda e, qh=qh, kn=kn: e.matmul(ps[2][0:64, hs(qh)], lhsT=kn[:, cs], rhs=kn[:, cs], start=True, stop=True),
                  reads=[b_kn], writes=[bp[2]])
            s.add("pe", lambda e, qh=qh, kn=kn, qn=qn: e.matmul(ps[2][0:64, 128 + qh * 64:192 + qh * 64], lhsT=kn[:, cs], rhs=qn[:, cs],
                                                                start=True, stop=True), reads=[b_kn, b_qn], writes=[bp[2]])
        P0, b_P0 = cx.nP[0], cx.b_nP[0]
        for h in range(4):
            s.add("dve", lambda e, h=h: e.scalar_tensor_tensor(out=P0[:, hb(h)], in0=ps[2][0:64, hs(h // 2)], scalar=sm[:, 8 + h:9 + h],
                                                               in1=dA[:, hs(h)], op0=ALU.mult, op1=ALU.mult),
                  reads=[bp[2], b_sm, b_dA], writes=[b_P0])
            s.add("dve", lambda e, h=h: e.tensor_tensor(out=aqt[:, hb(h)], in0=ps[2][0:64, 128 + (h // 2) * 64:192 + (h // 2) * 64],
                                                        in1=dT[:, hs(h)], op=ALU.mult), reads=[bp[2], b_dT], writes=[b_aqt])

    def neu(mt, ci, xT, b_xT):
        P0, b_P0 = cx.nP[0], cx.b_nP[0]
        c = mt * 8 + ci
        par = c % 2
        pp = (c // 2) % 2
        cs = slice(ci * 64, (ci + 1) * 64)
        sm, b_sm = cx.sm[c % 4], cx.b_sm[c % 4]
        sm2, b_sm2 = cx.sm2[c % 4], cx.b_sm2[c % 4]
        gz, b_gz = cx.w_gz[par], cx.b_w_gz[par]
        og, b_og = cx.w_og[par], cx.b_w_og[par]
        nR, b_nR = cx.nR[0], cx.b_nR[0]
        rbf, b_rbf = cx.rbf[pp], cx.b_rbf[pp]
        gb, b_gb = cx.gb[0], cx.b_gb[0]
        dA, b_dA = cx.dA[0], cx.b_dA[0]
        dT, b_dT = cx.dT[0], cx.b_dT[0]
        aqt, b_aqt = cx.aqt[pp], cx.b_aqt[pp]
        vb, b_vb = cx.vb[0], cx.b_vb[0]
        kbe, b_kbe = cx.kbe[0], cx.b_kbe[0]
        kdg, b_kdg = cx.kdg[0], cx.b_kdg[0]
        wtn, b_wtn = cx.wtn[0], cx.b_wtn[0]
        vnew, b_vnew = cx.vnew[0], cx.b_vnew[0]
        oi, b_oi = cx.oi[0], cx.b_oi[0]
        oo, b_oo = cx.oo[0], cx.b_oo[0]

        def hs(h):
            return slice(h * 64, (h + 1) * 64)

        def hv(h):
            return slice(h * 128, (h + 1) * 128)

        def hb(h):
            return slice((ci % 2) * 256 + h * 64, (ci % 2) * 256 + (h + 1) * 64)

        if GDN_STOP < 1:
            return
        if GDN_STOP < 6:
            return
        Q0, b_Q0 = cx.nQ[0], cx.b_nQ[0]
        rsh, b_rsh = cx.rsh[0], cx.b_rsh[0]
        for h in range(8):
            s.add("pe", lambda e, h=h: e.matmul(ps[7][0:64, hs(h)], lhsT=P0[:, hs(h)], rhs=IDB, start=True, stop=True),
                  reads=[b_P0, cx.b_cstb], writes=[bp[7]])
        s.add("act", lambda e: e.copy(out=Q0[:], in_=ps[7][0:64, 0:512]), reads=[bp[7]], writes=[b_Q0])
        s.add("dve", lambda e: e.scalar_tensor_tensor(out=rsh[:], in0=ps[7][0:64, 0:512], scalar=-1.0, in1=ID8, op0=ALU.mult, op1=ALU.add),
              reads=[cx.b_msk, bp[7]], writes=[b_rsh])
        s.add("dve", lambda e: e.scalar_tensor_tensor(out=nR[:], in0=ps[7][0:64, 0:512], scalar=-1.0, in1=ID8, op0=ALU.mult, op1=ALU.add),
              reads=[cx.b_msk, bp[7]], writes=[b_nR])
        a = 0
        for lvl in range(1, 6):
            P, b_P, Q, b_Q = cx.nP[a], cx.b_nP[a], cx.nQ[a], cx.b_nQ[a]
            Pn, b_Pn, Qn, b_Qn = cx.nP[1 - a], cx.b_nP[1 - a], cx.nQ[1 - a], cx.b_nQ[1 - a]
            for h in range(8):
                s.add("pe", lambda e, h=h, P=P, Q=Q: e.matmul(ps[6][0:64, hs(h)], lhsT=Q[:, hs(h)], rhs=P[:, hs(h)], start=True, stop=True),
                      reads=[b_P, b_Q], writes=[bp[6]])
            s.add("act", lambda e, Pn=Pn: e.copy(out=Pn[:], in_=ps[6][0:64, 0:512]), reads=[bp[6]], writes=[b_Pn])
            if lvl < 5:
                for h in range(8):
                    s.add("pe", lambda e, h=h, P=P, Q=Q: e.matmul(ps[7][0:64, hs(h)], lhsT=P[:, hs(h)], rhs=Q[:, hs(h)], start=True, stop=True),
                          reads=[b_P, b_Q], writes=[bp[7]])
                s.add("dve", lambda e, Qn=Qn: e.tensor_copy(out=Qn[:], in_=ps[7][0:64, 0:512]), reads=[bp[7]], writes=[b_Qn])
            for h in range(8):
                s.add("pe", lambda e, h=h, Pn=Pn: e.matmul(ps[2][0:64, hs(h)], lhsT=Pn[:, hs(h)], rhs=rsh[:, hs(h)], start=True, stop=True),
                      reads=[b_Pn, b_rsh], writes=[bp[2]])
            if lvl < 5:
                s.add("dve", lambda e: e.tensor_tensor(out=rsh[:], in0=ps[2][0:64, 0:512], in1=nR[:], op=ALU.add),
                      reads=[b_nR, bp[2]], writes=[b_rsh])
                s.add("dve", lambda e: e.tensor_tensor(out=nR[:], in0=ps[2][0:64, 0:512], in1=nR[:], op=ALU.add),
                      reads=[b_nR, bp[2]], writes=[b_nR])
            else:
                s.add("dve", lambda e: e.tensor_tensor(out=rbf[:], in0=ps[2][0:64, 0:512], in1=nR[:], op=ALU.add),
                      reads=[b_nR, bp[2]], writes=[b_rbf])
            a = 1 - a

    def post(mt, ci, xT, b_xT):
        c = mt * 8 + ci
        par = c % 2
        pp = (c // 2) % 2
        cs = slice(ci * 64, (ci + 1) * 64)
        sm, b_sm = cx.sm[c % 4], cx.b_sm[c % 4]
        sm2, b_sm2 = cx.sm2[c % 4], cx.b_sm2[c % 4]
        gz, b_gz = cx.w_gz[par], cx.b_w_gz[par]
        og, b_og = cx.w_og[par], cx.b_w_og[par]
        nR, b_nR = cx.nR[0], cx.b_nR[0]
        rbf, b_rbf = cx.rbf[pp], cx.b_rbf[pp]
        gb, b_gb = cx.gb[0], cx.b_gb[0]
        dA, b_dA = cx.dA[0], cx.b_dA[0]
        dT, b_dT = cx.dT[0], cx.b_dT[0]
        aqt, b_aqt = cx.aqt[pp], cx.b_aqt[pp]
        vb, b_vb = cx.vb[0], cx.b_vb[0]
        kbe, b_kbe = cx.kbe[0], cx.b_kbe[0]
        kdg, b_kdg = cx.kdg[0], cx.b_kdg[0]
        wtn, b_wtn = cx.wtn[0], cx.b_wtn[0]
        vnew, b_vnew = cx.vnew[0], cx.b_vnew[0]
        oi, b_oi = cx.oi[0], cx.b_oi[0]
        oo, b_oo = cx.oo[0], cx.b_oo[0]

        def hs(h):
            return slice(h * 64, (h + 1) * 64)

        def hv(h):
            return slice(h * 128, (h + 1) * 128)

        def hb(h):
            return slice((ci % 2) * 256 + h * 64, (ci % 2) * 256 + (h + 1) * 64)

        if GDN_STOP < 1:
            return
        tpz = ps[0].bitcast(BF16)
        for g in range(4):
            s.add("pe", lambda e, g=g: e.transpose(tpz[0:64, g * 128:(g + 1) * 128], cx.zc[g][:, cs], cx.ident_b),
                  reads=[cx.b_zc[g], cx.b_cstb], writes=[bp[0]])
        s.add("dve", lambda e: e.tensor_tensor(out=gz[:], in0=tpz[0:64, 0:512], in1=cx.mparam[:, 0:512], op=ALU.mult),
              reads=[bp[0], cx.b_mparam], writes=[b_gz])
        if GDN_STOP < 8:
            return
        tpb = ps[0].bitcast(BF16)
        for g in range(6):
            src_, b_src = (cx.qkn[2 + g], cx.b_qkn[2 + g]) if g < 2 else (cx.vc[g - 2], cx.b_vc[g - 2])
            s.add("pe", lambda e, g=g, src_=src_: e.transpose(tpb[0:64, g * 128:(g + 1) * 128], src_[:, cs], cx.ident_b),
                  reads=[b_src, cx.b_cstb], writes=[bp[0]])
        for h in range(4):
            s.add("dve", lambda e, h=h: e.tensor_scalar(out=vb[:, hv(h)], in0=tpb[0:64, (2 + h) * 128:(3 + h) * 128], scalar1=sm[:, 8 + h:9 + h],
                                                        scalar2=None, op0=ALU.mult), reads=[bp[0], b_sm], writes=[b_vb])
            s.add("dve", lambda e, h=h: e.tensor_scalar(out=kbe[:, hv(h)], in0=tpb[0:64, (h // 2) * 128:(h // 2 + 1) * 128],
                                                        scalar1=sm[:, 24 + h:25 + h], scalar2=None, op0=ALU.mult),
                  reads=[bp[0], b_sm], writes=[b_kbe])
            s.add("dve", lambda e, h=h: e.tensor_scalar(out=kdg[:, hv(h)], in0=tpb[0:64, (h // 2) * 128:(h // 2 + 1) * 128],
                                                        scalar1=sm[:, 20 + h:21 + h], scalar2=None, op0=ALU.mult),
                  reads=[bp[0], b_sm], writes=[b_kdg])
        if GDN_STOP < 9:
            return
        for h in range(4):
            s.add("pe", lambda e, h=h: e.matmul(ps[1][:, hs(h)], lhsT=kbe[:, hv(h)], rhs=rbf[:, hb(h)], start=True, stop=True),
                  reads=[b_kbe, b_rbf], writes=[bp[1]])
        s.add("act", lambda e: e.mul(out=wtn[:], in_=ps[1][:, 0:256], mul=-1.0), reads=[bp[1]], writes=[b_wtn])
        for h in range(4):
            s.add("pe", lambda e, h=h: e.matmul(ps[4][0:64, hv(h)], lhsT=rbf[:, hb(h)], rhs=vb[:, hv(h)], start=True, stop=False),
                  reads=[b_rbf, b_vb], writes=[bp[4]])
            s.add("pe", lambda e, h=h: e.matmul(ps[4][0:64, hv(h)], lhsT=wtn[:, hs(h)], rhs=cx.st_b[h][:, 0:128], start=False, stop=True),
                  reads=[b_wtn, cx.b_st_b[h]], writes=[bp[4]])
        s.add("act", lambda e: e.copy(out=vnew[:], in_=ps[4][0:64, :]), reads=[bp[4]], writes=[b_vnew])
        for h in range(4):
            qn, b_qn = cx.qkn[h // 2], cx.b_qkn[h // 2]
            s.add("pe", lambda e, h=h, qn=qn: e.matmul(ps[5][0:64, hv(h)], lhsT=qn[:, cs], rhs=cx.st_b[h][:, 0:128], start=True, stop=True),
                  reads=[b_qn, cx.b_st_b[h]], writes=[bp[5]])
        for h in range(4):
            s.add("pe", lambda e, h=h: e.matmul(ps[3][0:64, hv(h)], lhsT=aqt[:, hb(h)], rhs=vnew[:, hv(h)], start=True, stop=True),
                  reads=[b_aqt, b_vnew], writes=[bp[3]])
        s.add("act", lambda e: e.copy(out=oi[:], in_=ps[3][0:64, :]), reads=[bp[3]], writes=[b_oi])
        for h in range(4):
            s.add("dve", lambda e, h=h: e.scalar_tensor_tensor(out=oo[:, hv(h)], in0=ps[5][0:64, hv(h)], scalar=sm[:, 16 + h:17 + h],
                                                               in1=oi[:, hv(h)], op0=ALU.mult, op1=ALU.add),
                  reads=[bp[5], b_sm, b_oi], writes=[b_oo])
        for h in range(4):
            s.add("pe", lambda e, h=h: e.matmul(ps[1][:, hv(h)], lhsT=kdg[:, hv(h)], rhs=vnew[:, hv(h)], start=True, stop=True),
                  reads=[b_kdg, b_vnew], writes=[bp[1]])
        for h in range(4):
            s.add("dve", lambda e, h=h: e.scalar_tensor_tensor(out=cx.st_f[h][:, 0:128], in0=cx.st_f[h][:, 0:128], scalar=sm2[:, 4 + h:5 + h],
                                                               in1=ps[1][:, hv(h)], op0=ALU.mult, op1=ALU.add),
                  reads=[cx.b_st_f[h], b_sm2, bp[1]], writes=[cx.b_st_f[h]])
            s.add("act", lambda e, h=h: e.copy(out=cx.st_b[h][:, 0:128], in_=cx.st_f[h][:, 0:128]), reads=[cx.b_st_f[h]], writes=[cx.b_st_b[h]])
        if GDN_STOP < 10:
            return
        for h in range(4):
            s.add("act", lambda e, h=h: e.activation(out=oi[:, hv(h)], in_=oo[:, hv(h)], func=AF.Square, accum_out=sm[:, 32 + h:33 + h]),
                  reads=[b_oo], writes=[b_oi, b_sm])
        s.add("act", lambda e: e.activation(out=sm[:, 36:40], in_=sm[:, 32:36], func=AF.Ln, scale=1.0 / 128, bias=cx.epsq[0:64, 3:4]),
              reads=[b_sm, cx.b_epsq], writes=[b_sm])
        s.add("act", lambda e: e.activation(out=sm[:, 40:44], in_=sm[:, 36:40], func=AF.Exp, scale=-0.5), reads=[b_sm], writes=[b_sm])
        for h in range(4):
            s.add("dve", lambda e, h=h: e.scalar_tensor_tensor(out=og[:, hv(h)], in0=oo[:, hv(h)], scalar=sm[:, 40 + h:41 + h],
                                                               in1=gz[:, hv(h)], op0=ALU.mult, op1=ALU.mult),
                  reads=[b_oo, b_sm, b_gz], writes=[b_og])
        for j in range(4):
            s.add("pe", lambda e, j=j: e.transpose(tpb[:, j * 64:(j + 1) * 64], og[:, j * 128:(j + 1) * 128], IDB),
                  reads=[b_og, cx.b_cstb], writes=[bp[0]])
        s.add("act", lambda e: e.copy(out=cx.gomt[:, :, cs], in_=tpb[:, 0:256].rearrange("p (j t) -> p j t", j=4)),
              reads=[bp[0]], writes=[cx.b_gomt])
    return emit


KINDS = ["fox", "gla", "gdn", "fox"]
NCOLS = {"fox": 2052, "gla": 1552, "gdn": 1544}


def host_params(inp, hh):
    f32 = np.float32
    P = {}
    nw = np.asarray(inp["norm_w"], f32)
    P["normw"] = np.ascontiguousarray(nw.reshape(4, 8, 128).transpose(2, 0, 1).reshape(128, 32))
    for li in range(2):
        w = np.asarray(inp["fox_w_in"][li], f32)
        s = slice(hh * 512, hh * 512 + 512)
        P[f"fox_win{li}"] = np.ascontiguousarray(np.concatenate(
            [w[:, 0:1024][:, s], w[:, 1024:2048][:, s], w[:, 3072:4096][:, s], w[:, 2048:3072][:, s],
             w[:, 4096 + hh * 4:4096 + hh * 4 + 4]], axis=1))
        fx = np.zeros((128, 16), f32)
        fx[:, 0] = np.asarray(inp["fox_q_gain"][li], f32)
        fx[:, 1] = np.asarray(inp["fox_k_gain"][li], f32)
        fx[:, 8:12] = np.asarray(inp["fox_b_f"][li], f32)[hh * 4:hh * 4 + 4][None, :]
        P[f"fox_fx{li}"] = fx
    w = np.asarray(inp["gla_w_in"][0], f32)
    P["gla_win"] = np.ascontiguousarray(np.concatenate(
        [w[:, hh * 256:hh * 256 + 256], w[:, 512 + hh * 256:512 + hh * 256 + 256], w[:, 3072:3088],
         w[:, 1024 + hh * 512:1024 + hh * 512 + 512], w[:, 2048 + hh * 512:2048 + hh * 512 + 512]], axis=1))
    P["gla_wup"] = np.ascontiguousarray(np.concatenate(
        [np.asarray(inp["gla_w_gate_up"][0], f32)[:, hh * 256:hh * 256 + 256],
         np.asarray(inp["gla_b_gate"][0], f32)[None, hh * 256:hh * 256 + 256]], axis=0))
    P["gla_gain"] = np.ascontiguousarray(np.tile(np.asarray(inp["gla_o_gain"][0], f32)[None, :], (64, 2)))
    w = np.asarray(inp["gdn_w_in"][0], f32)
    qc = slice(hh * 256, hh * 256 + 256)
    kc = slice(512 + hh * 256, 512 + hh * 256 + 256)
    vs = slice(1024 + hh * 512, 1024 + hh * 512 + 512)
    P["gdn_win"] = np.ascontiguousarray(np.concatenate(
        [w[:, qc], w[:, kc], w[:, vs], w[:, 2048 + hh * 512:2048 + hh * 512 + 512],
         w[:, 3072 + hh * 4:3072 + hh * 4 + 4], w[:, 3080 + hh * 4:3080 + hh * 4 + 4]], axis=1))
    cw = np.asarray(inp["gdn_conv_w"][0], f32)
    cwc = np.concatenate([cw[:, qc], cw[:, kc], cw[:, vs]], axis=1)
    P["gdn_gp"] = np.ascontiguousarray(cwc.reshape(4, 8, 128).transpose(2, 1, 0).reshape(128, 32))
    mp = np.zeros((64, 1024), f32)
    mp[:, 0:512] = np.tile(np.asarray(inp["gdn_o_gain"][0], f32)[None, :], (64, 4))
    mp[:, 512:516] = np.asarray(inp["gdn_dt_bias"][0], f32)[hh * 4:hh * 4 + 4][None, :]
    mp[:, 516:520] = np.asarray(inp["gdn_a_log"][0], f32)[hh * 4:hh * 4 + 4][None, :]
    P["gdn_mp"] = mp
    P["wout"] = [np.ascontiguousarray(np.asarray(inp["fox_w_out"][0], f32)), np.ascontiguousarray(np.asarray(inp["gla_w_out"][0], f32)),
                 np.ascontiguousarray(np.asarray(inp["gdn_w_out"][0], f32)), np.ascontiguousarray(np.asarray(inp["fox_w_out"][1], f32))]
    return P


GROUPS = [[0, 1], [2, 3], [4, 5], [6, 7]]


def exchange(cx):
    for j in range(cx.NCH):
        cx.s.add("pool", lambda e, j=j: e.collective_compute("AllGather", ALU.bypass, replica_groups=cx.groups,
                                                             ins=[cx.gohalf_d[j]], outs=[cx.gofull_d[j]]),
                 reads=[cx.b_gohalf_d[j]], writes=[cx.b_gofull_d[j]], dma=True, cc=True)


def build_fused(S, groups=None):
    from contextlib import ExitStack
    nc = bass.Bass("TRN2", target_bir_lowering=False)
    cx = Ctx()
    cx.nc, cx.S = nc, S
    cx.s = Sched(nc)
    cx.groups = groups or GROUPS
    setup_consts(cx)
    common_setup(cx)
    out_d = nc.dram_tensor("out", [S, 1024], F32, kind="ExternalOutput").ap()
    cx.gohalf_d = [nc.dram_tensor(f"gohalf{j}", [512, cx.CW], BF16, kind="Internal").ap() for j in range(cx.NCH)]
    cx.b_gohalf_d = [Buf(f"gohalf{j}") for j in range(cx.NCH)]

    def din(name, shape):
        return nc.dram_tensor(name, shape, F32, kind="ExternalInput").ap()
    fox_w = [din("fox_win0", [1024, 2052]), din("fox_win1", [1024, 2052])]
    cx.fx_d = [din("fox_fx0", [128, 16]), din("fox_fx1", [128, 16])]
    gla_w, wup_d, gain_d = din("gla_win", [1024, 1552]), din("gla_wup", [17, 256]), din("gla_gain", [64, 512])
    gdn_w, gp_d, mp_d = din("gdn_win", [1024, 1544]), din("gdn_gp", [128, 32]), din("gdn_mp", [64, 1024])
    wout = [din(f"wout{i}", [1024, 1024]) for i in range(4)]
    with ExitStack() as es:
        cx.es = es
        fox_setup(cx)
    cx.es = None
    mixer_common_setup(cx)
    with ExitStack() as es:
        cx.es = es
        gla_setup(cx)
    cx.es = None
    gdn_setup(cx)

    cx.s.scopes = getattr(build_fused, "scopes", False)
    cx.s.phase = "L0_proj"
    fox_load_params(cx, 0)
    boundary(cx, 0, "fox", cx.x_in, cx.xres_d, None, fox_w[0], 2052, fox_proj(cx))
    cx.s.phase = "L0_attn"
    fox_attn(cx)
    exchange(cx)
    cx.s.barrier()
    cx.s.phase = "L1_gla"
    gla_load_params(cx, wup_d, gain_d)
    boundary(cx, 1, "gla", cx.x_in, cx.xres_d, wout[0], gla_w, 1552, gla_proj(cx))
    exchange(cx)
    cx.s.barrier()
    cx.s.phase = "L2_gdn"
    gdn_load_params(cx, gp_d, mp_d)
    boundary(cx, 2, "gdn", cx.xres_d, cx.xres_d, wout[1], gdn_w, 1544, gdn_proj(cx))
    exchange(cx)
    cx.s.barrier()
    cx.s.phase = "L3_proj"
    fox_load_params(cx, 1)
    boundary(cx, 3, "fox", cx.xres_d, cx.xres_d, wout[2], fox_w[1], 2052, fox_proj(cx))
    cx.s.phase = "L3_attn"
    fox_attn(cx)
    exchange(cx)
    cx.s.phase = "L4_final"
    boundary(cx, 4, None, cx.xres_d, cx.xres_d, wout[3], None, 0, None, final_out=out_d)
    cx.s.emit()
    return nc


def kernel(**inp):
    x = np.asarray(inp["x"], np.float32)
    B, S, D = x.shape
    cst = make_consts()
    params = [host_params(inp, hh) for hh in range(2)]
    nc = build_fused(S)
    in_maps = []
    for c in range(8):
        P = params[c % 2]
        m = {"x": np.ascontiguousarray(x[c // 2]), "cst": cst, "normw_in": P["normw"],
             "fox_win0": P["fox_win0"], "fox_win1": P["fox_win1"], "fox_fx0": P["fox_fx0"], "fox_fx1": P["fox_fx1"],
             "gla_win": P["gla_win"], "gla_wup": P["gla_wup"], "gla_gain": P["gla_gain"],
             "gdn_win": P["gdn_win"], "gdn_gp": P["gdn_gp"], "gdn_mp": P["gdn_mp"]}
        for i in range(4):
            m[f"wout{i}"] = P["wout"][i]
        in_maps.append(m)
    res = run_bass_kernel_spmd(nc, in_maps, core_ids=list(range(8)))
    out = np.empty((B, S, D), np.float32)
    for b in range(B):
        out[b] = np.asarray(res.results[2 * b]["out"])
    return out
```

```python
import numpy as np
import concourse.bass as bass
import concourse.mybir as mybir
from concourse.bass_utils import run_bass_kernel_spmd

F32 = mybir.dt.float32
BF16 = mybir.dt.bfloat16
AF = mybir.ActivationFunctionType
ALU = mybir.AluOpType
AX = mybir.AxisListType

ENGS = ["pe", "act", "dve", "pool", "sp"]
NDSEM = 8
SEM_ROLL = 20000


class Buf:
    __slots__ = ("name", "lw", "rd", "rd_dma", "psum", "wr_dma")

    def __init__(self, name="", psum=False):
        self.name = name
        self.psum = psum
        self.lw = None
        self.rd = {}
        self.rd_dma = []
        self.wr_dma = []


class Op:
    __slots__ = ("eng", "fn", "deps", "sig", "idx", "dma", "seq", "sem", "val", "barred", "cc", "phase")


class Sched:
    def __init__(self, nc):
        self.nc = nc
        self.phase = None
        self.scopes = False
        self.q = {e: [] for e in ENGS}

    def begin_record(self):
        self._rec = []

    def end_record(self):
        r, self._rec = self._rec, None
        return r

    def replay_zip(self, a, b):
        ia = ib = 0
        while ia < len(a) or ib < len(b):
            if ib >= len(b) or (ia < len(a) and ia * len(b) <= ib * len(a)):
                self.add(*a[ia])
                ia += 1
            else:
                self.add(*b[ib])
                ib += 1

    def add(self, eng, fn, reads=(), writes=(), dma=False, cc=False):
        if getattr(self, "_rec", None) is not None:
            self._rec.append((eng, fn, tuple(reads), tuple(writes), dma, cc))
            return None
        op = Op()
        op.eng, op.fn, op.dma, op.sig, op.cc = eng, fn, dma, False, cc
        op.seq = op.sem = op.val = None
        op.barred = False
        op.phase = self.phase
        deps, seen = [], set()

        def adddep(d):
            if d is None or id(d) in seen:
                return
            seen.add(id(d))
            deps.append(d)

        for b in reads:
            adddep(b.lw)
            for w in b.wr_dma:
                adddep(w)
            if b.psum:
                for e2, r in b.rd.items():
                    if e2 != eng:
                        adddep(r)
        for b in writes:
            had_readers = bool(b.rd) or bool(b.rd_dma)
            for r in b.rd.values():
                adddep(r)
            for r in b.rd_dma:
                adddep(r)
            if dma and not cc:
                if had_readers or (b.lw is not None and not b.lw.dma):
                    adddep(b.lw)
                    b.wr_dma = []
            else:
                adddep(b.lw)
                for w in b.wr_dma:
                    adddep(w)
        op.deps = [d for d in deps
                   if not (eng == "pe" and d.eng == "pe" and not d.dma and not dma)]
        for b in reads:
            if dma:
                b.rd_dma.append(op)
            else:
                b.rd[eng] = op
        for b in writes:
            b.rd = {}
            b.rd_dma = []
            if dma and not cc:
                b.wr_dma.append(op)
                b.lw = None
            else:
                b.lw = op
                b.wr_dma = []
        op.idx = len(self.q[eng])
        self.q[eng].append(op)
        return op

    def barrier(self):
        pre = []
        for e in ENGS:
            last = None
            for op in self.q[e]:
                if op.dma:
                    if not getattr(op, "barred", False):
                        pre.append(op)
                        op.barred = True
                else:
                    last = op
            if last is not None:
                pre.append(last)
        for e in ENGS:
            op = self.add(e, lambda eng: eng.nop())
            op.deps = [d for d in pre if d is not op]

    def emit(self):
        nc = self.nc
        for e in ENGS:
            for op in self.q[e]:
                for d in op.deps:
                    d.sig = True
        csem = {}
        for e in ENGS:
            nsig = sum(1 for op in self.q[e] if op.sig and not op.dma)
            csem[e] = [nc.alloc_semaphore(f"c_{e}_{i}") for i in range(nsig // SEM_ROLL + 1)]
            s = 0
            for op in self.q[e]:
                if op.sig and not op.dma:
                    op.sem = csem[e][s // SEM_ROLL]
                    op.val = s % SEM_ROLL + 1
                    op.seq = s
                    s += 1
        dsem = {}
        for e in ENGS:
            nd = sum(1 for op in self.q[e] if op.dma)
            if nd == 0:
                continue
            dsem[e] = [nc.alloc_semaphore(f"d_{e}_{i}") for i in range(NDSEM)]
            n = 0
            for op in self.q[e]:
                if op.dma and op.cc:
                    op.sem = nc.alloc_semaphore(f"cc_{e}_{op.idx}")
                    op.val = 1
                elif op.dma:
                    op.sem = dsem[e][n % NDSEM]
                    op.val = 16 * (n // NDSEM + 1)
                    n += 1

        def run_engine(e, eng):
            waited = {}

            def wait(sem, val):
                key = sem.num
                if waited.get(key, 0) >= val:
                    return
                waited[key] = val
                eng.wait_ge(sem, val)

            cur = [None, None]

            def scope(ph):
                if not self.scopes or ph == cur[0]:
                    return
                if cur[1] is not None:
                    cur[1].__exit__(None, None, None)
                    cur[1] = None
                cur[0] = ph
                if ph is not None:
                    cur[1] = nc.named_scope(ph)
                    cur[1].__enter__()

            for op in self.q[e]:
                scope(op.phase)
                for d in op.deps:
                    wait(d.sem, d.val)
                if op.dma and op.cc:
                    ins = op.fn(eng)
                    ins.then_inc(op.sem)
                elif op.dma:
                    if op.val > 16:
                        wait(op.sem, op.val - 16)
                    ins = op.fn(eng)
                    ins.then_inc(op.sem, 16)
                else:
                    ins = op.fn(eng)
                    if op.sig:
                        ins.then_inc(op.sem, 1)
            scope(None)
            if e in dsem:
                last = {}
                for op in self.q[e]:
                    if op.dma:
                        last[op.sem.num] = (op.sem, max(op.val, last.get(op.sem.num, (None, 0))[1]))
                for sem, val in last.values():
                    wait(sem, val)

        with nc.Block() as block:
            @block.tensor
            def _(eng):
                run_engine("pe", eng)

            @block.scalar
            def _(eng):
                run_engine("act", eng)

            @block.vector
            def _(eng):
                run_engine("dve", eng)

            @block.gpsimd
            def _(eng):
                run_engine("pool", eng)

            @block.sync
            def _(eng):
                run_engine("sp", eng)


D_MODEL = 1024
NB = 4
RMS_EPS = 1e-6
NCST = 7 * 128


def make_consts():
    c = np.zeros((128, NCST), np.float32)
    i = np.arange(128)
    c[:, 0:128] = np.eye(128)
    c[:, 128:256] = (i[:, None] <= i[None, :])
    same = (i[:, None] // 64) == (i[None, :] // 64)
    c[:, 256:384] = (i[:, None] <= i[None, :]) & same
    c[:, 384:512] = (i[:, None] < i[None, :]) & same
    c[:, 512:640] = 1.0
    c[:, 640:768] = same
    c[:, 768:896] = (i[:, None] > i[None, :]) & same
    return c


class Ctx:
    pass


def sb(cx, name, shape, dt):
    es = getattr(cx, "es", None)
    if es is not None:
        return es.enter_context(cx.nc.sbuf_tensor(name, shape, dt))
    return cx.nc.alloc_sbuf_tensor(name, shape, dt)


def setup_consts(cx):
    nc, S = cx.nc, cx.S
    cx.cst_d = nc.dram_tensor("cst", [128, NCST], F32, kind="ExternalInput").ap()
    cx.cst = sb(cx, "cst_sb", [128, NCST], F32)
    cx.cstb = sb(cx, "cstb_sb", [128, NCST], BF16)
    cx.b_cst = Buf("cst")
    cx.b_cstb = Buf("cstb")
    cx.s.add("sp", lambda e: e.dma_start(out=cx.cst[:], in_=cx.cst_d), writes=[cx.b_cst], dma=True)
    cx.s.add("dve", lambda e: e.tensor_copy(out=cx.cstb[:], in_=cx.cst[:]), reads=[cx.b_cst], writes=[cx.b_cstb])
    cx.ident_b = cx.cstb[:, 0:128]
    cx.ones_b = cx.cstb[:, 512:640]
    cx.tri_f = cx.cst[:, 128:256]
    cx.tri_b = cx.cstb[:, 128:256]
    cx.epsq = sb(cx, "epsq", [128, 4], F32)
    cx.b_epsq = Buf("epsq")
    cx.s.add("dve", lambda e: e.memset(cx.epsq[:, 0:1], float(128 * RMS_EPS)), writes=[cx.b_epsq])
    cx.s.add("dve", lambda e: e.memset(cx.epsq[:, 1:2], 1.0), writes=[cx.b_epsq])
    cx.s.add("dve", lambda e: e.memset(cx.epsq[:, 2:3], float(D_MODEL * RMS_EPS)), writes=[cx.b_epsq])
    cx.s.add("dve", lambda e: e.memset(cx.epsq[:, 3:4], float(RMS_EPS)), writes=[cx.b_epsq])
    cx.ident_f = cx.cst[:, 0:128]
    cx.ones_f = cx.cst[:, 512:640]
    cx.ps = [nc.alloc_psum_tensor(f"ps{i}", [128, 512], F32) for i in range(8)]
    cx.b_ps = [Buf(f"ps{i}", psum=True) for i in range(8)]


def load_weights(cx, tag, w_d, ncols, nw_row, wbuf, b_w):
    s = cx.s
    n = 0
    for kc in range(8):
        for c0 in range(0, ncols, 516):
            c1 = min(ncols, c0 + 516)
            st = cx.wst[n % 2]
            b_st = cx.b_wst[n % 2]
            n += 1
            s.add("sp", lambda e, st=st, kc=kc, c0=c0, c1=c1: e.dma_start(out=st[:, 0:c1 - c0], in_=w_d[kc * 128:(kc + 1) * 128, c0:c1]),
                  writes=[b_st], dma=True)
            if nw_row is not None:
                s.add("dve", lambda e, st=st, kc=kc, c0=c0, c1=c1: e.tensor_scalar(
                    out=wbuf[:, kc, c0:c1], in0=st[:, 0:c1 - c0], scalar1=cx.normw[:, nw_row * 8 + kc: nw_row * 8 + kc + 1],
                    scalar2=None, op0=ALU.mult), reads=[b_st, cx.b_normw], writes=[b_w])
            else:
                s.add("dve", lambda e, st=st, kc=kc, c0=c0, c1=c1: e.tensor_copy(out=wbuf[:, kc, c0:c1], in_=st[:, 0:c1 - c0]),
                      reads=[b_st], writes=[b_w])


def boundary(cx, L, kind, x_src, x_dst, wout_d, win_d, ncols, emit_proj, final_out=None, tok_range=None):
    nc, s, S = cx.nc, cx.s, cx.S
    MT = S // 512
    has_out = wout_d is not None
    if has_out:
        load_weights(cx, f"wo{L}", wout_d, 1024, None, cx.wout, cx.b_wout)
    if win_d is not None:
        load_weights(cx, f"wi{L}", win_d, ncols, L, cx.win, cx.b_win)
    mts = list(range(MT) if tok_range is None else tok_range)
    dst = final_out if final_out is not None else x_dst

    def phase_a(mt):
        if has_out:
            go, b_go = cx.goT[mt % 2], cx.b_goT[mt % 2]
            jj, c0 = (mt * 512) // cx.CW, (mt * 512) % cx.CW
            s.add("sp", lambda e: e.dma_start(
                out=go[:], in_=cx.gofull_d[jj][:, c0:c0 + 512].rearrange("(c p) t -> p c t", p=128)),
                reads=[cx.b_gofull_d[jj]], writes=[b_go], dma=True)
        for sub in range(4):
            sub_a(mt, sub)

    def sub_a(mt, sub):
        tt = mt * 4 + sub
        xt, b_xt = cx.xt[tt % 3], cx.b_xt[tt % 3]
        s.add("sp", lambda e: e.dma_start(out=xt[:], in_=x_src[tt * 128:(tt + 1) * 128, :]),
              reads=[cx.b_xres_d[tt]] if x_src is cx.xres_d else [], writes=[b_xt], dma=True)
        if has_out:
            go, b_go = cx.goT[mt % 2], cx.b_goT[mt % 2]
            for half in range(2):
                yp, b_yp = cx.ps[1], cx.b_ps[1]
                for kc in range(8):
                    s.add("pe", lambda e, kc=kc, half=half: e.matmul(
                        yp[:, :], lhsT=go[:, kc, sub * 128:(sub + 1) * 128],
                        rhs=cx.wout[:, kc, half * 512:(half + 1) * 512], start=(kc == 0), stop=(kc == 7)),
                        reads=[b_go, cx.b_wout], writes=[b_yp])
                s.add("dve", lambda e, half=half: e.tensor_tensor(
                    out=xt[:, half * 512:(half + 1) * 512], in0=yp[:, :], in1=xt[:, half * 512:(half + 1) * 512],
                    op=ALU.add), reads=[b_yp, b_xt], writes=[b_xt])
            s.add("pool", lambda e: e.dma_start(out=dst[tt * 128:(tt + 1) * 128, :], in_=xt[:]),
                  reads=[b_xt], writes=[cx.b_xres_d[tt]], dma=True)
        if win_d is None:
            return
        sq, b_sq = cx.junk[tt % 2], cx.b_junk[tt % 2]
        ssq, b_ssq = cx.stat[tt % 4], cx.b_stat[tt % 4]
        s.add("act", lambda e: e.activation(out=sq[:], in_=xt[:], func=AF.Square, accum_out=ssq[:, 0:1]),
              reads=[b_xt], writes=[b_sq, b_ssq])
        s.add("act", lambda e: e.activation(out=ssq[:, 2:3], in_=ssq[:, 0:1], func=AF.Ln, bias=cx.epsq[:, 2:3]),
              reads=[b_ssq, cx.b_epsq], writes=[b_ssq])
        s.add("act", lambda e: e.activation(out=ssq[:, 1:2], in_=ssq[:, 2:3], func=AF.Exp, scale=-0.5),
              reads=[b_ssq], writes=[b_ssq])
        xs, b_xs = cx.xs[tt % 4], cx.b_xs[tt % 4]
        s.add("dve", lambda e: e.tensor_scalar(
            out=xs[:], in0=xt[:], scalar1=ssq[:, 1:2], scalar2=float(np.sqrt(D_MODEL)),
            op0=ALU.mult, op1=ALU.mult), reads=[b_xt, b_ssq], writes=[b_xs])

    def phase_t(mt):
        xT, b_xT = cx.xT[mt % 2], cx.b_xT[mt % 2]
        tp = cx.ps[0].bitcast(BF16)
        b_tp = cx.b_ps[0]
        for sub in range(4):
            tt = mt * 4 + sub
            xs, b_xs = cx.xs[tt % 4], cx.b_xs[tt % 4]
            for kc in range(8):
                s.add("pe", lambda e, kc=kc, xs=xs: e.transpose(
                    tp[:, kc * 128:(kc + 1) * 128], xs[:, kc * 128:(kc + 1) * 128], cx.ident_b),
                    reads=[b_xs, cx.b_cstb], writes=[b_tp])
            s.add("act", lambda e, sub=sub: e.copy(
                out=xT[:, :, sub * 128:(sub + 1) * 128], in_=tp[:, :].rearrange("p (c t) -> p c t", c=8)),
                reads=[b_tp], writes=[b_xT])

    if win_d is None:
        for mt in mts:
            phase_a(mt)
        return
    phase_a(mts[0])
    phase_t(mts[0])
    for i, mt in enumerate(mts):
        nxt = mts[i + 1] if i + 1 < len(mts) else None
        if nxt is not None:
            phase_a(nxt)
        emit_proj(mt, cx.xT[mt % 2], cx.b_xT[mt % 2])
        if nxt is not None:
            phase_t(nxt)
    if hasattr(emit_proj, "flush"):
        emit_proj.flush()


def fox_setup(cx):
    nc, S = cx.nc, cx.S
    cx.qT_d = nc.dram_tensor("qT_d", [4, 128, S], BF16, kind="Internal").ap()
    cx.kT_d = nc.dram_tensor("kT_d", [4, 128, S], BF16, kind="Internal").ap()
    cx.gT_d = nc.dram_tensor("gT_d", [4, 128, S], BF16, kind="Internal").ap()
    cx.v_d = nc.dram_tensor("v_d", [S, 512], BF16, kind="Internal").ap()
    cx.crow_d = nc.dram_tensor("crow_d", [4, S], F32, kind="Internal").ap()
    cx.b_qT_d, cx.b_kT_d, cx.b_gT_d, cx.b_v_d, cx.b_crow_d = (Buf("qT_d"), Buf("kT_d"), Buf("gT_d"), Buf("v_d"), Buf("crow_d"))
    cx.fxp = sb(cx, "fxp", [128, 16], F32)
    cx.b_fxp = Buf("fxp")
    cx.cumcol = sb(cx, "cumcol", [128, S // 128, 4], F32)
    cx.b_cumcol = Buf("cumcol")
    cx.carry = sb(cx, "carry", [128, 4], F32)
    cx.b_carry = Buf("carry")
    cx.fl = [sb(cx, f"fl{i}", [128, 16], F32) for i in range(2)]
    cx.b_fl = [Buf(f"fl{i}") for i in range(2)]
    cx.crow_sb = [sb(cx, f"crow{i}", [4, 128], F32) for i in range(2)]
    cx.b_crow_sb = [Buf(f"crow{i}") for i in range(2)]
    cx.sqb = [sb(cx, f"sqb{i}", [128, 512], BF16) for i in range(2)]
    cx.b_sqb = [Buf(f"sqb{i}") for i in range(2)]
    cx.rs = [sb(cx, f"rs{i}", [128, 512], F32) for i in range(2)]
    cx.b_rs = [Buf(f"rs{i}") for i in range(2)]
    cx.ob = [sb(cx, f"ob{i}", [128, 512], BF16) for i in range(4)]
    cx.b_ob = [Buf(f"ob{i}") for i in range(4)]
    cx.b_p2 = [Buf(f"p2_{i}") for i in range(4)]
    cx.kT_sb = sb(cx, "kT_sb", [128, S], BF16)
    cx.b_kT_sb = Buf("kT_sb")
    cx.v_sb = sb(cx, "v_sb", [128, S // 128, 128], BF16)
    cx.b_v_sb = Buf("v_sb")
    cx.q_sb = [sb(cx, f"q_sb{i}", [128, 512], BF16) for i in range(2)]
    cx.b_q_sb = [Buf(f"q_sb{i}") for i in range(2)]
    cx.g_sb = [sb(cx, f"g_sb{i}", [128, 512], BF16) for i in range(2)]
    cx.b_g_sb = [Buf(f"g_sb{i}") for i in range(2)]
    cx.cnq = [sb(cx, f"cnq{i}", [128, 512], F32) for i in range(2)]
    cx.b_cnq = [Buf(f"cnq{i}") for i in range(2)]
    cx.tt_sb = [sb(cx, f"tt_sb{i}", [128, 512], F32) for i in range(4)]
    cx.b_tt_sb = [Buf(f"tt_sb{i}") for i in range(4)]
    cx.p_sb = [sb(cx, f"p_sb{i}", [128, 512], BF16) for i in range(4)]
    cx.b_p_sb = [Buf(f"p_sb{i}") for i in range(4)]
    cx.rl = [sb(cx, f"rl{i}", [128, 512], F32) for i in range(2)]
    cx.b_rl = [Buf(f"rl{i}") for i in range(2)]
    cx.o1 = [sb(cx, f"o1{i}", [128, 512], F32) for i in range(2)]
    cx.b_o1 = [Buf(f"o1{i}") for i in range(2)]
    cx.go_sb = [sb(cx, f"go_sb{i}", [128, 512], BF16) for i in range(2)]
    cx.b_go_sb = [Buf(f"go_sb{i}") for i in range(2)]


def fox_load_params(cx, li):
    s = cx.s
    d = cx.fx_d[li]
    s.add("sp", lambda e: e.dma_start(out=cx.fxp[:], in_=d), writes=[cx.b_fxp], dma=True)
    s.add("dve", lambda e: e.tensor_scalar(out=cx.fxp[:, 1:2], in0=cx.fxp[:, 1:2], scalar1=float(np.sqrt(128.0)),
                                           scalar2=None, op0=ALU.mult), reads=[cx.b_fxp], writes=[cx.b_fxp])
    s.add("dve", lambda e: e.memset(cx.carry[:], 0.0), writes=[cx.b_carry])


def fox_proj(cx):
    s = cx.s
    win = cx.win

    p2 = cx.ps[2]
    b_p2 = cx.b_ps[2]
    pend = []

    def phaseA(tt, sub, xT, b_xT):
        for kc in range(8):
            s.add("pe", lambda e, kc=kc: e.matmul(
                p2[:, 0:4], lhsT=xT[:, kc, sub * 128:(sub + 1) * 128], rhs=win[:, kc, 2048:2052],
                start=(kc == 0), stop=(kc == 7)), reads=[cx.b_win, b_xT], writes=[b_p2])
        fl, b_fl = cx.fl[tt % 2], cx.b_fl[tt % 2]
        s.add("dve", lambda e: e.tensor_tensor(out=fl[:, 0:4], in0=p2[:, 0:4], in1=cx.fxp[:, 8:12], op=ALU.add),
              reads=[b_p2, cx.b_fxp], writes=[b_fl])
        s.add("act", lambda e: e.activation(out=fl[:, 4:8], in_=fl[:, 0:4], func=AF.Exp, scale=-1.0),
              reads=[b_fl], writes=[b_fl])
        s.add("act", lambda e: e.activation(out=fl[:, 8:12], in_=fl[:, 4:8], func=AF.Ln, bias=cx.epsq[:, 1:2]),
              reads=[b_fl, cx.b_epsq], writes=[b_fl])

    def phaseB(tt):
        fl, b_fl = cx.fl[tt % 2], cx.b_fl[tt % 2]
        s.add("pe", lambda e: e.matmul(p2[:, 8:12], lhsT=cx.tri_f, rhs=fl[:, 8:12], start=True, stop=True),
              reads=[b_fl, cx.b_cst], writes=[b_p2])
        s.add("pe", lambda e: e.matmul(p2[:, 16:20], lhsT=cx.ones_f, rhs=fl[:, 8:12], start=True, stop=True),
              reads=[b_fl, cx.b_cst], writes=[b_p2])
        s.add("dve", lambda e: e.tensor_tensor(out=cx.cumcol[:, tt, :], in0=p2[:, 8:12], in1=cx.carry[:], op=ALU.add),
              reads=[b_p2, cx.b_carry], writes=[cx.b_cumcol])
        s.add("dve", lambda e: e.tensor_tensor(out=cx.carry[:], in0=p2[:, 16:20], in1=cx.carry[:], op=ALU.add),
              reads=[b_p2, cx.b_carry], writes=[cx.b_carry])

    def phaseC(tt):
        s.add("pe", lambda e: e.matmul(p2[0:4, 32:160], lhsT=cx.cumcol[:, tt, :], rhs=cx.ident_f, start=True, stop=True),
              reads=[cx.b_cumcol, cx.b_cst], writes=[b_p2])
        cr, b_cr = cx.crow_sb[tt % 2], cx.b_crow_sb[tt % 2]
        s.add("dve", lambda e: e.tensor_copy(out=cr[:], in_=p2[0:4, 32:160]), reads=[b_p2], writes=[b_cr])
        s.add("pool", lambda e: e.dma_start(out=cx.crow_d[:, tt * 128:(tt + 1) * 128], in_=cr[:]),
              reads=[b_cr], writes=[cx.b_crow_d], dma=True)

    def fchain(tt, sub, xT, b_xT):
        phaseA(tt, sub, xT, b_xT)
        if tt >= 1:
            phaseB(tt - 1)
        if tt >= 2:
            phaseC(tt - 2)

    def flush():
        TT = cx.S // 128
        phaseB(TT - 1)
        phaseC(TT - 2)
        phaseC(TT - 1)

    def emit(mt, xT, b_xT):
        for g in range(12):
            typ, h = g // 4, g % 4
            pb = 3 + (g % 2)
            ps, b_p = cx.ps[pb], cx.b_ps[pb]
            for kc in range(8):
                s.add("pe", lambda e, kc=kc, g=g, ps=ps: e.matmul(
                    ps[:, :], lhsT=win[:, kc, g * 128:(g + 1) * 128], rhs=xT[:, kc, :], start=(kc == 0), stop=(kc == 7)),
                    reads=[cx.b_win, b_xT], writes=[b_p])
            ob, b_ob = cx.ob[g % 4], cx.b_ob[g % 4]
            if typ < 2:
                sq, b_sq = cx.sqb[g % 2], cx.b_sqb[g % 2]
                rs, b_rs = cx.rs[g % 2], cx.b_rs[g % 2]
                s.add("act", lambda e, sq=sq, ps=ps: e.activation(out=sq[:], in_=ps[:, :], func=AF.Square),
                      reads=[b_p], writes=[b_sq])
                p5, b_p5 = cx.ps[5], cx.b_ps[5]
                s.add("pe", lambda e, sq=sq, p5=p5: e.matmul(p5[:, :], lhsT=cx.ones_b, rhs=sq[:], start=True, stop=True),
                      reads=[b_sq, cx.b_cstb], writes=[b_p5])
                s.add("act", lambda e, rs=rs, p5=p5: e.activation(
                    out=rs[:], in_=p5[:, :], func=AF.Ln, bias=cx.epsq[:, 0:1]),
                    reads=[b_p5, cx.b_epsq], writes=[b_rs])
                s.add("act", lambda e, rs=rs: e.activation(out=rs[:], in_=rs[:], func=AF.Exp, scale=-0.5),
                      reads=[b_rs], writes=[b_rs])
                s.add("dve", lambda e, ob=ob, ps=ps, rs=rs, typ=typ: e.scalar_tensor_tensor(
                    out=ob[:], in0=ps[:, :], scalar=cx.fxp[:, typ:typ + 1], in1=rs[:], op0=ALU.mult, op1=ALU.mult),
                    reads=[b_p, b_rs, cx.b_fxp], writes=[b_ob])
                dst, b_dst = (cx.qT_d, cx.b_qT_d) if typ == 0 else (cx.kT_d, cx.b_kT_d)
            else:
                s.add("act", lambda e, ob=ob, ps=ps: e.activation(out=ob[:], in_=ps[:, :], func=AF.Silu),
                      reads=[b_p], writes=[b_ob])
                dst, b_dst = cx.gT_d, cx.b_gT_d
            s.add("pool", lambda e, ob=ob, dst=dst, h=h, mt=mt: e.dma_start(
                out=dst[h, :, mt * 512:(mt + 1) * 512], in_=ob[:]), reads=[b_ob], writes=[b_dst], dma=True)
        for sub in range(4):
            tt = mt * 4 + sub
            pb = 6 + (sub % 2)
            ps, b_p = cx.ps[pb], cx.b_ps[pb]
            for kc in range(8):
                s.add("pe", lambda e, kc=kc, sub=sub, ps=ps: e.matmul(
                    ps[:, :], lhsT=xT[:, kc, sub * 128:(sub + 1) * 128], rhs=win[:, kc, 1536:2048],
                    start=(kc == 0), stop=(kc == 7)), reads=[cx.b_win, b_xT], writes=[b_p])
            ob, b_ob = cx.ob[sub % 4], cx.b_ob[sub % 4]
            s.add("act", lambda e, ob=ob, ps=ps: e.copy(out=ob[:], in_=ps[:, :]), reads=[b_p], writes=[b_ob])
            s.add("pool", lambda e, ob=ob, tt=tt: e.dma_start(out=cx.v_d[tt * 128:(tt + 1) * 128, :], in_=ob[:]),
                  reads=[b_ob], writes=[cx.b_v_d], dma=True)
            fchain(tt, sub, xT, b_xT)
    emit.flush = flush
    return emit


def fox_attn(cx):
    s, S = cx.s, cx.S
    NQ = S // 512
    for h in range(4):
        s.add("sp", lambda e, h=h: e.dma_start(out=cx.kT_sb[:], in_=cx.kT_d[h]), reads=[cx.b_kT_d], writes=[cx.b_kT_sb], dma=True)
        s.add("sp", lambda e, h=h: e.dma_start(
            out=cx.v_sb[:], in_=cx.v_d[:, h * 128:(h + 1) * 128].rearrange("(kb p) d -> p kb d", p=128)),
            reads=[cx.b_v_d], writes=[cx.b_v_sb], dma=True)
        steps = [(T, kb) for T in range(NQ) for kb in range(4 * T + 4)]
        LOOK = 3

        def front(i, h=h):
            T, kb = steps[i]
            if kb == 0:
                q, b_q = cx.q_sb[T % 2], cx.b_q_sb[T % 2]
                g, b_g = cx.g_sb[T % 2], cx.b_g_sb[T % 2]
                cn, b_cn = cx.cnq[T % 2], cx.b_cnq[T % 2]
                s.add("sp", lambda e: e.dma_start(out=q[:], in_=cx.qT_d[h, :, T * 512:(T + 1) * 512]),
                      reads=[cx.b_qT_d], writes=[b_q], dma=True)
                s.add("sp", lambda e: e.dma_start(out=g[:], in_=cx.gT_d[h, :, T * 512:(T + 1) * 512]),
                      reads=[cx.b_gT_d], writes=[b_g], dma=True)
                s.add("sp", lambda e: e.dma_start(out=cn[:], in_=cx.crow_d[h:h + 1, T * 512:(T + 1) * 512].partition_broadcast(128)),
                      reads=[cx.b_crow_d], writes=[b_cn], dma=True)
            q, b_q = cx.q_sb[T % 2], cx.b_q_sb[T % 2]
            cn, b_cn = cx.cnq[T % 2], cx.b_cnq[T % 2]
            j = kb - 4 * T
            c0 = max(0, j) * 128
            sp_, b_sp = cx.ps[i % 4], cx.b_ps[i % 4]
            tt_, b_tt = cx.tt_sb[i % 4], cx.b_tt_sb[i % 4]
            p_, b_pp = cx.p_sb[i % 4], cx.b_p_sb[i % 4]
            s.add("pe", lambda e: e.matmul(sp_[:, c0:512], lhsT=cx.kT_sb[:, kb * 128:(kb + 1) * 128], rhs=q[:, c0:512],
                                           start=True, stop=True), reads=[cx.b_kT_sb, b_q], writes=[b_sp])
            s.add("dve", lambda e: e.scalar_tensor_tensor(
                out=tt_[:, c0:512], in0=sp_[:, c0:512], scalar=cx.cumcol[:, kb, h:h + 1], in1=cn[:, c0:512],
                op0=ALU.add, op1=ALU.subtract), reads=[b_sp, cx.b_cumcol, b_cn], writes=[b_tt])
            s.add("act", lambda e: e.activation(out=p_[:, c0:512], in_=tt_[:, c0:512], func=AF.Exp),
                  reads=[b_tt], writes=[b_pp])
            if j >= 0:
                s.add("pool", lambda e: e.tensor_tensor(out=p_[:, c0:c0 + 128], in0=p_[:, c0:c0 + 128], in1=cx.tri_b, op=ALU.mult),
                      reads=[b_pp, cx.b_cstb], writes=[b_pp])

        def back(i, h=h):
            T, kb = steps[i]
            j = kb - 4 * T
            c0 = max(0, j) * 128
            p_, b_pp = cx.p_sb[i % 4], cx.b_p_sb[i % 4]
            op_, b_op = cx.ps[4 + T % 2], cx.b_ps[4 + T % 2]
            lp_, b_lp = cx.ps[6 + T % 2], cx.b_ps[6 + T % 2]
            last = (kb == 4 * T + 3)
            s.add("pe", lambda e: e.matmul(op_[:, c0:512], lhsT=cx.v_sb[:, kb, :], rhs=p_[:, c0:512],
                                           start=(kb == 0), stop=last, skip_group_check=True), reads=[cx.b_v_sb, b_pp], writes=[b_op])
            s.add("pe", lambda e: e.matmul(lp_[:, c0:512], lhsT=cx.ones_b, rhs=p_[:, c0:512],
                                           start=(kb == 0), stop=last, skip_group_check=True), reads=[cx.b_cstb, b_pp], writes=[b_lp])
            if last:
                rl, b_rl = cx.rl[T % 2], cx.b_rl[T % 2]
                o1, b_o1 = cx.o1[T % 2], cx.b_o1[T % 2]
                go, b_go = cx.go_sb[T % 2], cx.b_go_sb[T % 2]
                g, b_g = cx.g_sb[T % 2], cx.b_g_sb[T % 2]
                s.add("act", lambda e: e.activation(out=rl[:], in_=lp_[:, :], func=AF.Ln), reads=[b_lp], writes=[b_rl])
                s.add("act", lambda e: e.activation(out=rl[:], in_=rl[:], func=AF.Exp, scale=-1.0), reads=[b_rl], writes=[b_rl])
                s.add("dve", lambda e: e.tensor_tensor(out=o1[:], in0=op_[:, :], in1=rl[:], op=ALU.mult),
                      reads=[b_op, b_rl], writes=[b_o1])
                s.add("pool", lambda e: e.tensor_tensor(out=go[:], in0=o1[:], in1=g[:], op=ALU.mult),
                      reads=[b_o1, b_g], writes=[b_go])
                jj, gc0 = (T * 512) // cx.CW, (T * 512) % cx.CW
                s.add("pool", lambda e: e.dma_start(out=cx.gohalf_d[jj][h * 128:(h + 1) * 128, gc0:gc0 + 512], in_=go[:]),
                      reads=[b_go], writes=[cx.b_gohalf_d[jj]], dma=True)

        n = len(steps)
        for i in range(n + LOOK):
            if i < n:
                front(i)
            if i >= LOOK:
                back(i - LOOK)


def common_setup(cx):
    nc, S = cx.nc, cx.S
    cx.wst = [sb(cx, f"wst{i}", [128, 516], F32) for i in range(2)]
    cx.b_wst = [Buf(f"wst{i}") for i in range(2)]
    cx.win = sb(cx, "win", [128, 8, 2064], BF16)
    cx.b_win = Buf("win")
    cx.wout = sb(cx, "wout", [128, 8, 1024], BF16)
    cx.b_wout = Buf("wout")
    cx.normw = sb(cx, "normw", [128, 32], F32)
    cx.b_normw = Buf("normw")
    cx.normw_d = nc.dram_tensor("normw_in", [128, 32], F32, kind="ExternalInput").ap()
    cx.s.add("sp", lambda e: e.dma_start(out=cx.normw[:], in_=cx.normw_d), writes=[cx.b_normw], dma=True)
    cx.xT = [sb(cx, f"xT{i}", [128, 8, 512], BF16) for i in range(2)]
    cx.b_xT = [Buf(f"xT{i}") for i in range(2)]
    cx.goT = [sb(cx, f"goT{i}", [128, 8, 512], BF16) for i in range(2)]
    cx.b_goT = [Buf(f"goT{i}") for i in range(2)]
    cx.xt = [sb(cx, f"xt{i}", [128, 1024], F32) for i in range(3)]
    cx.b_xt = [Buf(f"xt{i}") for i in range(3)]
    cx.junk = [sb(cx, f"junk{i}", [128, 1024], BF16) for i in range(2)]
    cx.b_junk = [Buf(f"junk{i}") for i in range(2)]
    cx.stat = [sb(cx, f"stat{i}", [128, 4], F32) for i in range(4)]
    cx.b_stat = [Buf(f"stat{i}") for i in range(4)]
    cx.xs = [sb(cx, f"xs{i}", [128, 1024], BF16) for i in range(4)]
    cx.b_xs = [Buf(f"xs{i}") for i in range(4)]
    cx.x_in = nc.dram_tensor("x", [S, 1024], F32, kind="ExternalInput").ap()
    cx.xres_d = nc.dram_tensor("xres_d", [S, 1024], F32, kind=getattr(cx, "xres_kind", "Internal")).ap()
    cx.b_xres_d = [Buf(f"xres{t}") for t in range(S // 128)]
    cx.CW = min(1024, S)
    cx.NCH = S // cx.CW
    cx.gofull_d = [nc.dram_tensor(f"gofull_d{j}", [1024, cx.CW], BF16, kind="Internal").ap() for j in range(cx.NCH)]
    cx.b_gofull_d = [Buf(f"gofull_d{j}") for j in range(cx.NCH)]


def mixer_common_setup(cx):
    cx.gomt = sb(cx, "gomt", [128, 4, 512], BF16)
    cx.b_gomt = Buf("gomt")
    cx.st_f = [sb(cx, f"st_f{i}", [128, 256], F32) for i in range(4)]
    cx.b_st_f = [Buf(f"st_f{i}") for i in range(4)]
    cx.st_b = [sb(cx, f"st_b{i}", [128, 256], BF16) for i in range(4)]
    cx.b_st_b = [Buf(f"st_b{i}") for i in range(4)]
    cx.mparam = sb(cx, "mparam", [64, 1024], F32)
    cx.b_mparam = Buf("mparam")

    def two(name, shape, dt):
        return [sb(cx, f"{name}{i}", shape, dt) for i in range(2)], [Buf(f"{name}{i}") for i in range(2)]
    cx.w_gz, cx.b_w_gz = two("w_gz", [64, 512], F32)
    cx.w_og, cx.b_w_og = two("w_og", [64, 512], BF16)


def gla_setup(cx):
    cx.cmraw = [sb(cx, f"cmraw{i}", [128, 512], F32) for i in range(4)]
    cx.b_cmraw = [Buf(f"cmraw{i}") for i in range(4)]
    cx.glT = sb(cx, "glT", [32, 512], F32)
    cx.b_glT = Buf("glT")
    cx.wup = sb(cx, "wup", [32, 256], F32)
    cx.b_wup = Buf("wup")

    def two(name, shape, dt):
        return [sb(cx, f"{name}{i}", shape, dt) for i in range(2)], [Buf(f"{name}{i}") for i in range(2)]
    cx.w_a, cx.b_w_a = two("w_a", [64, 512], F32)
    cx.w_b, cx.b_w_b = two("w_b", [64, 512], F32)
    cx.w_c, cx.b_w_c = two("w_c", [64, 512], F32)
    cx.w_v, cx.b_w_v = two("w_v", [64, 512], BF16)
    cx.w_kd, cx.b_w_kd = two("w_kd", [64, 512], BF16)
    cx.w_e1, cx.b_w_e1 = two("w_e1", [128, 256], F32)
    cx.w_e2, cx.b_w_e2 = two("w_e2", [128, 256], F32)
    cx.w_qd, cx.b_w_qd = two("w_qd", [128, 256], BF16)
    cx.w_ki, cx.b_w_ki = two("w_ki", [128, 256], BF16)
    cx.w_at, cx.b_w_at = two("w_at", [64, 256], BF16)
    cx.w_st, cx.b_w_st = two("w_st", [64, 16], F32)
    cx.w_nl, cx.b_w_nl = two("w_nl", [64, 256], BF16)
    cx.kb16 = [sb(cx, f"kb16_{i}", [128, 512], BF16) for i in range(2)]
    cx.b_kb16 = [Buf(f"kb16_{i}") for i in range(2)]
    cx.vcT = [sb(cx, f"vcT{i}", [128, 512], BF16) for i in range(4)]
    cx.b_vcT = [Buf(f"vcT{i}") for i in range(4)]
    cx.zcT = [sb(cx, f"zcT{i}", [128, 512], BF16) for i in range(4)]
    cx.b_zcT = [Buf(f"zcT{i}") for i in range(4)]
    cx.glTb = sb(cx, "glTb", [32, 512], BF16)
    cx.b_glTb = Buf("glTb")
    cx.wupb = sb(cx, "wupb", [32, 256], BF16)
    cx.b_wupb = Buf("wupb")


def gla_load_params(cx, wup_d, gain_d):
    s = cx.s
    s.add("dve", lambda e: e.memset(cx.glT[:], 1.0), writes=[cx.b_glT])
    s.add("sp", lambda e: e.dma_start(out=cx.wup[0:17, :], in_=wup_d), writes=[cx.b_wup], dma=True)
    s.add("dve", lambda e: e.tensor_copy(out=cx.wupb[0:17, :], in_=cx.wup[0:17, :]), reads=[cx.b_wup], writes=[cx.b_wupb])
    s.add("dve", lambda e: e.memset(cx.glTb[:], 1.0), writes=[cx.b_glTb])
    s.add("sp", lambda e: e.dma_start(out=cx.mparam[:, 0:512], in_=gain_d), writes=[cx.b_mparam], dma=True)
    for h in range(2):
        s.add("dve", lambda e, h=h: e.memset(cx.st_f[h][:], 0.0), writes=[cx.b_st_f[h]])
        s.add("dve", lambda e, h=h: e.memset(cx.st_b[h][:], 0.0), writes=[cx.b_st_b[h]])


def gla_proj(cx):
    s, win = cx.s, cx.win
    U = cx.cst[0:64, 128:192]
    Ub = cx.cstb[0:64, 128:192]
    ONES = cx.cst[0:64, 512:576]
    IDB = cx.cstb[0:64, 0:64]
    ps, bp = cx.ps, cx.b_ps

    def emit(mt, xT, b_xT):
        for g in range(4):
            pb = 2
            for kc in range(8):
                s.add("pe", lambda e, kc=kc, g=g: e.matmul(
                    ps[pb][:, :], lhsT=win[:, kc, g * 128:(g + 1) * 128], rhs=xT[:, kc, :], start=(kc == 0), stop=(kc == 7)),
                    reads=[cx.b_win, b_xT], writes=[bp[pb]])
            s.add("act", lambda e, g=g: e.copy(out=cx.cmraw[g][:], in_=ps[pb][:, :]), reads=[bp[pb]], writes=[cx.b_cmraw[g]])
            if g >= 2:
                s.add("act", lambda e, g=g: e.copy(out=cx.kb16[g - 2][:], in_=ps[pb][:, :]), reads=[bp[pb]], writes=[cx.b_kb16[g - 2]])
        for kc in range(8):
            s.add("pe", lambda e, kc=kc: e.matmul(
                ps[2][0:16, :], lhsT=win[:, kc, 512:528], rhs=xT[:, kc, :], start=(kc == 0), stop=(kc == 7)),
                reads=[cx.b_win, b_xT], writes=[bp[2]])
        s.add("act", lambda e: e.copy(out=cx.glTb[0:16, :], in_=ps[2][0:16, :]), reads=[bp[2]], writes=[cx.b_glTb])
        for g in range(8):
            pb = 2 + (g % 2)
            c0 = (528 if g < 4 else 1040) + (g % 4) * 128
            for kc in range(8):
                s.add("pe", lambda e, kc=kc, c0=c0, pb=pb: e.matmul(
                    ps[pb][:, :], lhsT=win[:, kc, c0:c0 + 128], rhs=xT[:, kc, :], start=(kc == 0), stop=(kc == 7)),
                    reads=[cx.b_win, b_xT], writes=[bp[pb]])
            if g < 4:
                s.add("act", lambda e, g=g, pb=pb: e.copy(out=cx.vcT[g][:], in_=ps[pb][:, :]), reads=[bp[pb]], writes=[cx.b_vcT[g]])
            else:
                s.add("act", lambda e, g=g, pb=pb: e.activation(out=cx.zcT[g - 4][:], in_=ps[pb][:, :], func=AF.Silu),
                      reads=[bp[pb]], writes=[cx.b_zcT[g - 4]])
        chunk(mt, 0, xT, b_xT, 'Y')
        for ci in range(8):
            if ci + 1 < 8:
                s.begin_record()
                chunk(mt, ci, xT, b_xT, 'X')
                lx = s.end_record()
                s.begin_record()
                chunk(mt, ci + 1, xT, b_xT, 'Y')
                ly = s.end_record()
                s.replay_zip(lx, ly)
            else:
                chunk(mt, ci, xT, b_xT, 'X')
        jj, c0 = (mt * 512) // cx.CW, (mt * 512) % cx.CW
        s.add("pool", lambda e: e.dma_start(out=cx.gohalf_d[jj][:, c0:c0 + 512].rearrange("(j p) t -> p j t", p=128),
                                            in_=cx.gomt[:]), reads=[cx.b_gomt], writes=[cx.b_gohalf_d[jj]], dma=True)
        if c0 + 512 == cx.CW:
            exchange_chunk(cx, jj)

    def chunk(mt, ci, xT, b_xT, part):
        if True:
            c = mt * 8 + ci
            par = c % 2
            cs = slice(ci * 64, (ci + 1) * 64)
            wa, b_wa = cx.w_a[par], cx.b_w_a[par]
            wb, b_wb = cx.w_b[par], cx.b_w_b[par]
            wc, b_wc = cx.w_c[par], cx.b_w_c[par]
            wv, b_wv = cx.w_v[par], cx.b_w_v[par]
            gz, b_gz = cx.w_gz[par], cx.b_w_gz[par]
            kd, b_kd = cx.w_kd[par], cx.b_w_kd[par]
            e1, b_e1 = cx.w_e1[par], cx.b_w_e1[par]
            e2, b_e2 = cx.w_e2[par], cx.b_w_e2[par]
            qd, b_qd = cx.w_qd[par], cx.b_w_qd[par]
            ki, b_ki = cx.w_ki[par], cx.b_w_ki[par]
            at, b_at = cx.w_at[par], cx.b_w_at[par]
            og, b_og = cx.w_og[par], cx.b_w_og[par]
            st, b_st = cx.w_st[par], cx.b_w_st[par]
            if part == 'Y':
                tpv = ps[4].bitcast(BF16)
                tpz = ps[5].bitcast(BF16)
                for g in range(4):
                    s.add("pe", lambda e, g=g: e.transpose(tpv[0:64, g * 128:(g + 1) * 128], cx.vcT[g][:, cs], cx.ident_b),
                          reads=[cx.b_vcT[g], cx.b_cstb], writes=[bp[4]])
                for h in range(2):
                    s.add("pe", lambda e, h=h: e.transpose(tpv[0:64, 512 + h * 128:512 + (h + 1) * 128], cx.kb16[h][:, cs], cx.ident_b),
                          reads=[cx.b_kb16[h], cx.b_cstb], writes=[bp[4]])
                s.add("act", lambda e: e.copy(out=wv[:], in_=tpv[0:64, 0:512]), reads=[bp[4]], writes=[b_wv])
                for g in range(4):
                    s.add("pe", lambda e, g=g: e.transpose(tpz[0:64, g * 128:(g + 1) * 128], cx.zcT[g][:, cs], cx.ident_b),
                          reads=[cx.b_zcT[g], cx.b_cstb], writes=[bp[5]])
                s.add("dve", lambda e: e.tensor_tensor(out=gz[:], in0=tpz[0:64, 0:512], in1=cx.mparam[:, 0:512], op=ALU.mult),
                      reads=[bp[5], cx.b_mparam], writes=[b_gz])
                nl, b_nl = cx.w_nl[par], cx.b_w_nl[par]
                s.add("pe", lambda e: e.matmul(ps[6][0:64, 0:256], lhsT=cx.glTb[0:17, cs], rhs=cx.wupb[0:17, :], start=True, stop=True),
                      reads=[cx.b_glTb, cx.b_wupb], writes=[bp[6]])
                s.add("act", lambda e: e.activation(out=wa[:, 0:256], in_=ps[6][0:64, 0:256], func=AF.Exp, scale=-1.0),
                      reads=[bp[6]], writes=[b_wa])
                s.add("act", lambda e: e.activation(out=nl[:], in_=wa[:, 0:256], func=AF.Ln, bias=cx.epsq[0:64, 1:2]),
                      reads=[b_wa, cx.b_epsq], writes=[b_nl])
                s.add("pe", lambda e: e.matmul(ps[6][0:64, 0:256], lhsT=Ub, rhs=nl[:], start=True, stop=True),
                      reads=[b_nl, cx.b_cstb], writes=[bp[6]])
                s.add("pe", lambda e: e.matmul(ps[6][0:64, 256:512], lhsT=cx.cstb[0:64, 512:576], rhs=nl[:], start=True, stop=True),
                      reads=[b_nl, cx.b_cstb], writes=[bp[6]])
                for h in range(2):
                    s.add("pe", lambda e, h=h: e.matmul(ps[7][:, h * 64:(h + 1) * 64], lhsT=nl[:, h * 128:(h + 1) * 128], rhs=Ub,
                                                        start=True, stop=True), reads=[b_nl, cx.b_cstb], writes=[bp[7]])
                s.add("act", lambda e: e.copy(out=wb[:, 0:256], in_=ps[6][0:64, 0:256]), reads=[bp[6]], writes=[b_wb])
                s.add("dve", lambda e: e.tensor_tensor(out=wb[:, 256:512], in0=ps[6][0:64, 256:512], in1=wb[:, 0:256], op=ALU.subtract),
                      reads=[bp[6], b_wb], writes=[b_wb])
                s.add("act", lambda e: e.activation(out=wb[:, 256:512], in_=wb[:, 256:512], func=AF.Exp, scale=-1.0 / 16),
                      reads=[b_wb], writes=[b_wb])
                s.add("dve", lambda e: e.tensor_tensor(out=kd[:, 0:256], in0=ps[4].bitcast(BF16)[0:64, 512:768], in1=wb[:, 256:512], op=ALU.mult),
                      reads=[bp[4], b_wb], writes=[b_kd])
                s.add("act", lambda e: e.activation(out=e1[:, 0:128], in_=ps[7][:, 0:128], func=AF.Exp, scale=-1.0 / 16),
                      reads=[bp[7]], writes=[b_e1])
                s.add("act", lambda e: e.activation(out=e2[:, 0:128], in_=ps[7][:, 0:128], func=AF.Exp, scale=1.0 / 16),
                      reads=[bp[7]], writes=[b_e2])
                for h in range(2):
                    s.add("dve", lambda e, h=h: e.scalar_tensor_tensor(
                        out=qd[:, h * 64:(h + 1) * 64], in0=cx.cmraw[h][:, cs], scalar=float(128 ** -0.5),
                        in1=e1[:, h * 64:(h + 1) * 64], op0=ALU.mult, op1=ALU.mult),
                        reads=[cx.b_cmraw[h], b_e1], writes=[b_qd])
                    s.add("dve", lambda e, h=h: e.tensor_tensor(
                        out=ki[:, h * 64:(h + 1) * 64], in0=cx.cmraw[2 + h][:, cs], in1=e2[:, h * 64:(h + 1) * 64], op=ALU.mult),
                        reads=[cx.b_cmraw[2 + h], b_e2], writes=[b_ki])
                for h in range(2):
                    s.add("pe", lambda e, h=h: e.matmul(ps[7][0:64, 128 + h * 64:128 + (h + 1) * 64], lhsT=ki[:, h * 64:(h + 1) * 64],
                                                        rhs=qd[:, h * 64:(h + 1) * 64], start=True, stop=True),
                          reads=[b_ki, b_qd], writes=[bp[7]])
                for h in range(2):
                    s.add("dve", lambda e, h=h: e.tensor_tensor(out=at[:, h * 64:(h + 1) * 64], in0=ps[7][0:64, 128 + h * 64:128 + (h + 1) * 64],
                                                                in1=U, op=ALU.mult), reads=[bp[7], cx.b_cst], writes=[b_at])
                return
            for h in range(2):
                s.add("pe", lambda e, h=h: e.matmul(ps[3][0:64, h * 256:(h + 1) * 256], lhsT=at[:, h * 64:(h + 1) * 64],
                                                    rhs=wv[:, h * 256:(h + 1) * 256], start=True, stop=False),
                      reads=[b_at, b_wv], writes=[bp[3]])
                s.add("pe", lambda e, h=h: e.matmul(ps[3][0:64, h * 256:(h + 1) * 256], lhsT=qd[:, h * 64:(h + 1) * 64],
                                                    rhs=cx.st_b[h][:], start=False, stop=True),
                      reads=[b_qd, cx.b_st_b[h]], writes=[bp[3]])
            for h in range(2):
                s.add("pe", lambda e, h=h: e.matmul(ps[1][:, h * 256:(h + 1) * 256], lhsT=kd[:, h * 128:(h + 1) * 128],
                                                    rhs=wv[:, h * 256:(h + 1) * 256], start=True, stop=True),
                      reads=[b_kd, b_wv], writes=[bp[1]])
            for h in range(2):
                s.add("dve", lambda e, h=h: e.scalar_tensor_tensor(
                    out=cx.st_f[h][:], in0=cx.st_f[h][:], scalar=e1[:, h * 64 + 63:h * 64 + 64], in1=ps[1][:, h * 256:(h + 1) * 256],
                    op0=ALU.mult, op1=ALU.add), reads=[cx.b_st_f[h], b_e1, bp[1]], writes=[cx.b_st_f[h]])
                s.add("act", lambda e, h=h: e.copy(out=cx.st_b[h][:], in_=cx.st_f[h][:]), reads=[cx.b_st_f[h]], writes=[cx.b_st_b[h]])
            for h in range(2):
                s.add("act", lambda e, h=h: e.activation(out=wc[:, h * 256:(h + 1) * 256], in_=ps[3][0:64, h * 256:(h + 1) * 256],
                                                         func=AF.Square, accum_out=st[:, h:h + 1]), reads=[bp[3]], writes=[b_wc, b_st])
            s.add("act", lambda e: e.activation(out=st[:, 2:4], in_=st[:, 0:2], func=AF.Ln, scale=1.0 / 256, bias=cx.epsq[0:64, 3:4]),
                  reads=[b_st, cx.b_epsq], writes=[b_st])
            s.add("act", lambda e: e.activation(out=st[:, 4:6], in_=st[:, 2:4], func=AF.Exp, scale=-0.5), reads=[b_st], writes=[b_st])
            for h in range(2):
                s.add("dve", lambda e, h=h: e.scalar_tensor_tensor(
                    out=og[:, h * 256:(h + 1) * 256], in0=ps[3][0:64, h * 256:(h + 1) * 256], scalar=st[:, 4 + h:5 + h],
                    in1=gz[:, h * 256:(h + 1) * 256], op0=ALU.mult, op1=ALU.mult), reads=[bp[3], b_st, b_gz], writes=[b_og])
            tp = cx.ps[0].bitcast(BF16)
            for j in range(4):
                s.add("pe", lambda e, j=j: e.transpose(tp[:, j * 64:(j + 1) * 64], og[:, j * 128:(j + 1) * 128], IDB),
                      reads=[b_og, cx.b_cstb], writes=[bp[0]])
            s.add("act", lambda e: e.copy(out=cx.gomt[:, :, cs], in_=tp[:, 0:256].rearrange("p (j t) -> p j t", j=4)),
                  reads=[bp[0]], writes=[cx.b_gomt])
    return emit


def gdn_setup(cx):
    def mk(name, shape, dt, n):
        return [sb(cx, f"{name}{i}", shape, dt) for i in range(n)], [Buf(f"{name}{i}") for i in range(n)]
    cx.ub, cx.b_ub = mk("ub", [128, 516], F32, 2)
    cx.uh, cx.b_uh = mk("uh", [128, 4], F32, 8)
    cx.qkn, cx.b_qkn = mk("qkn", [128, 512], BF16, 4)
    cx.vc, cx.b_vc = mk("vc", [128, 512], BF16, 4)
    cx.cacc, cx.b_cacc = mk("cacc", [128, 512], F32, 2)
    cx.grs, cx.b_grs = mk("grs", [128, 512], F32, 2)
    cx.zc, cx.b_zc = mk("zc", [128, 512], BF16, 4)
    cx.gp = sb(cx, "gp", [128, 32], F32)
    cx.b_gp = Buf("gp")
    cx.nP, cx.b_nP = mk("nP", [64, 512], BF16, 2)
    cx.nQ, cx.b_nQ = mk("nQ", [64, 512], BF16, 2)
    cx.rsh, cx.b_rsh = mk("rsh", [64, 512], BF16, 1)
    cx.nR, cx.b_nR = mk("nR", [64, 512], F32, 1)
    cx.rbf, cx.b_rbf = mk("rbf", [64, 512], BF16, 2)
    cx.gb, cx.b_gb = mk("gb", [64, 256], F32, 1)
    cx.dA, cx.b_dA = mk("dA", [64, 256], F32, 1)
    cx.dT, cx.b_dT = mk("dT", [64, 256], F32, 1)
    cx.aqt, cx.b_aqt = mk("aqt", [64, 512], BF16, 2)
    cx.sm, cx.b_sm = mk("sm", [64, 64], F32, 4)
    cx.sm2, cx.b_sm2 = mk("sm2", [128, 8], F32, 4)
    cx.vb, cx.b_vb = mk("vb", [64, 512], BF16, 1)
    cx.kbe, cx.b_kbe = mk("kbe", [64, 512], BF16, 1)
    cx.kdg, cx.b_kdg = mk("kdg", [64, 512], BF16, 1)
    cx.wtn, cx.b_wtn = mk("wtn", [128, 256], BF16, 1)
    cx.vnew, cx.b_vnew = mk("vnew", [64, 512], BF16, 1)
    cx.oi, cx.b_oi = mk("oi", [64, 512], F32, 1)
    cx.oo, cx.b_oo = mk("oo", [64, 512], F32, 1)
    cx.msk = sb(cx, "msk", [64, 1024], F32)
    cx.b_msk = Buf("msk")


def gdn_load_params(cx, gp_d, mp_d):
    s = cx.s
    s.add("sp", lambda e: e.dma_start(out=cx.gp[:], in_=gp_d), writes=[cx.b_gp], dma=True)
    s.add("sp", lambda e: e.dma_start(out=cx.mparam[:], in_=mp_d), writes=[cx.b_mparam], dma=True)
    s.add("act", lambda e: e.activation(out=cx.mparam[:, 516:520], in_=cx.mparam[:, 516:520], func=AF.Exp),
          reads=[cx.b_mparam], writes=[cx.b_mparam])
    for h in range(4):
        s.add("dve", lambda e, h=h: e.memset(cx.st_f[h][:], 0.0), writes=[cx.b_st_f[h]])
        s.add("dve", lambda e, h=h: e.memset(cx.st_b[h][:], 0.0), writes=[cx.b_st_b[h]])
    for g in range(8):
        s.add("dve", lambda e, g=g: e.memset(cx.uh[g][:], 0.0), writes=[cx.b_uh[g]])
    for h in range(4, 8):
        s.add("dve", lambda e, h=h: e.tensor_copy(out=cx.msk[:, 512 + h * 64:512 + (h + 1) * 64], in_=cx.cst[0:64, 0:64]),
              reads=[cx.b_cst], writes=[cx.b_msk])
    for h in range(4):
        s.add("dve", lambda e, h=h: e.tensor_copy(out=cx.msk[:, h * 64:(h + 1) * 64], in_=cx.cst[0:64, 768:832]),
              reads=[cx.b_cst], writes=[cx.b_msk])
        s.add("dve", lambda e, h=h: e.tensor_copy(out=cx.msk[:, 256 + h * 64:256 + (h + 1) * 64], in_=cx.cst[0:64, 128:192]),
              reads=[cx.b_cst], writes=[cx.b_msk])
        s.add("dve", lambda e, h=h: e.tensor_copy(out=cx.msk[:, 512 + h * 64:512 + (h + 1) * 64], in_=cx.cst[0:64, 0:64]),
              reads=[cx.b_cst], writes=[cx.b_msk])


GDN_STOP = 99


def gdn_proj(cx):
    s, win = cx.s, cx.win
    U = cx.cst[0:64, 128:192]
    ONES = cx.cst[0:64, 512:640]
    IDF = cx.cst[0:64, 0:64]
    IDB = cx.cstb[0:64, 0:64]
    ps, bp = cx.ps, cx.b_ps
    LS4, UI4, ID8 = cx.msk[:, 0:256], cx.msk[:, 256:512], cx.msk[:, 512:1024]

    def prep_group(g, xT, b_xT, mt):
        ub, b_ub = cx.ub[g % 2], cx.b_ub[g % 2]
        uh, b_uh = cx.uh[g], cx.b_uh[g]
        acc, b_acc = cx.cacc[g % 2], cx.b_cacc[g % 2]
        pb = 2
        s.add("dve", lambda e: e.tensor_copy(out=ub[:, 1:4], in_=uh[:, 1:4]), reads=[b_uh], writes=[b_ub])
        for kc in range(8):
            s.add("pe", lambda e, kc=kc: e.matmul(ps[pb][:, :], lhsT=win[:, kc, g * 128:(g + 1) * 128], rhs=xT[:, kc, :],
                                                  start=(kc == 0), stop=(kc == 7)), reads=[cx.b_win, b_xT], writes=[bp[pb]])
        s.add("act", lambda e: e.copy(out=ub[:, 4:516], in_=ps[pb][:, :]), reads=[bp[pb]], writes=[b_ub])
        s.add("act", lambda e: e.copy(out=uh[:, 1:4], in_=ps[pb][:, 509:512]), reads=[bp[pb]], writes=[b_uh])
        s.add("dve", lambda e: e.tensor_scalar(out=acc[:], in0=ub[:, 4:516], scalar1=cx.gp[:, g * 4 + 3:g * 4 + 4], scalar2=None,
                                               op0=ALU.mult), reads=[b_ub, cx.b_gp], writes=[b_acc])
        for j in range(3):
            s.add("dve", lambda e, j=j: e.scalar_tensor_tensor(
                out=acc[:], in0=ub[:, 1 + j:513 + j], scalar=cx.gp[:, g * 4 + j:g * 4 + j + 1], in1=acc[:],
                op0=ALU.mult, op1=ALU.add), reads=[b_ub, cx.b_gp, b_acc], writes=[b_acc])
        if g >= 4:
            vc, b_vc = cx.vc[g - 4], cx.b_vc[g - 4]
            s.add("act", lambda e: e.activation(out=vc[:], in_=acc[:], func=AF.Silu), reads=[b_acc], writes=[b_vc])
            return
        s.add("act", lambda e: e.activation(out=acc[:], in_=acc[:], func=AF.Silu), reads=[b_acc], writes=[b_acc])
        sq, b_sq = cx.junk[0][:, 0:512], cx.b_junk[0]
        rs, b_rs = cx.grs[g % 2], cx.b_grs[g % 2]
        s.add("act", lambda e: e.activation(out=sq, in_=acc[:], func=AF.Square), reads=[b_acc], writes=[b_sq])
        s.add("pe", lambda e: e.matmul(ps[3][:, :], lhsT=cx.ones_b, rhs=sq, start=True, stop=True),
              reads=[b_sq, cx.b_cstb], writes=[bp[3]])
        s.add("act", lambda e: e.activation(out=rs[:], in_=ps[3][:, :], func=AF.Ln, bias=cx.epsq[:, 3:4]),
              reads=[bp[3], cx.b_epsq], writes=[b_rs])
        s.add("act", lambda e: e.activation(out=rs[:], in_=rs[:], func=AF.Exp, scale=-0.5), reads=[b_rs], writes=[b_rs])
        qk, b_qk = cx.qkn[g], cx.b_qkn[g]
        sc = float(128 ** -0.5) if g < 2 else 1.0
        s.add("dve", lambda e: e.scalar_tensor_tensor(out=qk[:], in0=acc[:], scalar=sc, in1=rs[:], op0=ALU.mult, op1=ALU.mult),
              reads=[b_acc, b_rs], writes=[b_qk])

    def emit(mt, xT, b_xT):
        for g in range(8):
            prep_group(g, xT, b_xT, mt)
        for g in range(4):
            for kc in range(8):
                s.add("pe", lambda e, kc=kc, g=g: e.matmul(ps[2][:, :], lhsT=win[:, kc, 1024 + g * 128:1024 + (g + 1) * 128], rhs=xT[:, kc, :],
                                                           start=(kc == 0), stop=(kc == 7)), reads=[cx.b_win, b_xT], writes=[bp[2]])
            s.add("act", lambda e, g=g: e.activation(out=cx.zc[g][:], in_=ps[2][:, :], func=AF.Silu), reads=[bp[2]], writes=[cx.b_zc[g]])
        def Y(pr):
            pre(mt, 2 * pr, xT, b_xT)
            pre(mt, 2 * pr + 1, xT, b_xT)
            neu(mt, 2 * pr, xT, b_xT)

        def X(pr):
            post(mt, 2 * pr, xT, b_xT)
            post(mt, 2 * pr + 1, xT, b_xT)

        Y(0)
        for pr in range(4):
            if pr + 1 < 4:
                s.begin_record()
                Y(pr + 1)
                ly = s.end_record()
                s.begin_record()
                X(pr)
                lx = s.end_record()
                s.replay_zip(lx, ly)
            else:
                X(pr)
        jj, c0 = (mt * 512) // cx.CW, (mt * 512) % cx.CW
        s.add("pool", lambda e: e.dma_start(out=cx.gohalf_d[jj][:, c0:c0 + 512].rearrange("(j p) t -> p j t", p=128),
                                            in_=cx.gomt[:]), reads=[cx.b_gomt], writes=[cx.b_gohalf_d[jj]], dma=True)
        if c0 + 512 == cx.CW:
            exchange_chunk(cx, jj)

    def pre(mt, ci, xT, b_xT):
        c = mt * 8 + ci
        par = c % 2
        pp = (c // 2) % 2
        cs = slice(ci * 64, (ci + 1) * 64)
        sm, b_sm = cx.sm[c % 4], cx.b_sm[c % 4]
        sm2, b_sm2 = cx.sm2[c % 4], cx.b_sm2[c % 4]
        gz, b_gz = cx.w_gz[par], cx.b_w_gz[par]
        og, b_og = cx.w_og[par], cx.b_w_og[par]
        nR, b_nR = cx.nR[0], cx.b_nR[0]
        rbf, b_rbf = cx.rbf[pp], cx.b_rbf[pp]
        gb, b_gb = cx.gb[0], cx.b_gb[0]
        dA, b_dA = cx.dA[0], cx.b_dA[0]
        dT, b_dT = cx.dT[0], cx.b_dT[0]
        aqt, b_aqt = cx.aqt[pp], cx.b_aqt[pp]
        vb, b_vb = cx.vb[0], cx.b_vb[0]
        kbe, b_kbe = cx.kbe[0], cx.b_kbe[0]
        kdg, b_kdg = cx.kdg[0], cx.b_kdg[0]
        wtn, b_wtn = cx.wtn[0], cx.b_wtn[0]
        vnew, b_vnew = cx.vnew[0], cx.b_vnew[0]
        oi, b_oi = cx.oi[0], cx.b_oi[0]
        oo, b_oo = cx.oo[0], cx.b_oo[0]

        def hs(h):
            return slice(h * 64, (h + 1) * 64)

        def hv(h):
            return slice(h * 128, (h + 1) * 128)

        def hb(h):
            return slice((ci % 2) * 256 + h * 64, (ci % 2) * 256 + (h + 1) * 64)

        if GDN_STOP < 1:
            return
        if GDN_STOP < 2:
            return
        for kc in range(8):
            s.add("pe", lambda e, kc=kc: e.matmul(ps[2][0:64, 0:8], lhsT=xT[:, kc, cs], rhs=win[:, kc, 1536:1544],
                                                  start=(kc == 0), stop=(kc == 7)), reads=[cx.b_win, b_xT], writes=[bp[2]])
        s.add("dve", lambda e: e.tensor_tensor(out=sm[:, 0:4], in0=ps[2][0:64, 0:4], in1=cx.mparam[:, 512:516], op=ALU.add),
              reads=[bp[2], cx.b_mparam], writes=[b_sm])
        s.add("act", lambda e: e.activation(out=sm[:, 8:12], in_=ps[2][0:64, 4:8], func=AF.Exp, scale=-1.0), reads=[bp[2]], writes=[b_sm])
        s.add("act", lambda e: e.activation(out=sm[:, 0:4], in_=sm[:, 0:4], func=AF.Exp), reads=[b_sm], writes=[b_sm])
        s.add("act", lambda e: e.activation(out=sm[:, 0:4], in_=sm[:, 0:4], func=AF.Ln, bias=cx.epsq[0:64, 1:2]),
              reads=[b_sm, cx.b_epsq], writes=[b_sm])
        s.add("dve", lambda e: e.tensor_tensor(out=sm[:, 4:8], in0=sm[:, 0:4], in1=cx.mparam[:, 516:520], op=ALU.mult),
              reads=[b_sm, cx.b_mparam], writes=[b_sm])
        s.add("dve", lambda e: e.tensor_scalar(out=sm[:, 8:12], in0=sm[:, 8:12], scalar1=1.0, scalar2=None, op0=ALU.add),
              reads=[b_sm], writes=[b_sm])
        s.add("dve", lambda e: e.reciprocal(out=sm[:, 8:12], in_=sm[:, 8:12]), reads=[b_sm], writes=[b_sm])
        if GDN_STOP < 3:
            return
        s.add("pe", lambda e: e.matmul(ps[6][0:64, 0:4], lhsT=U, rhs=sm[:, 4:8], start=True, stop=True),
              reads=[b_sm, cx.b_cst], writes=[bp[6]])
        s.add("pe", lambda e: e.matmul(ps[6][:, 8:12], lhsT=ONES, rhs=sm[:, 4:8], start=True, stop=True),
              reads=[b_sm, cx.b_cst], writes=[bp[6]])
        s.add("dve", lambda e: e.tensor_copy(out=sm[:, 12:16], in_=ps[6][0:64, 0:4]), reads=[bp[6]], writes=[b_sm])
        s.add("act", lambda e: e.activation(out=sm[:, 16:20], in_=ps[6][0:64, 0:4], func=AF.Exp, scale=-1.0), reads=[bp[6]], writes=[b_sm])
        s.add("dve", lambda e: e.tensor_tensor(out=sm[:, 20:24], in0=ps[6][0:64, 8:12], in1=sm[:, 12:16], op=ALU.subtract),
              reads=[bp[6], b_sm], writes=[b_sm])
        s.add("act", lambda e: e.activation(out=sm[:, 20:24], in_=sm[:, 20:24], func=AF.Exp, scale=-1.0), reads=[b_sm], writes=[b_sm])
        s.add("act", lambda e: e.activation(out=sm2[:, 4:8], in_=ps[6][:, 8:12], func=AF.Exp, scale=-1.0), reads=[bp[6]], writes=[b_sm2])
        s.add("dve", lambda e: e.tensor_tensor(out=sm[:, 24:28], in0=sm[:, 8:12], in1=sm[:, 16:20], op=ALU.mult),
              reads=[b_sm], writes=[b_sm])
        if GDN_STOP < 4:
            return
        for h in range(4):
            s.add("dve", lambda e, h=h: e.tensor_copy(out=gb[:, hs(h)], in_=sm[:, 4 + h:5 + h].to_broadcast([64, 64])),
                  reads=[b_sm], writes=[b_gb])
        for h in range(4):
            s.add("pe", lambda e, h=h: e.matmul(ps[7][0:64, hs(h)], lhsT=gb[:, hs(h)], rhs=U, start=True, stop=True),
                  reads=[b_gb, cx.b_cst], writes=[bp[7]])
        for h in range(4):
            s.add("dve", lambda e, h=h: e.tensor_scalar(out=dA[:, hs(h)], in0=ps[7][0:64, hs(h)], scalar1=sm[:, 12 + h:13 + h], scalar2=0.0,
                                                        op0=ALU.subtract, op1=ALU.min), reads=[bp[7], b_sm], writes=[b_dA])
            s.add("dve", lambda e, h=h: e.tensor_scalar(out=dT[:, hs(h)], in0=ps[7][0:64, hs(h)], scalar1=sm[:, 12 + h:13 + h], scalar2=0.0,
                                                        op0=ALU.subtract, op1=ALU.max), reads=[bp[7], b_sm], writes=[b_dT])
        s.add("act", lambda e: e.activation(out=dA[:], in_=dA[:], func=AF.Exp), reads=[b_dA], writes=[b_dA])
        s.add("act", lambda e: e.activation(out=dT[:], in_=dT[:], func=AF.Exp, scale=-1.0), reads=[b_dT], writes=[b_dT])
        s.add("pool", lambda e: e.tensor_tensor(out=dA[:], in0=dA[:], in1=LS4, op=ALU.mult), reads=[b_dA, cx.b_msk], writes=[b_dA])
        s.add("pool", lambda e: e.tensor_tensor(out=dT[:], in0=dT[:], in1=UI4, op=ALU.mult), reads=[b_dT, cx.b_msk], writes=[b_dT])
        if GDN_STOP < 5:
            return
        for qh in range(2):
            kn, b_kn = cx.qkn[2 + qh], cx.b_qkn[2 + qh]
            qn, b_qn = cx.qkn[qh], cx.b_qkn[qh]
            s.add("pe", lambda e, qh=qh, kn=kn: e.matmul(ps[2][0:64, hs(qh)], lhsT=kn[:, cs], rhs=kn[:, cs], start=True, stop=True),
                  reads=[b_kn], writes=[bp[2]])
            s.add("pe", lambda e, qh=qh, kn=kn, qn=qn: e.matmul(ps[2][0:64, 128 + qh * 64:192 + qh * 64], lhsT=kn[:, cs], rhs=qn[:, cs],
                                                                start=True, stop=True), reads=[b_kn, b_qn], writes=[bp[2]])
        P0, b_P0 = cx.nP[0], cx.b_nP[0]
        for h in range(4):
            s.add("dve", lambda e, h=h: e.scalar_tensor_tensor(out=P0[:, hb(h)], in0=ps[2][0:64, hs(h // 2)], scalar=sm[:, 8 + h:9 + h],
                                                               in1=dA[:, hs(h)], op0=ALU.mult, op1=ALU.mult),
                  reads=[bp[2], b_sm, b_dA], writes=[b_P0])
            s.add("dve", lambda e, h=h: e.tensor_tensor(out=aqt[:, hb(h)], in0=ps[2][0:64, 128 + (h // 2) * 64:192 + (h // 2) * 64],
                                                        in1=dT[:, hs(h)], op=ALU.mult), reads=[bp[2], b_dT], writes=[b_aqt])

    def neu(mt, ci, xT, b_xT):
        P0, b_P0 = cx.nP[0], cx.b_nP[0]
        c = mt * 8 + ci
        par = c % 2
        pp = (c // 2) % 2
        cs = slice(ci * 64, (ci + 1) * 64)
        sm, b_sm = cx.sm[c % 4], cx.b_sm[c % 4]
        sm2, b_sm2 = cx.sm2[c % 4], cx.b_sm2[c % 4]
        gz, b_gz = cx.w_gz[par], cx.b_w_gz[par]
        og, b_og = cx.w_og[par], cx.b_w_og[par]
        nR, b_nR = cx.nR[0], cx.b_nR[0]
        rbf, b_rbf = cx.rbf[pp], cx.b_rbf[pp]
        gb, b_gb = cx.gb[0], cx.b_gb[0]
        dA, b_dA = cx.dA[0], cx.b_dA[0]
        dT, b_dT = cx.dT[0], cx.b_dT[0]
        aqt, b_aqt = cx.aqt[pp], cx.b_aqt[pp]
        vb, b_vb = cx.vb[0], cx.b_vb[0]
        kbe, b_kbe = cx.kbe[0], cx.b_kbe[0]
        kdg, b_kdg = cx.kdg[0], cx.b_kdg[0]
        wtn, b_wtn = cx.wtn[0], cx.b_wtn[0]
        vnew, b_vnew = cx.vnew[0], cx.b_vnew[0]
        oi, b_oi = cx.oi[0], cx.b_oi[0]
        oo, b_oo = cx.oo[0], cx.b_oo[0]

        def hs(h):
            return slice(h * 64, (h + 1) * 64)

        def hv(h):
            return slice(h * 128, (h + 1) * 128)

        def hb(h):
            return slice((ci % 2) * 256 + h * 64, (ci % 2) * 256 + (h + 1) * 64)

        if GDN_STOP < 1:
            return
        if GDN_STOP < 6:
            return
        Q0, b_Q0 = cx.nQ[0], cx.b_nQ[0]
        rsh, b_rsh = cx.rsh[0], cx.b_rsh[0]
        for h in range(8):
            s.add("pe", lambda e, h=h: e.matmul(ps[7][0:64, hs(h)], lhsT=P0[:, hs(h)], rhs=IDB, start=True, stop=True),
                  reads=[b_P0, cx.b_cstb], writes=[bp[7]])
        s.add("act", lambda e: e.copy(out=Q0[:], in_=ps[7][0:64, 0:512]), reads=[bp[7]], writes=[b_Q0])
        s.add("dve", lambda e: e.scalar_tensor_tensor(out=rsh[:], in0=ps[7][0:64, 0:512], scalar=-1.0, in1=ID8, op0=ALU.mult, op1=ALU.add),
              reads=[cx.b_msk, bp[7]], writes=[b_rsh])
        s.add("dve", lambda e: e.scalar_tensor_tensor(out=nR[:], in0=ps[7][0:64, 0:512], scalar=-1.0, in1=ID8, op0=ALU.mult, op1=ALU.add),
              reads=[cx.b_msk, bp[7]], writes=[b_nR])
        a = 0
        for lvl in range(1, 6):
            P, b_P, Q, b_Q = cx.nP[a], cx.b_nP[a], cx.nQ[a], cx.b_nQ[a]
            Pn, b_Pn, Qn, b_Qn = cx.nP[1 - a], cx.b_nP[1 - a], cx.nQ[1 - a], cx.b_nQ[1 - a]
            for h in range(8):
                s.add("pe", lambda e, h=h, P=P, Q=Q: e.matmul(ps[6][0:64, hs(h)], lhsT=Q[:, hs(h)], rhs=P[:, hs(h)], start=True, stop=True),
                      reads=[b_P, b_Q], writes=[bp[6]])
            s.add("act", lambda e, Pn=Pn: e.copy(out=Pn[:], in_=ps[6][0:64, 0:512]), reads=[bp[6]], writes=[b_Pn])
            if lvl < 5:
                for h in range(8):
                    s.add("pe", lambda e, h=h, P=P, Q=Q: e.matmul(ps[7][0:64, hs(h)], lhsT=P[:, hs(h)], rhs=Q[:, hs(h)], start=True, stop=True),
                          reads=[b_P, b_Q], writes=[bp[7]])
                s.add("dve", lambda e, Qn=Qn: e.tensor_copy(out=Qn[:], in_=ps[7][0:64, 0:512]), reads=[bp[7]], writes=[b_Qn])
            for h in range(8):
                s.add("pe", lambda e, h=h, Pn=Pn: e.matmul(ps[2][0:64, hs(h)], lhsT=Pn[:, hs(h)], rhs=rsh[:, hs(h)], start=True, stop=True),
                      reads=[b_Pn, b_rsh], writes=[bp[2]])
            if lvl < 5:
                s.add("dve", lambda e: e.tensor_tensor(out=rsh[:], in0=ps[2][0:64, 0:512], in1=nR[:], op=ALU.add),
                      reads=[b_nR, bp[2]], writes=[b_rsh])
                s.add("dve", lambda e: e.tensor_tensor(out=nR[:], in0=ps[2][0:64, 0:512], in1=nR[:], op=ALU.add),
                      reads=[b_nR, bp[2]], writes=[b_nR])
            else:
                s.add("dve", lambda e: e.tensor_tensor(out=rbf[:], in0=ps[2][0:64, 0:512], in1=nR[:], op=ALU.add),
                      reads=[b_nR, bp[2]], writes=[b_rbf])
            a = 1 - a

    def post(mt, ci, xT, b_xT):
        c = mt * 8 + ci
        par = c % 2
        pp = (c // 2) % 2
        cs = slice(ci * 64, (ci + 1) * 64)
        sm, b_sm = cx.sm[c % 4], cx.b_sm[c % 4]
        sm2, b_sm2 = cx.sm2[c % 4], cx.b_sm2[c % 4]
        gz, b_gz = cx.w_gz[par], cx.b_w_gz[par]
        og, b_og = cx.w_og[par], cx.b_w_og[par]
        nR, b_nR = cx.nR[0], cx.b_nR[0]
        rbf, b_rbf = cx.rbf[pp], cx.b_rbf[pp]
        gb, b_gb = cx.gb[0], cx.b_gb[0]
        dA, b_dA = cx.dA[0], cx.b_dA[0]
        dT, b_dT = cx.dT[0], cx.b_dT[0]
        aqt, b_aqt = cx.aqt[pp], cx.b_aqt[pp]
        vb, b_vb = cx.vb[0], cx.b_vb[0]
        kbe, b_kbe = cx.kbe[0], cx.b_kbe[0]
        kdg, b_kdg = cx.kdg[0], cx.b_kdg[0]
        wtn, b_wtn = cx.wtn[0], cx.b_wtn[0]
        vnew, b_vnew = cx.vnew[0], cx.b_vnew[0]
        oi, b_oi = cx.oi[0], cx.b_oi[0]
        oo, b_oo = cx.oo[0], cx.b_oo[0]

        def hs(h):
            return slice(h * 64, (h + 1) * 64)

        def hv(h):
            return slice(h * 128, (h + 1) * 128)

        def hb(h):
            return slice((ci % 2) * 256 + h * 64, (ci % 2) * 256 + (h + 1) * 64)

        if GDN_STOP < 1:
            return
        tpz = ps[0].bitcast(BF16)
        for g in range(4):
            s.add("pe", lambda e, g=g: e.transpose(tpz[0:64, g * 128:(g + 1) * 128], cx.zc[g][:, cs], cx.ident_b),
                  reads=[cx.b_zc[g], cx.b_cstb], writes=[bp[0]])
        s.add("dve", lambda e: e.tensor_tensor(out=gz[:], in0=tpz[0:64, 0:512], in1=cx.mparam[:, 0:512], op=ALU.mult),
              reads=[bp[0], cx.b_mparam], writes=[b_gz])
        if GDN_STOP < 8:
            return
        tpb = ps[0].bitcast(BF16)
        for g in range(6):
            src_, b_src = (cx.qkn[2 + g], cx.b_qkn[2 + g]) if g < 2 else (cx.vc[g - 2], cx.b_vc[g - 2])
            s.add("pe", lambda e, g=g, src_=src_: e.transpose(tpb[0:64, g * 128:(g + 1) * 128], src_[:, cs], cx.ident_b),
                  reads=[b_src, cx.b_cstb], writes=[bp[0]])
        for h in range(4):
            s.add("dve", lambda e, h=h: e.tensor_scalar(out=vb[:, hv(h)], in0=tpb[0:64, (2 + h) * 128:(3 + h) * 128], scalar1=sm[:, 8 + h:9 + h],
                                                        scalar2=None, op0=ALU.mult), reads=[bp[0], b_sm], writes=[b_vb])
            s.add("dve", lambda e, h=h: e.tensor_scalar(out=kbe[:, hv(h)], in0=tpb[0:64, (h // 2) * 128:(h // 2 + 1) * 128],
                                                        scalar1=sm[:, 24 + h:25 + h], scalar2=None, op0=ALU.mult),
                  reads=[bp[0], b_sm], writes=[b_kbe])
            s.add("dve", lambda e, h=h: e.tensor_scalar(out=kdg[:, hv(h)], in0=tpb[0:64, (h // 2) * 128:(h // 2 + 1) * 128],
                                                        scalar1=sm[:, 20 + h:21 + h], scalar2=None, op0=ALU.mult),
                  reads=[bp[0], b_sm], writes=[b_kdg])
        if GDN_STOP < 9:
            return
        for h in range(4):
            s.add("pe", lambda e, h=h: e.matmul(ps[1][:, hs(h)], lhsT=kbe[:, hv(h)], rhs=rbf[:, hb(h)], start=True, stop=True),
                  reads=[b_kbe, b_rbf], writes=[bp[1]])
        s.add("act", lambda e: e.mul(out=wtn[:], in_=ps[1][:, 0:256], mul=-1.0), reads=[bp[1]], writes=[b_wtn])
        for h in range(4):
            s.add("pe", lambda e, h=h: e.matmul(ps[4][0:64, hv(h)], lhsT=rbf[:, hb(h)], rhs=vb[:, hv(h)], start=True, stop=False),
                  reads=[b_rbf, b_vb], writes=[bp[4]])
            s.add("pe", lambda e, h=h: e.matmul(ps[4][0:64, hv(h)], lhsT=wtn[:, hs(h)], rhs=cx.st_b[h][:, 0:128], start=False, stop=True),
                  reads=[b_wtn, cx.b_st_b[h]], writes=[bp[4]])
        s.add("act", lambda e: e.copy(out=vnew[:], in_=ps[4][0:64, :]), reads=[bp[4]], writes=[b_vnew])
        for h in range(4):
            qn, b_qn = cx.qkn[h // 2], cx.b_qkn[h // 2]
            s.add("pe", lambda e, h=h, qn=qn: e.matmul(ps[5][0:64, hv(h)], lhsT=qn[:, cs], rhs=cx.st_b[h][:, 0:128], start=True, stop=True),
                  reads=[b_qn, cx.b_st_b[h]], writes=[bp[5]])
        for h in range(4):
            s.add("pe", lambda e, h=h: e.matmul(ps[3][0:64, hv(h)], lhsT=aqt[:, hb(h)], rhs=vnew[:, hv(h)], start=True, stop=True),
                  reads=[b_aqt, b_vnew], writes=[bp[3]])
        s.add("act", lambda e: e.copy(out=oi[:], in_=ps[3][0:64, :]), reads=[bp[3]], writes=[b_oi])
        for h in range(4):
            s.add("dve", lambda e, h=h: e.scalar_tensor_tensor(out=oo[:, hv(h)], in0=ps[5][0:64, hv(h)], scalar=sm[:, 16 + h:17 + h],
                                                               in1=oi[:, hv(h)], op0=ALU.mult, op1=ALU.add),
                  reads=[bp[5], b_sm, b_oi], writes=[b_oo])
        for h in range(4):
            s.add("pe", lambda e, h=h: e.matmul(ps[1][:, hv(h)], lhsT=kdg[:, hv(h)], rhs=vnew[:, hv(h)], start=True, stop=True),
                  reads=[b_kdg, b_vnew], writes=[bp[1]])
        for h in range(4):
            s.add("dve", lambda e, h=h: e.scalar_tensor_tensor(out=cx.st_f[h][:, 0:128], in0=cx.st_f[h][:, 0:128], scalar=sm2[:, 4 + h:5 + h],
                                                               in1=ps[1][:, hv(h)], op0=ALU.mult, op1=ALU.add),
                  reads=[cx.b_st_f[h], b_sm2, bp[1]], writes=[cx.b_st_f[h]])
            s.add("act", lambda e, h=h: e.copy(out=cx.st_b[h][:, 0:128], in_=cx.st_f[h][:, 0:128]), reads=[cx.b_st_f[h]], writes=[cx.b_st_b[h]])
        if GDN_STOP < 10:
            return
        for h in range(4):
            s.add("act", lambda e, h=h: e.activation(out=oi[:, hv(h)], in_=oo[:, hv(h)], func=AF.Square, accum_out=sm[:, 32 + h:33 + h]),
                  reads=[b_oo], writes=[b_oi, b_sm])
        s.add("act", lambda e: e.activation(out=sm[:, 36:40], in_=sm[:, 32:36], func=AF.Ln, scale=1.0 / 128, bias=cx.epsq[0:64, 3:4]),
              reads=[b_sm, cx.b_epsq], writes=[b_sm])
        s.add("act", lambda e: e.activation(out=sm[:, 40:44], in_=sm[:, 36:40], func=AF.Exp, scale=-0.5), reads=[b_sm], writes=[b_sm])
        for h in range(4):
            s.add("dve", lambda e, h=h: e.scalar_tensor_tensor(out=og[:, hv(h)], in0=oo[:, hv(h)], scalar=sm[:, 40 + h:41 + h],
                                                               in1=gz[:, hv(h)], op0=ALU.mult, op1=ALU.mult),
                  reads=[b_oo, b_sm, b_gz], writes=[b_og])
        for j in range(4):
            s.add("pe", lambda e, j=j: e.transpose(tpb[:, j * 64:(j + 1) * 64], og[:, j * 128:(j + 1) * 128], IDB),
                  reads=[b_og, cx.b_cstb], writes=[bp[0]])
        s.add("act", lambda e: e.copy(out=cx.gomt[:, :, cs], in_=tpb[:, 0:256].rearrange("p (j t) -> p j t", j=4)),
              reads=[bp[0]], writes=[cx.b_gomt])
    return emit


KINDS = ["fox", "gla", "gdn", "fox"]
NCOLS = {"fox": 2052, "gla": 1552, "gdn": 1544}


def host_params(inp, hh):
    f32 = np.float32
    P = {}
    nw = np.asarray(inp["norm_w"], f32)
    P["normw"] = np.ascontiguousarray(nw.reshape(4, 8, 128).transpose(2, 0, 1).reshape(128, 32))
    for li in range(2):
        w = np.asarray(inp["fox_w_in"][li], f32)
        s = slice(hh * 512, hh * 512 + 512)
        P[f"fox_win{li}"] = np.ascontiguousarray(np.concatenate(
            [w[:, 0:1024][:, s], w[:, 1024:2048][:, s], w[:, 3072:4096][:, s], w[:, 2048:3072][:, s],
             w[:, 4096 + hh * 4:4096 + hh * 4 + 4]], axis=1))
        fx = np.zeros((128, 16), f32)
        fx[:, 0] = np.asarray(inp["fox_q_gain"][li], f32)
        fx[:, 1] = np.asarray(inp["fox_k_gain"][li], f32)
        fx[:, 8:12] = np.asarray(inp["fox_b_f"][li], f32)[hh * 4:hh * 4 + 4][None, :]
        P[f"fox_fx{li}"] = fx
    w = np.asarray(inp["gla_w_in"][0], f32)
    P["gla_win"] = np.ascontiguousarray(np.concatenate(
        [w[:, hh * 256:hh * 256 + 256], w[:, 512 + hh * 256:512 + hh * 256 + 256], w[:, 3072:3088],
         w[:, 1024 + hh * 512:1024 + hh * 512 + 512], w[:, 2048 + hh * 512:2048 + hh * 512 + 512]], axis=1))
    P["gla_wup"] = np.ascontiguousarray(np.concatenate(
        [np.asarray(inp["gla_w_gate_up"][0], f32)[:, hh * 256:hh * 256 + 256],
         np.asarray(inp["gla_b_gate"][0], f32)[None, hh * 256:hh * 256 + 256]], axis=0))
    P["gla_gain"] = np.ascontiguousarray(np.tile(np.asarray(inp["gla_o_gain"][0], f32)[None, :], (64, 2)))
    w = np.asarray(inp["gdn_w_in"][0], f32)
    qc = slice(hh * 256, hh * 256 + 256)
    kc = slice(512 + hh * 256, 512 + hh * 256 + 256)
    vs = slice(1024 + hh * 512, 1024 + hh * 512 + 512)
    P["gdn_win"] = np.ascontiguousarray(np.concatenate(
        [w[:, qc], w[:, kc], w[:, vs], w[:, 2048 + hh * 512:2048 + hh * 512 + 512],
         w[:, 3072 + hh * 4:3072 + hh * 4 + 4], w[:, 3080 + hh * 4:3080 + hh * 4 + 4]], axis=1))
    cw = np.asarray(inp["gdn_conv_w"][0], f32)
    cwc = np.concatenate([cw[:, qc], cw[:, kc], cw[:, vs]], axis=1)
    P["gdn_gp"] = np.ascontiguousarray(cwc.reshape(4, 8, 128).transpose(2, 1, 0).reshape(128, 32))
    mp = np.zeros((64, 1024), f32)
    mp[:, 0:512] = np.tile(np.asarray(inp["gdn_o_gain"][0], f32)[None, :], (64, 4))
    mp[:, 512:516] = np.asarray(inp["gdn_dt_bias"][0], f32)[hh * 4:hh * 4 + 4][None, :]
    mp[:, 516:520] = np.asarray(inp["gdn_a_log"][0], f32)[hh * 4:hh * 4 + 4][None, :]
    P["gdn_mp"] = mp
    P["wout"] = [np.ascontiguousarray(np.asarray(inp["fox_w_out"][0], f32)), np.ascontiguousarray(np.asarray(inp["gla_w_out"][0], f32)),
                 np.ascontiguousarray(np.asarray(inp["gdn_w_out"][0], f32)), np.ascontiguousarray(np.asarray(inp["fox_w_out"][1], f32))]
    return P


GROUPS = [[0, 1], [2, 3], [4, 5], [6, 7]]


def exchange_chunk(cx, j):
    cx.s.add("pool", lambda e: e.collective_compute("AllGather", ALU.bypass, replica_groups=cx.groups,
                                                    ins=[cx.gohalf_d[j]], outs=[cx.gofull_d[j]]),
             reads=[cx.b_gohalf_d[j]], writes=[cx.b_gofull_d[j]], dma=True, cc=True)


def exchange(cx):
    for j in range(cx.NCH):
        exchange_chunk(cx, j)


def build_fused(S, groups=None):
    from contextlib import ExitStack
    nc = bass.Bass("TRN2", target_bir_lowering=False)
    cx = Ctx()
    cx.nc, cx.S = nc, S
    cx.s = Sched(nc)
    cx.groups = groups or GROUPS
    setup_consts(cx)
    common_setup(cx)
    out_d = nc.dram_tensor("out", [S, 1024], F32, kind="ExternalOutput").ap()
    cx.gohalf_d = [nc.dram_tensor(f"gohalf{j}", [512, cx.CW], BF16, kind="Internal").ap() for j in range(cx.NCH)]
    cx.b_gohalf_d = [Buf(f"gohalf{j}") for j in range(cx.NCH)]

    def din(name, shape):
        return nc.dram_tensor(name, shape, F32, kind="ExternalInput").ap()
    fox_w = [din("fox_win0", [1024, 2052]), din("fox_win1", [1024, 2052])]
    cx.fx_d = [din("fox_fx0", [128, 16]), din("fox_fx1", [128, 16])]
    gla_w, wup_d, gain_d = din("gla_win", [1024, 1552]), din("gla_wup", [17, 256]), din("gla_gain", [64, 512])
    gdn_w, gp_d, mp_d = din("gdn_win", [1024, 1544]), din("gdn_gp", [128, 32]), din("gdn_mp", [64, 1024])
    wout = [din(f"wout{i}", [1024, 1024]) for i in range(4)]
    with ExitStack() as es:
        cx.es = es
        fox_setup(cx)
    cx.es = None
    mixer_common_setup(cx)
    with ExitStack() as es:
        cx.es = es
        gla_setup(cx)
    cx.es = None
    gdn_setup(cx)

    cx.s.scopes = getattr(build_fused, "scopes", False)
    cx.s.phase = "L0_proj"
    fox_load_params(cx, 0)
    boundary(cx, 0, "fox", cx.x_in, cx.xres_d, None, fox_w[0], 2052, fox_proj(cx))
    cx.s.phase = "L0_attn"
    fox_attn(cx)
    exchange(cx)
    cx.s.barrier()
    cx.s.phase = "L1_gla"
    gla_load_params(cx, wup_d, gain_d)
    boundary(cx, 1, "gla", cx.x_in, cx.xres_d, wout[0], gla_w, 1552, gla_proj(cx))
    cx.s.barrier()
    cx.s.phase = "L2_gdn"
    gdn_load_params(cx, gp_d, mp_d)
    boundary(cx, 2, "gdn", cx.xres_d, cx.xres_d, wout[1], gdn_w, 1544, gdn_proj(cx))
    cx.s.barrier()
    cx.s.phase = "L3_proj"
    fox_load_params(cx, 1)
    boundary(cx, 3, "fox", cx.xres_d, cx.xres_d, wout[2], fox_w[1], 2052, fox_proj(cx))
    cx.s.phase = "L3_attn"
    fox_attn(cx)
    exchange(cx)
    cx.s.phase = "L4_final"
    boundary(cx, 4, None, cx.xres_d, cx.xres_d, wout[3], None, 0, None, final_out=out_d)
    cx.s.emit()
    return nc


def kernel(**inp):
    x = np.asarray(inp["x"], np.float32)
    B, S, D = x.shape
    cst = make_consts()
    params = [host_params(inp, hh) for hh in range(2)]
    nc = build_fused(S)
    in_maps = []
    for c in range(8):
        P = params[c % 2]
        m = {"x": np.ascontiguousarray(x[c // 2]), "cst": cst, "normw_in": P["normw"],
             "fox_win0": P["fox_win0"], "fox_win1": P["fox_win1"], "fox_fx0": P["fox_fx0"], "fox_fx1": P["fox_fx1"],
             "gla_win": P["gla_win"], "gla_wup": P["gla_wup"], "gla_gain": P["gla_gain"],
             "gdn_win": P["gdn_win"], "gdn_gp": P["gdn_gp"], "gdn_mp": P["gdn_mp"]}
        for i in range(4):
            m[f"wout{i}"] = P["wout"][i]
        in_maps.append(m)
    res = run_bass_kernel_spmd(nc, in_maps, core_ids=list(range(8)))
    out = np.empty((B, S, D), np.float32)
    for b in range(B):
        out[b] = np.asarray(res.results[2 * b]["out"])
    return out
```
